# Optimizing a Trainium2 kernel written in Bass

```python
import jax, jax.numpy as jnp
from jax import lax

D_MODEL = 1024
BATCH = 4
SEQ = 8192
DEPTH = 2

GRID_W = 64
CTX_LEN = 256
HEAD_DIM = 64
N_NA_HEADS = 8
NA_WIDTH = N_NA_HEADS * HEAD_DIM
N_FOURIER_GROUPS = 4
FOURIER_WIDTH = D_MODEL // 4
FOURIER_GROUP = FOURIER_WIDTH // N_FOURIER_GROUPS
CONV_WIDTH = D_MODEL // 4
CONV_K = 3
WIN_ROWS = 8
WIN_COLS = 16
MLP_HIDDEN = 4 * D_MODEL
N_BRANCHES = 3
N_MOD = 6
SPLIT_SIZES = (FOURIER_WIDTH, CONV_WIDTH, CONV_WIDTH, CONV_WIDTH, NA_WIDTH, NA_WIDTH, NA_WIDTH, D_MODEL, D_MODEL, D_MODEL)
IN_WIDTH = FOURIER_WIDTH + 3 * CONV_WIDTH + 3 * NA_WIDTH + N_BRANCHES * D_MODEL
KV_START = FOURIER_WIDTH + 3 * CONV_WIDTH + NA_WIDTH
KV_END = KV_START + 2 * NA_WIDTH
EPS = 1e-6
NEG = -1e30

kernel_name = "hybrid_fourier_conv_natten_dit_block"


def rmsnorm(x, g):
    xf = x.astype(jnp.float32)
    y = xf * lax.rsqrt(jnp.mean(xf * xf, axis=-1, keepdims=True) + EPS)
    return (y * g.astype(jnp.float32)).astype(x.dtype)


def modulate(h, shift, scale):
    return h * (1 + scale) + shift


def split_proj(p):
    parts, off = [], 0
    for s in SPLIT_SIZES:
        parts.append(p[..., off:off + s])
        off += s
    return parts


def to_heads(t):
    return t.reshape(t.shape[0], t.shape[1], N_NA_HEADS, HEAD_DIM)


def fourier_mix(u):
    b, n, _ = u.shape
    uf = u.astype(jnp.float32).reshape(b, n, N_FOURIER_GROUPS, FOURIER_GROUP)
    y = jnp.fft.fftn(uf, axes=(1, 3), norm="ortho").real
    return y.reshape(b, n, FOURIER_WIDTH).astype(u.dtype)


def short_conv(u, gate_b, gate_c, w):
    z = gate_c * u
    y = lax.conv_general_dilated(
        z, w[:, None, :].astype(z.dtype), window_strides=(1,),
        padding=((CONV_K // 2, CONV_K // 2),),
        dimension_numbers=("NWC", "WIO", "NWC"), feature_group_count=CONV_WIDTH)
    return gate_b * y


def neighbourhood_attention(q, k, v, k_ctx, v_ctx, rel_bias):
    b, n, _ = q.shape
    rows = n // GRID_W
    kr = min(WIN_ROWS, rows)
    kc = WIN_COLS
    grid = (b, rows, GRID_W, N_NA_HEADS, HEAD_DIM)
    q, k, v = q.reshape(grid), k.reshape(grid), v.reshape(grid)
    r = jnp.arange(rows)
    row_start = jnp.clip(r - kr // 2, 0, rows - kr)
    row_idx = row_start[:, None] + jnp.arange(kr)[None, :]
    k_rows = k[:, row_idx]
    v_rows = v[:, row_idx]
    cols = jnp.arange(GRID_W)
    col_start = jnp.clip(cols - kc // 2, 0, GRID_W - kc)
    col_in = (cols[None, :] >= col_start[:, None]) & (cols[None, :] < col_start[:, None] + kc)
    dr = row_idx - r[:, None] + (WIN_ROWS - 1)
    dc = jnp.clip(cols[None, :] - cols[:, None], -(WIN_COLS - 1), WIN_COLS - 1) + (WIN_COLS - 1)
    bias = rel_bias[:, dr[:, None, :, None], dc[None, :, None, :]].astype(jnp.float32)
    scale = HEAD_DIM ** -0.5
    s_win = jnp.einsum('brqhd,brikhd->bhrqik', q, k_rows).astype(jnp.float32) * scale + bias
    s_win = jnp.where(col_in[None, None, None, :, None, :], s_win, NEG)
    s_ctx = jnp.einsum('brqhd,bjhd->bhrqj', q, k_ctx).astype(jnp.float32) * scale
    n_win = kr * GRID_W
    logits = jnp.concatenate([s_win.reshape(b, N_NA_HEADS, rows, GRID_W, n_win), s_ctx], axis=-1)
    p = jax.nn.softmax(logits, axis=-1).astype(v.dtype)
    p_win = p[..., :n_win].reshape(s_win.shape)
    p_ctx = p[..., n_win:]
    out = (jnp.einsum('bhrqik,brikhd->brqhd', p_win, v_rows)
           + jnp.einsum('bhrqj,bjhd->brqhd', p_ctx, v_ctx))
    return out.reshape(b, n, NA_WIDTH)


def context_attention(q, k, v):
    b, l = q.shape[0], q.shape[1]
    s = jnp.einsum('blhd,bmhd->bhlm', q, k).astype(jnp.float32) * (HEAD_DIM ** -0.5)
    p = jax.nn.softmax(s, axis=-1).astype(v.dtype)
    return jnp.einsum('bhlm,bmhd->blhd', p, v).reshape(b, l, NA_WIDTH)


def merge_branches(f, cv, at, g_f, g_c, g_a, w_f, w_c, w_a, w_o):
    m = (jax.nn.sigmoid(g_f) * (f @ w_f) + jax.nn.sigmoid(g_c) * (cv @ w_c)
         + jax.nn.sigmoid(g_a) * (at @ w_a))
    return m @ w_o


def sq_relu_mlp(h, w1, w2):
    a = jax.nn.relu(h @ w1)
    return (a * a) @ w2


def setup_inputs(seed: int = 0) -> dict:
    key = jax.random.key(seed)
    ks = jax.random.split(key, 20)
    nrm = lambda k, shape, s: jax.random.normal(k, shape, jnp.float32) * s
    return {
        "x": nrm(ks[0], (BATCH, SEQ, D_MODEL), 1.0),
        "c": nrm(ks[1], (BATCH, D_MODEL), 1.0),
        "ctx": nrm(ks[2], (BATCH, CTX_LEN, D_MODEL), 1.0),
        "c_ctx": nrm(ks[3], (D_MODEL,), 1.0),
        "ada_w": nrm(ks[4], (DEPTH, D_MODEL, N_MOD * D_MODEL), 0.5 * D_MODEL ** -0.5),
        "ada_b": nrm(ks[5], (DEPTH, N_MOD * D_MODEL), 0.02),
        "norm1_g": 1.0 + nrm(ks[6], (DEPTH, D_MODEL), 0.02),
        "norm2_g": 1.0 + nrm(ks[7], (DEPTH, D_MODEL), 0.02),
        "w_in": nrm(ks[8], (DEPTH, D_MODEL, IN_WIDTH), D_MODEL ** -0.5),
        "conv_w": nrm(ks[9], (DEPTH, CONV_K, CONV_WIDTH), CONV_K ** -0.5),
        "rel_bias": nrm(ks[10], (DEPTH, N_NA_HEADS, 2 * WIN_ROWS - 1, 2 * WIN_COLS - 1), 0.1),
        "w_fourier": nrm(ks[11], (DEPTH, FOURIER_WIDTH, D_MODEL), FOURIER_WIDTH ** -0.5),
        "w_conv": nrm(ks[12], (DEPTH, CONV_WIDTH, D_MODEL), CONV_WIDTH ** -0.5),
        "w_attn": nrm(ks[13], (DEPTH, NA_WIDTH, D_MODEL), NA_WIDTH ** -0.5),
        "w_o": nrm(ks[14], (DEPTH, D_MODEL, D_MODEL), D_MODEL ** -0.5),
        "mlp_w1": nrm(ks[15], (DEPTH, D_MODEL, MLP_HIDDEN), D_MODEL ** -0.5),
        "mlp_w2": nrm(ks[16], (DEPTH, MLP_HIDDEN, D_MODEL), MLP_HIDDEN ** -0.5),
        "final_g": 1.0 + nrm(ks[17], (D_MODEL,), 0.02),
    }


def reference(x, c, ctx, c_ctx, ada_w, ada_b, norm1_g, norm2_g, w_in, conv_w, rel_bias,
              w_fourier, w_conv, w_attn, w_o, mlp_w1, mlp_w2, final_g):
    for l in range(DEPTH):
        last = l == DEPTH - 1
        mod = jax.nn.silu(c) @ ada_w[l] + ada_b[l]
        sh1, sc1, g1, sh2, sc2, g2 = [m[:, None, :] for m in jnp.split(mod, N_MOD, axis=-1)]
        mod_c = jax.nn.silu(c_ctx) @ ada_w[l] + ada_b[l]
        csh1, csc1, cg1, csh2, csc2, cg2 = jnp.split(mod_c, N_MOD, axis=-1)

        hc = modulate(rmsnorm(ctx, norm1_g[l]), csh1, csc1)
        if last:
            kv_c = hc @ w_in[l][:, KV_START:KV_END]
            k_c, v_c = to_heads(kv_c[..., :NA_WIDTH]), to_heads(kv_c[..., NA_WIDTH:])
        else:
            pc = split_proj(hc @ w_in[l])
            k_c, v_c = to_heads(pc[5]), to_heads(pc[6])

        hx = modulate(rmsnorm(x, norm1_g[l]), sh1, sc1)
        px = split_proj(hx @ w_in[l])
        f_x = fourier_mix(px[0])
        cv_x = short_conv(px[1], px[2], px[3], conv_w[l])
        at_x = neighbourhood_attention(px[4], px[5], px[6], k_c, v_c, rel_bias[l])
        x = x + g1 * merge_branches(f_x, cv_x, at_x, px[7], px[8], px[9],
                                    w_fourier[l], w_conv[l], w_attn[l], w_o[l])

        if not last:
            f_c = fourier_mix(pc[0])
            cv_c = short_conv(pc[1], pc[2], pc[3], conv_w[l])
            at_c = context_attention(to_heads(pc[4]), k_c, v_c)
            ctx = ctx + cg1 * merge_branches(f_c, cv_c, at_c, pc[7], pc[8], pc[9],
                                             w_fourier[l], w_conv[l], w_attn[l], w_o[l])

        x = x + g2 * sq_relu_mlp(modulate(rmsnorm(x, norm2_g[l]), sh2, sc2), mlp_w1[l], mlp_w2[l])
        if not last:
            ctx = ctx + cg2 * sq_relu_mlp(modulate(rmsnorm(ctx, norm2_g[l]), csh2, csc2),
                                          mlp_w1[l], mlp_w2[l])
    return rmsnorm(x, final_g)
```

```python
import numpy as np
import ml_dtypes
import concourse.bass as bass
import concourse.mybir as mybir
from concourse.bass_utils import run_bass_kernel_spmd

F32 = mybir.dt.float32
BF16 = mybir.dt.bfloat16
AF = mybir.ActivationFunctionType
ALU = mybir.AluOpType

NSLOT = 16
EPOCH = 8000
NCORES = 8
DM = 1024
TOK = 4096
NBLK = 8
INW = 5632
RX = 2080
NE = 22
TABW = NE * 64
EPS = 1e-6


class Sched:
    ENGS = ["pe", "act", "dve", "pool", "sp"]

    def __init__(self):
        self.ops = {e: [] for e in self.ENGS}
        self.res = {}
        self.ndma = {e: 0 for e in self.ENGS}
        self.pending = {e: set() for e in self.ENGS}
        self.extra = {}

    def _collect(self, eng, reads, writes):
        deps = set(self.pending[eng])
        self.pending[eng] = set()
        for k in reads:
            r = self.res.get(k)
            if r is not None and r[0] is not None:
                t = r[0]
                if t[0] == "c" and t[1] == eng and eng in ("pe", "sp"):
                    continue
                deps.add(t)
        for k in writes:
            if k in self.extra:
                deps |= self.extra.pop(k)
            r = self.res.get(k)
            if r is not None:
                for t in ([r[0]] if r[0] is not None else []) + list(r[1]):
                    if t[0] == "c" and t[1] == eng and eng == "pe":
                        continue
                    deps.add(t)
        return deps

    def _update(self, reads, writes, tok):
        for k in reads:
            r = self.res.setdefault(k, [None, []])
            r[1].append(tok)
        for k in writes:
            self.res[k] = [tok, []]

    def op(self, eng, fn, reads=(), writes=(), force_sig=False):
        deps = self._collect(eng, reads, writes)
        tok = ("c", eng, len(self.ops[eng]))
        self.ops[eng].append(dict(fn=fn, deps=deps, tok=tok, dma=None, force_sig=force_sig, cc=force_sig))
        self._update(reads, writes, tok)
        return tok

    def dma(self, eng, fn, reads=(), writes=()):
        deps = self._collect(eng, reads, writes)
        j = self.ndma[eng]
        self.ndma[eng] += 1
        if j >= NSLOT:
            deps.add(("d", eng, j - NSLOT))
        tok = ("d", eng, j)
        self.ops[eng].append(dict(fn=fn, deps=deps, tok=tok, dma=j))
        self._update(reads, writes, tok)
        return tok

    def barrier(self, skip_cc=False):
        toks = set()
        for e in self.ENGS:
            last = None
            for o in reversed(self.ops[e]):
                if skip_cc and o.get("cc"):
                    continue
                if o["dma"] is None and o["fn"] is not None:
                    last = o["tok"]
                    break
            if last is not None:
                toks.add(last)
        for q in self.ENGS:
            for j in range(max(0, self.ndma[q] - NSLOT), self.ndma[q]):
                toks.add(("d", q, j))
        for e in self.ENGS:
            self.pending[e] |= toks

    def wait_all(self, eng):
        self.barrier()
        deps = set(self.pending[eng])
        self.pending[eng] = set()
        self.ops[eng].append(dict(fn=None, deps=deps, tok=("c", eng, len(self.ops[eng])), dma=None))

    def plan(self):
        sig = {e: set() for e in self.ENGS}
        for e in self.ENGS:
            fc = {}
            fd = {}
            for idx, o in enumerate(self.ops[e]):
                waits = []
                for t in sorted(o["deps"], key=str):
                    if t[0] == "c":
                        _, e2, i2 = t
                        if e2 == e and i2 >= idx:
                            continue
                        if fc.get(e2, -1) >= i2:
                            continue
                        fc[e2] = i2
                        waits.append(t)
                        sig[e2].add(i2)
                    else:
                        q, j = t[1], t[2]
                        slot, val = (q, j % NSLOT), j // NSLOT
                        if fd.get(slot, -1) >= val:
                            continue
                        fd[slot] = val
                        waits.append(t)
                o["waits"] = waits
        self.ordinal = {}
        self.nsig = {}
        for e in self.ENGS:
            for idx, o in enumerate(self.ops[e]):
                if o.get("force_sig"):
                    sig[e].add(idx)
        for e in self.ENGS:
            for n, i in enumerate(sorted(sig[e])):
                self.ordinal[(e, i)] = n
            self.nsig[e] = len(sig[e])
        return self.nsig

    def emit(self, eng, h, esems, dsems):
        for idx, o in enumerate(self.ops[eng]):
            for t in o["waits"]:
                if t[0] == "c":
                    n = self.ordinal[(t[1], t[2])]
                    h.wait_ge(esems[t[1]][n // EPOCH], n % EPOCH + 1)
                else:
                    q, j = t[1], t[2]
                    h.wait_ge(dsems[q][j % NSLOT], 16 * (j // NSLOT + 1))
            if o["fn"] is None:
                continue
            ins = o["fn"](h)
            if o["dma"] is not None:
                ins.then_inc(dsems[eng][o["dma"] % NSLOT], 16)
            elif (eng, idx) in self.ordinal:
                n = self.ordinal[(eng, idx)]
                ins.then_inc(esems[eng][n // EPOCH], 1)


def _bf(a):
    return np.ascontiguousarray(a.astype(ml_dtypes.bfloat16))


def host_constants():
    C = {}
    C["ident"] = _bf(np.eye(128, dtype=np.float32))
    r = np.arange(128)[:, None].astype(np.float64)
    k1 = np.arange(128)[None, :].astype(np.float64)
    ang = 2 * np.pi * r * k1 / 128.0
    C["cs128"] = _bf(np.concatenate([np.cos(ang), -np.sin(ang)], axis=1))
    norm = 1.0 / np.sqrt(8192.0 * 64.0)
    p = np.arange(128)
    chp = 2 * (p % 64) + (p // 64)
    co = np.arange(128)
    same = (chp[:, None] // 64) == (co[None, :] // 64)
    a3 = 2 * np.pi * (chp[:, None] % 64) * (co[None, :] % 64) / 64.0
    C["ch3"] = _bf(np.stack([np.cos(a3) * same * norm, np.sin(a3) * same * norm], 0).transpose(1, 0, 2))
    same2 = (p[:, None] // 64) == (co[None, :] // 64)
    a4 = 2 * np.pi * (p[:, None] % 64) * (co[None, :] % 64) / 64.0
    C["chc"] = _bf(np.stack([np.cos(a4) * same2, np.sin(a4) * same2], 0).transpose(1, 0, 2))
    n = np.arange(256)[:, None].astype(np.float64)
    k = np.arange(256)[None, :].astype(np.float64)
    a5 = 2 * np.pi * n * k / 256.0
    nc_ = 1.0 / np.sqrt(256.0 * 64.0)
    t = np.stack([np.cos(a5) * nc_, -np.sin(a5) * nc_], 0)
    C["dft256"] = _bf(t.reshape(2, 2, 128, 256).transpose(2, 0, 1, 3))
    pp = np.arange(128)
    krl = pp // 64
    kc = pp % 64
    e = np.arange(NE)
    qc = np.arange(64)
    dr = 17 - e[None, :, None] + krl[:, None, None]
    cs = np.clip(qc - 8, 0, 48)
    colin = (kc[:, None, None] >= cs[None, None, :]) & (kc[:, None, None] < cs[None, None, :] + 16)
    m = (dr >= 0) & (dr <= 14) & colin
    C["maskF"] = _bf(m.reshape(128, TABW).astype(np.float32))
    mi = m & (dr >= 3) & (dr <= 10)
    C["maskI"] = _bf(mi[:, 7:16, :].reshape(128, 9 * 64).astype(np.float32))
    C["ones"] = _bf(np.ones((128, 128), np.float32))
    return C


def core_constants(half):
    C = {}
    p = np.arange(128)
    c = (p % 64)[:, None, None].astype(np.float64)
    k1 = np.arange(128)[None, :, None].astype(np.float64)
    k2 = (32 * half + np.arange(32))[None, None, :].astype(np.float64)
    ang = 2 * np.pi * c * (k1 + 128.0 * k2) / 8192.0
    gr, gi = np.cos(ang), -np.sin(ang)
    C["gtab"] = _bf(np.concatenate([-gi, gr, gi], axis=2))
    krl = p // 64
    rm = np.zeros((128, NBLK, 8, 8), np.float32)
    for i in range(NBLK):
        for j in range(8):
            for q in range(8):
                rq = 64 * half + 8 * i + q
                kr = 64 * half + 8 * i - 4 + 2 * j + krl
                rs = min(max(rq - 4, 0), 120)
                rm[:, i, j, q] = ((kr >= 0) & (kr < 128) & (kr >= rs) & (kr < rs + 8)).astype(np.float32)
    C["rowmask"] = _bf(rm)
    zf = np.zeros((128, 2), np.float32)
    zf[:, 0] = 0.0 if half == 0 else 1.0
    zf[:, 1] = 0.0 if half == 1 else 1.0
    C["zflag"] = zf
    return C


def gather_relbias(rel_bias):
    p = np.arange(128)
    krl = (p // 64)[:, None, None]
    kc = (p % 64)[:, None, None]
    e = np.arange(NE)[None, :, None]
    qc = np.arange(64)[None, None, :]
    dr = np.clip(17 - e + krl, 0, 14) + 0 * qc
    dc = np.clip(kc - qc, -15, 15) + 15 + 0 * e
    t = rel_bias[:, :, dr, dc]
    return np.ascontiguousarray(t.reshape(rel_bias.shape[0], rel_bias.shape[1], 128, TABW).astype(np.float32))


def build_nc(stage=99, debug=(), ncores=NCORES):
    nc = bass.Bass("TRN2", target_bir_lowering=False)
    S = Sched()
    D = {}

    def din(name, shape, dt=F32):
        D[name] = nc.dram_tensor(name, list(shape), dt, kind="ExternalInput").ap()

    def dscr(name, shape, dt):
        D[name] = nc.dram_tensor(name, list(shape), dt).ap()

    def dout(name, shape, dt=F32):
        D[name] = nc.dram_tensor(name, list(shape), dt, kind="ExternalOutput").ap()

    din("x", [TOK, DM]); din("ctx", [256, DM]); din("cvecT", [128, 16])
    din("ada_w", [2, DM, 6 * DM]); din("ada_bT", [2, 128, 48]); din("ada_b", [2, 6 * DM])
    din("n1gT", [2, 128, 8]); din("n2gT", [2, 128, 8]); din("final_g", [1, DM])
    din("w_in", [2, DM, INW]); din("conv_wT", [2, 128, 6]); din("rbt", [2, 8, 128, TABW])
    din("w_fourier", [2, 256, DM]); din("w_conv", [2, 256, DM]); din("w_attn", [2, 512, DM])
    din("w_o", [2, DM, DM]); din("mlp_w1", [2, DM, 4 * DM]); din("mlp_w2", [2, 4 * DM, DM])
    din("ident", [128, 128], BF16); din("cs128", [128, 256], BF16); din("ch3", [128, 2, 128], BF16)
    din("chc", [128, 2, 128], BF16); din("dft256", [128, 2, 2, 256], BF16); din("maskF", [128, TABW], BF16); din("maskI", [128, 576], BF16)
    din("ones", [128, 128], BF16); din("gtab", [128, 128, 96], BF16); din("rowmask", [128, NBLK, 8, 8], BF16)
    din("zflag", [128, 2])
    dout("out", [TOK, DM])
    dscr("x1", [TOK, DM], F32)
    dscr("hx_s", [NBLK, 128, 8, 512], BF16)
    dscr("xiu", [TOK, 256], BF16)
    dscr("xou", [2 * TOK, 256], BF16)
    dscr("exch_in", [RX, 256], BF16)
    dscr("exch_out", [2 * RX, 256], BF16)
    dscr("k_s", [128, 4, 72 * 64], BF16)
    dscr("v_s", [36, 128, 512], BF16)
    dscr("z_s", [128, 2, TOK + 32], BF16)
    dscr("y_s", [128, 2, TOK], BF16)
    dscr("gbc", [2, 2, 2, 128, DM], F32)
    WSHP = {"w_in": [DM, INW], "w_fourier": [256, DM], "w_conv": [256, DM], "w_attn": [512, DM], "w_o": [DM, DM],
            "mlp_w1": [DM, 4 * DM], "mlp_w2": [4 * DM, DM]}
    for nm_, shp_ in WSHP.items():
        dscr("b_" + nm_, [2] + shp_, BF16)

    dbg_layer = 1 if "L1" in debug else 0
    debug = [d_ for d_ in debug if d_ != "L1"]
    if "mix" in debug:
        dout("dbg_at", [128, 4, 512], BF16); dout("dbg_cv", [128, 2, 512], BF16); dout("dbg_m", [128, 8, 512], BF16)
        dout("dbg_xmix", [128, 4, DM], F32)
        dout("dbg_h2", [128, 8, 512], BF16); dout("dbg_h1", [128, 32, 512], BF16)
    ARENA = 212000
    arena = nc.alloc_sbuf_tensor("arena", [128, ARENA // 2], BF16)
    ps = [nc.alloc_psum_tensor(f"ps{i}", [128, 512], F32) for i in range(8)]

    class Region:
        def __init__(self, base, size):
            self.base, self.size, self.cur = base, size, base

        def reset(self):
            self.cur = self.base

        def alloc(self, shape, dt):
            n = int(np.prod(shape))
            nb = n * (4 if dt == F32 else 2)
            off = (self.cur + 63) // 64 * 64
            assert off + nb <= self.base + self.size, ("arena overflow", shape, off + nb - self.base, self.size)
            self.cur = off + nb
            ap = arena[:, off // 2:(off + nb) // 2]
            if dt == F32:
                ap = ap.bitcast(F32)
            if len(shape) == 2:
                ap = ap.rearrange("p (a b) -> p a b", b=int(shape[1]))
            elif len(shape) == 3:
                ap = ap.rearrange("p (a b c) -> p a b c", b=int(shape[1]), c=int(shape[2]))
            elif len(shape) == 4:
                ap = ap.rearrange("p (a b c d) -> p a b c d", b=int(shape[1]), c=int(shape[2]), d=int(shape[3]))
            return ap

    PERS = Region(0, 58000)
    TMP = Region(58000, ARENA - 58000)

    bank_ctr = [0]

    def nbank():
        b = bank_ctr[0] % 8
        bank_ctr[0] += 1
        return b

    def PK(b):
        return ("ps", b)

    def mm(out, lhsT, rhs, start, stop, reads, writes, tp=None):
        if tp is None:
            S.op("pe", lambda e: e.matmul(out, lhsT, rhs, start=start, stop=stop), reads, writes)
        else:
            S.op("pe", lambda e: e.matmul(out, lhsT, rhs, start=start, stop=stop, tile_position=tp), reads, writes)

    def act(out, in_, func, reads, writes, **kw):
        S.op("act", lambda e: e.activation(out, in_, func, **kw), reads, writes)

    def tt(out, in0, in1, op, reads, writes, eng="dve"):
        S.op(eng, lambda e: e.tensor_tensor(out, in0, in1, op), reads, writes)

    def ld(out, in_, reads, writes, eng="sp", **kw):
        S.dma(eng, lambda e: e.dma_start(out=out, in_=in_, **kw), reads, writes)

    def ldw(out, in_, reads, writes):
        S.dma("pool", lambda e: e.dma_start(out=out, in_=in_), reads, writes)

    def wview(name, l, r0, r1, c0, c1):
        return D[name][l, r0:r1, c0:c1].rearrange("(k p) n -> p k n", p=128)

    WMODE = {"cast_store": False}

    def ldwb(out, name, l, r0, r1, c0, c1, wkey):
        if WMODE["cast_store"]:
            ldw(out, wview(name, l, r0, r1, c0, c1), [], [wkey])
            S.dma("sp", lambda e: e.dma_start(out=D["b_" + name][l, r0:r1, c0:c1].rearrange("(k p) n -> p k n", p=128), in_=out),
                  [wkey], [("wbpart", name, l, r0, c0)])
            return
        S.dma("sp", lambda e: e.dma_start(out=out, in_=D["b_" + name][l, r0:r1, c0:c1].rearrange("(k p) n -> p k n", p=128)),
              [("wb", name, l)], [wkey])

    CONV_PIECES = [("w_in", 0, DM, 512, 768), ("w_in", 0, DM, 1024, 1536), ("w_in", 0, DM, 2560, 3584), ("w_in", 0, DM, 3584, 4608),
                   ("w_in", 0, DM, 4608, 5632), ("w_fourier", 0, 256, 0, DM), ("w_conv", 0, 256, 0, DM), ("w_attn", 0, 512, 0, DM),
                   ("w_o", 0, DM, 0, DM)] + [("mlp_w1", 0, DM, c * 1024, (c + 1) * 1024) for c in range(4)] + \
                  [("mlp_w2", r * 1024, (r + 1) * 1024, 0, DM) for r in range(4)]

    def convert_pieces(l, pieces, after_key):
        for (nm_, r0, r1, c0, c1) in pieces:
            S.dma("pool", lambda e, nm_=nm_, r0=r0, r1=r1, c0=c0, c1=c1: e.dma_start(
                out=D["b_" + nm_][l, r0:r1, c0:c1], in_=D[nm_][l, r0:r1, c0:c1]), [after_key], [("wb", nm_, l)])

    ident = PERS.alloc([128], BF16)
    ones = PERS.alloc([128], BF16)
    modc = PERS.alloc([2, 48, 2], F32)
    acol = PERS.alloc([2, 2, 4, 8], F32)
    convw = PERS.alloc([2, 6], F32)
    zflag = PERS.alloc([2], F32)
    rowmask = PERS.alloc([NBLK, 8, 8], BF16)
    tab = PERS.alloc([8, TABW], BF16)
    tabI = PERS.alloc([8, 576], BF16)
    kc_sb = PERS.alloc([4, 256], BF16)
    vc_sb = PERS.alloc([2, 512], BF16)
    ctx_sb = PERS.alloc([2, DM], F32)
    g1bc = PERS.alloc([DM], F32)
    g2bc = PERS.alloc([DM], F32)

    ld(ident, D["ident"], [], ["ident"])
    ld(ones, D["ones"], [], ["ones"])
    ld(convw, D["conv_wT"].rearrange("l p k -> p l k"), [], ["convw"])
    ld(zflag, D["zflag"], [], ["zflag"])
    ld(rowmask, D["rowmask"], [], ["rowmask"])
    ld(ctx_sb, D["ctx"].rearrange("(t p) d -> p t d", p=128), [], ["ctx_sb"])

    def phase0(layers):
        TMP.reset()
        cT = TMP.alloc([16], F32)
        sT = TMP.alloc([16], F32)
        sl = TMP.alloc([16, 128], BF16)
        sb = TMP.alloc([16], BF16)
        abT = TMP.alloc([2, 48], F32)
        ngT = TMP.alloc([2, 2, 8], F32)
        abb = TMP.alloc([2, DM], F32)
        W = [TMP.alloc([8, 1024], BF16) for _ in range(6)]
        gt = [TMP.alloc([512], F32) for _ in range(2)]
        ld(cT, D["cvecT"], [], ["cT"])
        ld(abT, D["ada_bT"].rearrange("l p c -> p l c"), [], ["abT"])
        ld(ngT[:, :, 0, :], D["n1gT"].rearrange("l p c -> p l c"), [], ["ngT0"])
        ld(ngT[:, :, 1, :], D["n2gT"].rearrange("l p c -> p l c"), [], ["ngT1"])
        act(sT, cT, AF.Silu, ["cT"], ["sT"])
        S.op("dve", lambda e: e.tensor_copy(sb, sT), ["sT"], ["sb"])
        S.op("dve", lambda e: e.tensor_copy(sl, sT.unsqueeze(2).to_broadcast([128, 16, 128])), ["sT"], ["sl"])
        for l in layers:
            for s in range(6):
                ldw(W[s], wview("ada_w", l, 0, DM, s * 1024, (s + 1) * 1024), [], [("p0w", s)])
            for gi, c0 in enumerate((2048, 5120)):
                ld(abb[:, gi], D["ada_b"][l:l + 1, c0:c0 + 1024].partition_broadcast(128), [("gbcw", l, gi)], [("abb", gi)])
            for s in range(6):
                w = W[s]
                wk = ("p0w", s)
                b = nbank()
                for fc in range(8):
                    for k in range(8):
                        mm(ps[b][:, 2 * fc:2 * fc + 2], w[:, k, fc * 128:(fc + 1) * 128], sb[:, 2 * k:2 * k + 2],
                           k == 0, k == 7, [wk, "sb"], [PK(b)])
                tt(modc[:, l, s * 8:(s + 1) * 8, :], ps[b][:, 0:16].rearrange("p (c t) -> p c t", t=2),
                   abT[:, l, s * 8:(s + 1) * 8].unsqueeze(2).to_broadcast([128, 8, 2]), ALU.add,
                   [PK(b), "abT"], [("modc", l)])
                if s in (2, 5):
                    gi = 0 if s == 2 else 1
                    for t in range(2):
                        for hh in range(2):
                            b = nbank()
                            for k in range(8):
                                mm(ps[b][:, :], sl[:, 2 * k + t, :], w[:, k, hh * 512:(hh + 1) * 512],
                                   k == 0, k == 7, [wk, "sl"], [PK(b)])
                            g = gt[(t * 2 + hh) % 2]
                            gk = ("gt", (t * 2 + hh) % 2)
                            tt(g, ps[b][:, :], abb[:, gi, hh * 512:(hh + 1) * 512], ALU.add, [PK(b), ("abb", gi)], [gk])
                            ld(D["gbc"][l, t, gi, :, hh * 512:(hh + 1) * 512], g, [gk], [("gbcw", l, gi), ("gbc", l, t)])
            for t in range(2):
                for (vi, sc_c, ng_i) in ((0, 8, 0), (2, 32, 1)):
                    S.op("dve", lambda e, l=l, t=t, vi=vi, sc_c=sc_c, ng_i=ng_i: e.scalar_tensor_tensor(
                        acol[:, l, t, vi, :], modc[:, l, sc_c:sc_c + 8, t], 1.0, ngT[:, l, ng_i, :], ALU.add, ALU.mult),
                        [("modc", l), "ngT0", "ngT1"], [("acol", l)])
                for (vi, sh_c) in ((1, 0), (3, 24)):
                    S.op("dve", lambda e, l=l, t=t, vi=vi, sh_c=sh_c: e.tensor_copy(
                        acol[:, l, t, vi, :], modc[:, l, sh_c:sh_c + 8, t]), [("modc", l)], [("acol", l)])
        S.barrier()

    phase0([0])

    def norm_transpose(xt, ntile, A, B, hT, keys_x, key_h, scr):
        junk, ss, xn = scr
        for t in range(ntile):
            act(xn[:, t, :], xt[:, t, :], AF.Square, keys_x, [("xn", t), ("ss", t)], accum_out=ss[:, t:t + 1])
            S.op("dve", lambda e, t=t: e.tensor_scalar(ss[:, t:t + 1], ss[:, t:t + 1], 1.0 / DM, EPS, ALU.mult, ALU.add),
                 [("ss", t)], [("ss", t)])
            act(ss[:, t:t + 1], ss[:, t:t + 1], AF.Sqrt, [("ss", t)], [("ss", t)])
            S.op("dve", lambda e, t=t: e.reciprocal(ss[:, t:t + 1], ss[:, t:t + 1]), [("ss", t)], [("ss", t)])
            act(xn[:, t, :], xt[:, t, :], AF.Copy, keys_x + [("ss", t)], [("xn", t)], scale=ss[:, t:t + 1])
        for k in range(8):
            b = nbank()
            pb = ps[b].bitcast(BF16)
            for t in range(ntile):
                S.op("pe", lambda e, t=t, k=k, pb=pb: e.transpose(pb[:, t * 128:(t + 1) * 128], xn[:, t, k * 128:(k + 1) * 128], ident),
                     [("xn", t), "ident"], [PK(b)])
            act(hT[:, k, 0:ntile * 128], pb[:, 0:ntile * 128], AF.Identity, [PK(b)], [key_h],
                scale=A[:, k:k + 1], bias=B[:, k:k + 1])

    def proj_fm(hT, T, W, wkey, c0, nchunk, sink, hkey):
        for c in range(nchunk):
            b = nbank()
            for k in range(8):
                mm(ps[b][:, 0:T], W[:, k, c0 + c * 128:c0 + (c + 1) * 128], hT[:, k, 0:T], k == 0, k == 7,
                   [wkey, hkey], [PK(b)])
            sink(c, b)

    def alloc_scr():
        return (TMP.alloc([DM], BF16), TMP.alloc([8], F32), TMP.alloc([4, DM], BF16))

    class NS:
        pass

    U1K = ["kwin", "vwin", "zwin", "ywin", ("rden", 0), ("rden", 1)] + [("cacc", c) for c in range(2)] + \
          [("E", n) for n in range(4)] + [("P", n) for n in range(4)] + [("sg", n) for n in range(2)] + [("mtmp", n) for n in range(2)]

    def alloc_mixer():
        M = NS()
        M.WS = [TMP.alloc([4096], BF16) for _ in range(4)]
        M.ws_ctr = 0
        u0 = TMP.cur
        M.kwin = TMP.alloc([4, 1024], BF16)
        M.vwin = TMP.alloc([8, 512], BF16)
        M.Et = [TMP.alloc([512], BF16) for _ in range(4)]
        M.Pt = [TMP.alloc([512], BF16) for _ in range(4)]
        M.sg = [TMP.alloc([512], F32) for _ in range(2)]
        M.mtmp = [TMP.alloc([512], F32) for _ in range(2)]
        M.cacc = TMP.alloc([2, 512], F32)
        M.zwin = TMP.alloc([2, 514], BF16)
        M.ywin = TMP.alloc([2, 512], BF16)
        M.rden = TMP.alloc([512], F32)
        u1 = TMP.cur
        TMP.cur = u0
        M.h1 = TMP.alloc([32, 512], BF16)
        TMP.cur = max(TMP.cur, u1)
        M.atT = TMP.alloc([4, 512], BF16)
        M.cvT = TMP.alloc([2, 512], BF16)
        M.q_sb = TMP.alloc([4, 2, 512], BF16)
        S.op("dve", lambda e, q=M.q_sb: e.memset(q, 0.0), [], ["q_sb"])
        M.mT = TMP.alloc([8, 512], BF16)
        M.rr = [TMP.alloc([512], BF16) for _ in range(2)]
        M.otmp = [TMP.alloc([512], F32) for _ in range(2)]
        M.ep_ctr = 0
        M.sg_ctr = 0
        M.h1_free = set()
        return M

    def layer(l):
        last = (l == 1)
        xin = D["x"] if l == 0 else D["x1"]
        xout = D["x1"] if l == 0 else D["out"]
        xkin = "xin0" if l == 0 else "x1"
        xkout = "x1" if l == 0 else "outk"
        A1 = acol[:, l, 0, 0, :]; B1 = acol[:, l, 0, 1, :]; A2 = acol[:, l, 0, 2, :]; B2 = acol[:, l, 0, 3, :]
        cA1 = acol[:, l, 1, 0, :]; cB1 = acol[:, l, 1, 1, :]; cA2 = acol[:, l, 1, 2, :]; cB2 = acol[:, l, 1, 3, :]

        def load_wk():
            wk1 = TMP.alloc([8, 768], BF16)
            wk2 = TMP.alloc([8, 1024], BF16)
            ldw(wk1[:, :, 0:512], wview("w_in", l, 0, DM, 0, 512), [], ["wk1"])
            ldw(wk1[:, :, 512:768], wview("w_in", l, 0, DM, 768, 1024), [], ["wk1"])
            ldw(wk2, wview("w_in", l, 0, DM, 1536, 2560), [], ["wk2"])
            return wk1, wk2

        TMP.reset()
        maskF = TMP.alloc([TABW], BF16)
        ld(maskF, D["maskF"], [], ["maskF"])
        maskI = TMP.alloc([576], BF16)
        ld(maskI, D["maskI"], [], ["maskI"])
        rb = [TMP.alloc([TABW], F32) for _ in range(2)]
        eb = [TMP.alloc([TABW], BF16) for _ in range(2)]
        for h in range(8):
            ld(rb[h % 2], D["rbt"][l, h], [], [("rb", h % 2)])
            act(eb[h % 2], rb[h % 2], AF.Exp, [("rb", h % 2)], [("eb", h % 2)])
            tt(tab[:, h, :], eb[h % 2], maskF, ALU.mult, [("eb", h % 2), "maskF"], ["tab"])
            tt(tabI[:, h, :], eb[h % 2][:, 7 * 64:16 * 64], maskI, ALU.mult, [("eb", h % 2), "maskI"], ["tab"])
        S.barrier()

        def slab(M):
            n = M.ws_ctr % 4
            M.ws_ctr += 1
            return M.WS[n], ("ws", n)

        def mixer_block(M, scr, T, hT, hkey, xres, xkey, yT, ykey, zw, zkey, keychunks, Am, Bm, pre_wo=None):
            nt = T // 128
            cacc, q_sb, atT, cvT, mT, h1 = M.cacc, M.q_sb, M.atT, M.cvT, M.mT, M.h1
            for c in range(2):
                S.op("dve", lambda e, c=c: e.tensor_scalar(cacc[:, c, 0:T], zw[:, c, 0:T], convw[:, l, 3 * c:3 * c + 1], None, ALU.mult),
                     [zkey, "convw"], [("cacc", c)])
                for kk in (1, 2):
                    S.op("dve", lambda e, c=c, kk=kk: e.scalar_tensor_tensor(cacc[:, c, 0:T], zw[:, c, kk:kk + T],
                                                                            convw[:, l, 3 * c + kk:3 * c + kk + 1], cacc[:, c, 0:T], ALU.mult, ALU.add),
                         [zkey, "convw", ("cacc", c)], [("cacc", c)])
            w, wk = slab(M)
            wv = w.rearrange("p (k n) -> p k n", k=8)
            ldwb(wv[:, :, 0:256], "w_in", l, 0, DM, 512, 768, wk)

            def sink_cb(c, b):
                tt(cvT[:, c, 0:T], ps[b][:, 0:T], cacc[:, c, 0:T], ALU.mult, [PK(b), ("cacc", c)], ["cvT"])
            proj_fm(hT, T, wv, wk, 0, 2, sink_cb, hkey)
            w, wk = slab(M)
            wv = w.rearrange("p (k n) -> p k n", k=8)
            ldwb(wv, "w_in", l, 0, DM, 1024, 1536, wk)

            def sink_q(c, b):
                act(q_sb[0:64, c, 0, 0:T], ps[b][0:64, 0:T], AF.Copy, [PK(b)], ["q_sb"])
                act(q_sb[64:128, c, 1, 0:T], ps[b][64:128, 0:T], AF.Copy, [PK(b)], ["q_sb"])
            proj_fm(hT, T, wv, wk, 0, 4, sink_q, hkey)
            nkc = len(keychunks)
            sctr = 0
            for hp in range(4):
                OB, DB = (3, 5), (4, 6)
                order = [nkc - 2] + list(range(nkc - 2)) + [nkc - 1]
                items = [(hh, ci) for hh in range(2) for ci in order]
                sbanks = {}

                def crange(ci):
                    lo, hi = keychunks[ci][6], keychunks[ci][7]
                    return (0, T) if lo is None else (lo * 64, (hi + 1) * 64)

                def emit_S(it):
                    nonlocal sctr
                    hh, ci = it
                    kT_ap, kkey = keychunks[ci][0], keychunks[ci][4]
                    c0, c1 = crange(ci)
                    b = sctr % 3
                    sctr += 1
                    sbanks[it] = b
                    mm(ps[b][:, c0:c1], kT_ap(hp), q_sb[:, hp, hh, c0:c1], True, True, [kkey, "q_sb"], [PK(b)])

                def emit_PV(it):
                    hh, ci = it
                    _, v_ap, tspec, rmask, kkey, vkey, lo, hi = keychunks[ci]
                    c0, c1 = crange(ci)
                    b = sbanks[it]
                    n = M.ep_ctr % 4
                    M.ep_ctr += 1
                    E, P = M.Et[n], M.Pt[n]
                    act(E[:, c0:c1], ps[b][:, c0:c1], AF.Exp, [PK(b)], [("E", n)], scale=0.125)
                    src, skey = E, ("E", n)
                    h = 2 * hp + hh
                    if tspec is not None:
                        kind, e0 = tspec
                        if kind == "int":
                            tt(P[:, c0:c1], E[:, c0:c1], tabI[:, h, e0 * 64:e0 * 64 + (c1 - c0)], ALU.mult, [("E", n), "tab"], [("P", n)])
                        else:
                            tt(P[:, 0:T], E[:, 0:T], tab[:, h, e0 * 64:e0 * 64 + T], ALU.mult, [("E", n), "tab"], [("P", n)])
                            Pv = P[:, 0:T].rearrange("p (a b) -> p a b", b=64)
                            tt(Pv, Pv, rmask.unsqueeze(2).to_broadcast([128, 8, 64]), ALU.mult, [("P", n), "rowmask"], [("P", n)], eng="pool")
                        src, skey = P, ("P", n)
                    first, lastc = (ci == order[0]), (ci == order[-1])
                    mm(ps[OB[hh]][:, c0:c1], v_ap(hp), src[:, c0:c1], first, lastc, [skey, vkey], [PK(OB[hh])])
                    mm(ps[DB[hh]][:, c0:c1], ones, src[:, c0:c1], first, lastc, [skey, "ones"], [PK(DB[hh])])

                LA = 2
                for n_, it in enumerate(items):
                    emit_S(it)
                    if n_ >= LA:
                        emit_PV(items[n_ - LA])
                for it in items[len(items) - LA:]:
                    emit_PV(it)
                for hh in range(2):
                    pr = slice(64 * hh, 64 * hh + 64)
                    S.op("dve", lambda e, hh=hh, pr=pr: e.reciprocal(M.rden[pr, 0:T], ps[DB[hh]][pr, 0:T]), [PK(DB[hh])], [("rden", hh)])
                    tt(atT[pr, hp, 0:T], ps[OB[hh]][pr, 0:T], M.rden[pr, 0:T], ALU.mult, [PK(OB[hh]), ("rden", hh)], ["atT"])
            branches = ((2560, "w_fourier", 256, yT, ykey, 2), (3584, "w_conv", 256, cvT, "cvT", 2), (4608, "w_attn", 512, atT, "atT", 4))
            for bi, (gc0, wname, wrows, src, skey, nk) in enumerate(branches):
                wb, wbk = slab(M)
                wbv = wb.rearrange("p (k n) -> p k n", n=1024)
                ldwb(wbv[:, 0:nk, :], wname, l, 0, wrows, 0, DM, wbk)
                for half in range(2):
                    wg, wgk = slab(M)
                    wgv = wg.rearrange("p (k n) -> p k n", k=8)
                    ldwb(wgv, "w_in", l, 0, DM, gc0 + half * 512, gc0 + (half + 1) * 512, wgk)
                    for o4 in range(4):
                        oc = half * 4 + o4
                        bg = nbank()
                        for k in range(8):
                            mm(ps[bg][:, 0:T], wgv[:, k, o4 * 128:(o4 + 1) * 128], hT[:, k, 0:T], k == 0, k == 7, [wgk, hkey], [PK(bg)])
                        n = M.sg_ctr % 2
                        M.sg_ctr += 1
                        act(M.sg[n][:, 0:T], ps[bg][:, 0:T], AF.Sigmoid, [PK(bg)], [("sg", n)])
                        bp = nbank()
                        for k in range(nk):
                            mm(ps[bp][:, 0:T], wbv[:, k, oc * 128:(oc + 1) * 128], src[:, k, 0:T], k == 0, k == nk - 1,
                               [wbk, skey], [PK(bp)])
                        if bi == 0:
                            tt(mT[:, oc, 0:T], ps[bp][:, 0:T], M.sg[n][:, 0:T], ALU.mult, [PK(bp), ("sg", n)], [("mT", oc)])
                        else:
                            tt(M.mtmp[n][:, 0:T], ps[bp][:, 0:T], M.sg[n][:, 0:T], ALU.mult, [PK(bp), ("sg", n)], [("mtmp", n)])
                            tt(mT[:, oc, 0:T], mT[:, oc, 0:T], M.mtmp[n][:, 0:T], ALU.add, [("mT", oc), ("mtmp", n)], [("mT", oc)], eng="pool")
            mTk = [("mT", oc) for oc in range(8)]
            if getattr(M, "dbg", False) and T == 512:
                ld(D["dbg_at"], atT, ["atT"], [])
                ld(D["dbg_cv"], cvT, ["cvT"], [])
                ld(D["dbg_m"], mT, mTk, [])
            if pre_wo is not None:
                pre_wo()
            for hh in range(2):
                wo, wok = slab(M)
                wov = wo.rearrange("p (k n) -> p k n", k=8)
                ldwb(wov, "w_o", l, 0, DM, hh * 512, (hh + 1) * 512, wok)
                for t in range(nt):
                    b = nbank()
                    for k in range(8):
                        mm(ps[b][:, :], mT[:, k, t * 128:(t + 1) * 128], wov[:, k, :], k == 0, k == 7, mTk + [wok], [PK(b)])
                    n = M.sg_ctr % 2
                    M.sg_ctr += 1
                    tt(M.otmp[n], ps[b][:, :], g1bc[:, hh * 512:(hh + 1) * 512], ALU.mult, [PK(b), "g1bc"], [("otmp", n)])
                    tt(xres[:, t, hh * 512:(hh + 1) * 512], xres[:, t, hh * 512:(hh + 1) * 512], M.otmp[n], ALU.add,
                       [xkey, ("otmp", n)], [xkey], eng="pool")
            if getattr(M, "dbg", False) and T == 512:
                ld(D["dbg_xmix"], xres, [xkey], [])
            norm_transpose(xres, nt, Am, Bm, hT, [xkey], hkey, scr)
            first = True
            for s in range(8):
                w1, w1k = slab(M)
                w1v = w1.rearrange("p (k n) -> p k n", k=8)
                ldwb(w1v, "mlp_w1", l, 0, DM, s * 512, (s + 1) * 512, w1k)
                for c in range(4):
                    b = nbank()
                    for k in range(8):
                        mm(ps[b][:, 0:T], w1v[:, k, c * 128:(c + 1) * 128], hT[:, k, 0:T], k == 0, k == 7, [w1k, hkey], [PK(b)])
                    n = M.sg_ctr % 2
                    M.sg_ctr += 1
                    act(M.rr[n][:, 0:T], ps[b][:, 0:T], AF.Relu, [PK(b)], [("rr", n)])
                    tt(h1[:, s * 4 + c, 0:T], M.rr[n][:, 0:T], M.rr[n][:, 0:T], ALU.mult, [("rr", n)],
                       ["h1"] + (U1K if first else []))
                    first = False
            if getattr(M, "dbg", False) and T == 512:
                ld(D["dbg_h2"], hT, [hkey], [])
                ld(D["dbg_h1"], h1, ["h1"], [])
                M.dbg = False
            obanks = [(t, hh, nbank()) for t in range(nt) for hh in range(2)]
            lasttok = None
            for s in range(8):
                w2, w2k = slab(M)
                w2v = w2.rearrange("p (k n) -> p k n", k=4)
                ldwb(w2v, "mlp_w2", l, s * 512, (s + 1) * 512, 0, DM, w2k)
                for (t, hh, b) in obanks:
                    for k in range(4):
                        mm(ps[b][:, :], h1[:, s * 4 + k, t * 128:(t + 1) * 128], w2v[:, k, hh * 512:(hh + 1) * 512],
                           s == 0 and k == 0, s == 7 and k == 3, ["h1", w2k], [PK(b)])
            lasttok = ("c", "pe", len(S.ops["pe"]) - 1)
            for k_ in U1K:
                S.extra.setdefault(k_, set()).add(lasttok)
            for (t, hh, b) in obanks:
                n = M.sg_ctr % 2
                M.sg_ctr += 1
                tt(M.otmp[n], ps[b][:, :], g2bc[:, hh * 512:(hh + 1) * 512], ALU.mult, [PK(b), "g2bc"], [("otmp", n)])
                tt(xres[:, t, hh * 512:(hh + 1) * 512], xres[:, t, hh * 512:(hh + 1) * 512], M.otmp[n], ALU.add,
                   [xkey, ("otmp", n)], [xkey], eng="pool")

        def ctx_keychunks():
            kch = []
            for ci in range(2):
                kch.append((lambda hp, ci=ci: kc_sb[:, hp, ci * 128:(ci + 1) * 128],
                            lambda hp, ci=ci: vc_sb[:, ci, hp * 128:(hp + 1) * 128], None, None, "kc_sb", "vc_sb", None, None))
            return kch

        TMP.reset()
        scr = alloc_scr()
        wk1, wk2 = load_wk()
        xt = TMP.alloc([4, DM], F32)
        hxT = TMP.alloc([8, 512], BF16)
        cust = [TMP.alloc([512], F32) for _ in range(2)]
        kst = [TMP.alloc([512], BF16) for _ in range(4)]
        p1ctr = [0]
        XU = D["xiu"][0:TOK, :].rearrange("(r x) n -> r (x n)", x=64).rearrange("r (h c) -> r h c", c=64)

        def stg():
            n = p1ctr[0] % 4
            p1ctr[0] += 1
            return kst[n], ("kst", n)

        for i in range(NBLK):
            ld(xt, xin[i * 512:(i + 1) * 512, :].rearrange("(t p) d -> p t d", p=128), [xkin], ["xt"])
            norm_transpose(xt, 4, A1, B1, hxT, ["xt"], "hxT", scr)
            ld(D["hx_s"][i], hxT, ["hxT"], [("hx_s", i)])
            def sink_u(c, b, i=i):
                st_, sk_ = stg()
                S.op("dve", lambda e, st_=st_, b=b: e.tensor_copy(st_, ps[b][:, :]), [PK(b)], [sk_])
                ld(XU[8 * i:8 * i + 8, c * 128:(c + 1) * 128, :].rearrange("r h c -> h r c"),
                   st_.rearrange("p (r c) -> p r c", c=64), [sk_], ["xiu"])
            proj_fm(hxT, 512, wk1, "wk1", 0, 2, sink_u, "hxT")

            def sink_z(c, b, i=i):
                if c < 2:
                    act(cust[c], ps[b][:, :], AF.Copy, [PK(b)], [("cust", c)])
                else:
                    st_, sk_ = stg()
                    tt(st_, ps[b][:, :], cust[c - 2], ALU.mult, [PK(b), ("cust", c - 2)], [sk_])
                    ld(D["z_s"][:, c - 2, 16 + i * 512:16 + (i + 1) * 512], st_, [sk_], ["z_s"])
            proj_fm(hxT, 512, wk1, "wk1", 256, 4, sink_z, "hxT")

            def sink_k(c, b, i=i):
                st_, sk_ = stg()
                S.op("dve", lambda e, st_=st_, b=b: e.tensor_copy(st_, ps[b][:, :]), [PK(b)], [sk_])
                ld(D["k_s"][:, c, (4 + 8 * i) * 64:(4 + 8 * i) * 64 + 512], st_, [sk_], ["k_s"])
            proj_fm(hxT, 512, wk2, "wk2", 0, 4, sink_k, "hxT")
            for t in range(4):
                b = nbank()
                for k in range(8):
                    mm(ps[b][:, :], hxT[:, k, t * 128:(t + 1) * 128], wk2[:, k, 512:1024], k == 0, k == 7, ["wk2", "hxT"], [PK(b)])
                st_, sk_ = stg()
                S.op("dve", lambda e, st_=st_, b=b: e.tensor_copy(st_, ps[b][:, :]), [PK(b)], [sk_])
                ld(D["v_s"][2 + 4 * i + t], st_, [sk_], ["v_s"])

        XI, XO = D["exch_in"], D["exch_out"]
        ld(XI[0:512, :].rearrange("(p c) n -> p c n", c=4), D["k_s"][:, :, 256:512], ["k_s"], ["exch_in"])
        ld(XI[512:1024, :].rearrange("(p c) n -> p c n", c=4), D["k_s"][:, :, 64 * 64:68 * 64], ["k_s"], ["exch_in"])
        ld(XI[1024:1536, :].rearrange("(c p h) n -> c p (h n)", c=2, p=128), D["v_s"][2:4], ["v_s"], ["exch_in"])
        ld(XI[1536:2048, :].rearrange("(c p h) n -> c p (h n)", c=2, p=128), D["v_s"][32:34], ["v_s"], ["exch_in"])
        ld(XI[2048:2064, :].rearrange("r (pp c t) -> (r pp) c t", pp=8, c=2, t=16), D["z_s"][:, :, 16:32], ["z_s"], ["exch_in"])
        ld(XI[2064:2080, :].rearrange("r (pp c t) -> (r pp) c t", pp=8, c=2, t=16), D["z_s"][:, :, TOK:TOK + 16], ["z_s"], ["exch_in"])
        RG = [[2 * g, 2 * g + 1] for g in range(ncores // 2)]
        S.op("pool", lambda e: e.collective_compute("AllGather", ALU.bypass, replica_groups=RG, ins=[D["xiu"]], outs=[D["xou"]]),
             ["xiu"], ["xou"], force_sig=True)
        S.op("pool", lambda e: e.collective_compute("AllGather", ALU.bypass, replica_groups=RG, ins=[XI], outs=[XO]),
             ["exch_in"], ["exch_out"], force_sig=True)
        S.barrier(skip_cc=True)
        if l == 0:
            phase0([1])
        TMP.reset()
        scr = alloc_scr()
        hcT = TMP.alloc([8, 256], BF16)
        if not last:
            ucT = TMP.alloc([2, 256], BF16)
            zc = TMP.alloc([2, 258], BF16)
            cuc = TMP.alloc([2, 256], F32)
        markA = TMP.cur
        wk1, wk2 = load_wk()
        norm_transpose(ctx_sb, 2, cA1, cB1, hcT, ["ctx_sb"], "hcT", scr)

        def sink_kc(c, b):
            act(kc_sb[:, c, :], ps[b][:, 0:256], AF.Copy, [PK(b)], ["kc_sb"])
        proj_fm(hcT, 256, wk2, "wk2", 0, 4, sink_kc, "hcT")
        for t in range(2):
            b = nbank()
            for k in range(8):
                mm(ps[b][:, :], hcT[:, k, t * 128:(t + 1) * 128], wk2[:, k, 512:1024], k == 0, k == 7, ["wk2", "hcT"], [PK(b)])
            act(vc_sb[:, t, :], ps[b][:, :], AF.Copy, [PK(b)], ["vc_sb"])
        if not last:
            S.op("dve", lambda e: e.memset(zc, 0.0), [], ["zc"])

            def sink_c1(c, b):
                if c < 2:
                    act(ucT[:, c, :], ps[b][:, 0:256], AF.Copy, [PK(b)], ["ucT"])
                elif c < 4:
                    act(cuc[:, c - 2, :], ps[b][:, 0:256], AF.Copy, [PK(b)], [("cuc", c - 2)])
                else:
                    tt(zc[:, c - 4, 1:257], ps[b][:, 0:256], cuc[:, c - 4, :], ALU.mult, [PK(b), ("cuc", c - 4)], ["zc"])
            proj_fm(hcT, 256, wk1, "wk1", 0, 6, sink_c1, "hcT")
        S.barrier()
        if not last:
            TMP.cur = markA
            M = alloc_mixer()
            ld(g1bc, D["gbc"][l, 1, 0], [("gbc", l, 1)], ["g1bc"])
            ld(g2bc, D["gbc"][l, 1, 1], [("gbc", l, 1)], ["g2bc"])
            chc = TMP.alloc([2, 128], BF16)
            d256 = TMP.alloc([2, 2, 256], BF16)
            zri = TMP.alloc([2, 2, 256], BF16)
            ycT = TMP.alloc([2, 256], BF16)
            ld(chc, D["chc"], [], ["chc"])
            ld(d256, D["dft256"], [], ["d256"])
            for ri in range(2):
                for tc in range(2):
                    b = nbank()
                    for cc in range(2):
                        mm(ps[b][:, cc * 128:(cc + 1) * 128], ucT[:, cc, tc * 128:(tc + 1) * 128], chc[:, ri, :], True, True,
                           ["ucT", "chc"], [PK(b)])
                    act(zri[:, ri, tc, :], ps[b][:, 0:256], AF.Copy, [PK(b)], ["zri"])
            for cc in range(2):
                b = nbank()
                n_ = 0
                for ri in range(2):
                    for tc in range(2):
                        mm(ps[b][:, 0:256], zri[:, ri, tc, cc * 128:(cc + 1) * 128], d256[:, ri, tc, :], n_ == 0, n_ == 3,
                           ["zri", "d256"], [PK(b)])
                        n_ += 1
                act(ycT[:, cc, :], ps[b][:, 0:256], AF.Copy, [PK(b)], ["ycT"])
            WMODE["cast_store"] = True
            mixer_block(M, scr, 256, hcT, "hcT", ctx_sb, "ctx_sb", ycT, "ycT", zc, "zc", ctx_keychunks(), cA2, cB2)
            WMODE["cast_store"] = False
            S.barrier()

        ld(D["k_s"][:, :, 0:256], XO[512:1024, :].rearrange("(p c) n -> p c n", c=4), ["exch_out"], ["k_s"])
        ld(D["k_s"][:, :, 68 * 64:72 * 64], XO[RX:RX + 512, :].rearrange("(p c) n -> p c n", c=4), ["exch_out"], ["k_s"])
        ld(D["v_s"][0:2], XO[1536:2048, :].rearrange("(c p h) n -> c p (h n)", c=2, p=128), ["exch_out"], ["v_s"])
        ld(D["v_s"][34:36], XO[RX + 1024:RX + 1536, :].rearrange("(c p h) n -> c p (h n)", c=2, p=128), ["exch_out"], ["v_s"])
        ld(D["z_s"][:, :, 0:16], XO[2064:2080, :].rearrange("r (pp c t) -> (r pp) c t", pp=8, c=2, t=16), ["exch_out"], ["z_s"])
        ld(D["z_s"][:, :, TOK + 16:TOK + 32], XO[RX + 2048:RX + 2064, :].rearrange("r (pp c t) -> (r pp) c t", pp=8, c=2, t=16), ["exch_out"], ["z_s"])
        S.barrier()
        if stage <= 2:
            return

        TMP.reset()
        cs128 = TMP.alloc([256], BF16)
        ch3 = TMP.alloc([2, 128], BF16)
        gtab = TMP.alloc([128, 96], BF16)
        U = TMP.alloc([128, 64], BF16)
        Asb = TMP.alloc([64, 256], BF16)
        Xsb = TMP.alloc([2, TOK], BF16)
        yst = [TMP.alloc([512], BF16) for _ in range(2)]
        ld(cs128, D["cs128"], [], ["cs128"])
        ld(ch3, D["ch3"], [], ["ch3"])
        ld(gtab, D["gtab"], [], ["gtab"])
        Uf = U.rearrange("p h c -> p (h c)")
        Xv = Xsb.rearrange("p r (k2 k1) -> p r k2 k1", k1=128)
        for cc in range(2):
            XOU0 = D["xou"][0:TOK, :].rearrange("(r x) n -> r (x n)", x=64).rearrange("r (h c) -> r h c", c=64)
            XOU1 = D["xou"][TOK:2 * TOK, :].rearrange("(r x) n -> r (x n)", x=64).rearrange("r (h c) -> r h c", c=64)
            ld(U[0:64], XOU0[:, cc * 128:(cc + 1) * 128, :], ["xou"], ["U"])
            ld(U[64:128], XOU1[:, cc * 128:(cc + 1) * 128, :], ["xou"], ["U"])
            for qq in range(32):
                b = nbank()
                for j in range(2):
                    q = 2 * qq + j
                    mm(ps[b][:, j * 256:(j + 1) * 256], Uf[:, 128 * q:128 * q + 128], cs128, True, True, ["U", "cs128"], [PK(b)])
                src = ps[b][:, :].rearrange("p (a b) -> p a b", a=2)
                if qq % 2 == 0:
                    act(Asb[:, 2 * qq:2 * qq + 2, :], src, AF.Copy, [PK(b)], ["Asb"])
                else:
                    S.op("dve", lambda e, qq=qq, src=src: e.tensor_copy(Asb[:, 2 * qq:2 * qq + 2, :], src), [PK(b)], ["Asb"])
            for g8 in range(16):
                b = nbank()
                for kl in range(8):
                    k1 = g8 * 8 + kl
                    for j in range(2):
                        pr = slice(64 * j, 64 * j + 64)
                        mm(ps[b][pr, kl * 64:(kl + 1) * 64], Asb[pr, :, k1], gtab[pr, k1, 32:96], True, False,
                           ["Asb", "gtab"], [PK(b)], tp=(64 * j, 64 * j))
                        mm(ps[b][pr, kl * 64:(kl + 1) * 64], Asb[pr, :, 128 + k1], gtab[pr, k1, 0:64], False, True,
                           ["Asb", "gtab"], [PK(b)], tp=(64 * j, 64 * j))
                src = ps[b][:, :].rearrange("p (kl r k2) -> p r k2 kl", kl=8, r=2, k2=32)
                dst = Xv[:, :, :, g8 * 8:(g8 + 1) * 8]
                if g8 % 2 == 0:
                    act(dst, src, AF.Copy, [PK(b)], ["Xsb"])
                else:
                    S.op("dve", lambda e, dst=dst, src=src: e.tensor_copy(dst, src), [PK(b)], ["Xsb"])
            for i in range(NBLK):
                b = nbank()
                mm(ps[b][:, :], ch3[:, 0, :], Xsb[:, 0, i * 512:(i + 1) * 512], True, False, ["Xsb", "ch3"], [PK(b)])
                mm(ps[b][:, :], ch3[:, 1, :], Xsb[:, 1, i * 512:(i + 1) * 512], False, True, ["Xsb", "ch3"], [PK(b)])
                y = yst[i % 2]
                act(y, ps[b][:, :], AF.Copy, [PK(b)], [("yst", i % 2)])
                ld(D["y_s"][:, cc, i * 512:(i + 1) * 512], y, [("yst", i % 2)], ["y_s"])
        S.barrier()
        if stage <= 3:
            return

        TMP.reset()
        scr = alloc_scr()
        junk, ss, xn = scr
        M = alloc_mixer()
        M.dbg = ("mix" in debug) and l == dbg_layer
        hx2 = [TMP.alloc([8, 512], BF16) for _ in range(2)]
        xt2 = TMP.alloc([4, DM], F32)
        fgb = TMP.alloc([DM], F32)
        ld(g1bc, D["gbc"][l, 0, 0], [("gbc", l, 0)], ["g1bc"])
        ld(g2bc, D["gbc"][l, 0, 1], [("gbc", l, 0)], ["g2bc"])
        if last:
            ld(fgb, D["final_g"].partition_broadcast(128), [], ["fgb"])
        nblk2 = NBLK if stage > 4 else 1
        ld(hx2[0], D["hx_s"][0], [("hx_s", 0)], [("hxT", 0)])
        for i in range(nblk2):
            hxT, hkey_ = hx2[i % 2], ("hxT", i % 2)
            if i + 1 < nblk2:
                ld(hx2[(i + 1) % 2], D["hx_s"][i + 1], [("hx_s", i + 1)], [("hxT", (i + 1) % 2)])
            ld(M.zwin, D["z_s"][:, :, 15 + i * 512:15 + i * 512 + 514], ["z_s"], ["zwin"])
            ld(M.ywin, D["y_s"][:, :, i * 512:(i + 1) * 512], ["y_s"], ["ywin"])
            ld(M.kwin, D["k_s"][:, :, 8 * i * 64:(8 * i + 16) * 64], ["k_s"], ["kwin"])
            ld(M.vwin, D["v_s"][4 * i:4 * i + 8].rearrange("c p n -> p c n"), ["v_s"], ["vwin"])

            def pre_wo(i=i):
                ld(xt2, xin[i * 512:(i + 1) * 512, :].rearrange("(t p) d -> p t d", p=128), [xkin], ["xt2"])
            if i == 0:
                S.op("dve", lambda e: e.tensor_scalar(M.zwin[:, :, 0:1], M.zwin[:, :, 0:1], zflag[:, 0:1], None, ALU.mult),
                     ["zwin", "zflag"], ["zwin"])
            if i == NBLK - 1:
                S.op("dve", lambda e: e.tensor_scalar(M.zwin[:, :, 513:514], M.zwin[:, :, 513:514], zflag[:, 1:2], None, ALU.mult),
                     ["zwin", "zflag"], ["zwin"])
            kch = []
            RNG = [(0, 1), (0, 3), (0, 5), (0, 7), (1, 7), (3, 7), (5, 7), (7, 7)]
            for j in range(8):
                if 0 < i < NBLK - 1:
                    lo, hi = RNG[j]
                    kch.append((lambda hp, j=j: M.kwin[:, hp, j * 128:(j + 1) * 128],
                                lambda hp, j=j: M.vwin[:, j, hp * 128:(hp + 1) * 128], ("int", 7 - 2 * j + lo), None, "kwin", "vwin", lo, hi))
                else:
                    kch.append((lambda hp, j=j: M.kwin[:, hp, j * 128:(j + 1) * 128],
                                lambda hp, j=j: M.vwin[:, j, hp * 128:(hp + 1) * 128], ("full", 14 - 2 * j), rowmask[:, i, j, :], "kwin", "vwin", None, None))
            kch += ctx_keychunks()
            mixer_block(M, scr, 512, hxT, hkey_, xt2, "xt2", M.ywin, "ywin", M.zwin, "zwin", kch, A2, B2, pre_wo=pre_wo)
            if last:
                for t in range(4):
                    act(xn[:, t, :], xt2[:, t, :], AF.Square, ["xt2"], [("xn", t), ("ss", t)], accum_out=ss[:, t:t + 1])
                    S.op("dve", lambda e, t=t: e.tensor_scalar(ss[:, t:t + 1], ss[:, t:t + 1], 1.0 / DM, EPS, ALU.mult, ALU.add),
                         [("ss", t)], [("ss", t)])
                    act(ss[:, t:t + 1], ss[:, t:t + 1], AF.Sqrt, [("ss", t)], [("ss", t)])
                    S.op("dve", lambda e, t=t: e.reciprocal(ss[:, t:t + 1], ss[:, t:t + 1]), [("ss", t)], [("ss", t)])
                    S.op("dve", lambda e, t=t: e.scalar_tensor_tensor(xt2[:, t, :], xt2[:, t, :], ss[:, t:t + 1], fgb, ALU.mult, ALU.mult),
                         ["xt2", ("ss", t), "fgb"], ["xt2"])
            ld(xout[i * 512:(i + 1) * 512, :].rearrange("(t p) d -> p t d", p=128), xt2, ["xt2"], [xkout, ("blkdone", l, i)])
            if not last and stage > 5:
                pcs = [pc for n_, pc in enumerate(CONV_PIECES) if n_ % NBLK == i]
                convert_pieces(1, pcs, ("blkdone", l, i))
        S.barrier()

    for l in range(2):
        layer(l)
        if stage <= 5:
            break

    S.barrier()
    DUMPS = {"kc_sb": ([128, 4, 256], BF16, kc_sb), "vc_sb": ([128, 2, 512], BF16, vc_sb), "ctx_sb": ([128, 2, DM], F32, ctx_sb),
             "acol": ([128, 2, 2, 4, 8], F32, acol), "modc": ([128, 2, 48, 2], F32, modc), "tab": ([128, 8, TABW], BF16, tab)}
    for nm in debug:
        if nm == "mix":
            continue
        if nm in DUMPS:
            shp, dt, src = DUMPS[nm]
        else:
            src = D[nm]
            shp, dt = list(src.shape), src.dtype
        dout("dbg_" + nm, shp, dt)
        ld(D["dbg_" + nm], src, [], [])
    S.wait_all("sp")

    nsig = S.plan()
    import contextlib
    with contextlib.ExitStack() as st:
        esems = {e: [st.enter_context(nc.semaphore(f"s_{e}_{n}")) for n in range(nsig[e] // EPOCH + 1)] for e in Sched.ENGS}
        dsems = {q: [st.enter_context(nc.semaphore(f"d_{q}_{n}")) for n in range(NSLOT)] for q in ("sp", "pool")}
        block = st.enter_context(nc.Block())

        @block.tensor
        def _(h):
            S.emit("pe", h, esems, dsems)

        @block.scalar
        def _(h):
            S.emit("act", h, esems, dsems)

        @block.vector
        def _(h):
            S.emit("dve", h, esems, dsems)

        @block.gpsimd
        def _(h):
            S.emit("pool", h, esems, dsems)

        @block.sync
        def _(h):
            S.emit("sp", h, esems, dsems)
    return nc


def make_in_maps(inp):
    f32 = lambda a: np.ascontiguousarray(np.asarray(a, dtype=np.float32))
    HC = host_constants()
    CC = [core_constants(h) for h in range(2)]
    shared = {
        "ada_w": f32(inp["ada_w"]), "ada_b": f32(inp["ada_b"]),
        "ada_bT": f32(np.asarray(inp["ada_b"]).reshape(2, 48, 128).transpose(0, 2, 1)),
        "n1gT": f32(np.asarray(inp["norm1_g"]).reshape(2, 8, 128).transpose(0, 2, 1)),
        "n2gT": f32(np.asarray(inp["norm2_g"]).reshape(2, 8, 128).transpose(0, 2, 1)),
        "final_g": f32(np.asarray(inp["final_g"]).reshape(1, DM)),
        "w_in": f32(inp["w_in"]),
        "conv_wT": f32(np.asarray(inp["conv_w"]).reshape(2, 3, 2, 128).transpose(0, 3, 2, 1).reshape(2, 128, 6)),
        "rbt": gather_relbias(np.asarray(inp["rel_bias"], dtype=np.float32)),
        "w_fourier": f32(inp["w_fourier"]), "w_conv": f32(inp["w_conv"]), "w_attn": f32(inp["w_attn"]),
        "w_o": f32(inp["w_o"]), "mlp_w1": f32(inp["mlp_w1"]), "mlp_w2": f32(inp["mlp_w2"]),
    }
    shared.update(HC)
    maps = []
    x = np.asarray(inp["x"]); c = np.asarray(inp["c"]); ctx = np.asarray(inp["ctx"]); c_ctx = np.asarray(inp["c_ctx"])
    for core in range(NCORES):
        b, half = core // 2, core % 2
        m = dict(shared)
        m.update(CC[half])
        m["x"] = f32(x[b, half * TOK:(half + 1) * TOK])
        m["ctx"] = f32(ctx[b])
        cv = np.stack([c[b], c_ctx], axis=1).reshape(8, 128, 2).transpose(1, 0, 2).reshape(128, 16)
        m["cvecT"] = f32(cv)
        maps.append(m)
    return maps


_NC_CACHE = {}


def kernel(**inputs):
    if "nc" not in _NC_CACHE:
        _NC_CACHE["nc"] = build_nc()
    nc = _NC_CACHE["nc"]
    maps = make_in_maps(inputs)
    res = run_bass_kernel_spmd(nc, maps, core_ids=list(range(NCORES)))
    out = np.empty((4, 8192, DM), np.float32)
    for core in range(NCORES):
        b, half = core // 2, core % 2
        out[b, half * TOK:(half + 1) * TOK] = res.results[core]["out"]
    return out
```

```python
import numpy as np
import ml_dtypes
import concourse.bass as bass
import concourse.mybir as mybir
from concourse.bass_utils import run_bass_kernel_spmd

F32 = mybir.dt.float32
BF16 = mybir.dt.bfloat16
AF = mybir.ActivationFunctionType
ALU = mybir.AluOpType

NSLOT = 16
EPOCH = 8000
NCORES = 8
DM = 1024
TOK = 4096
NBLK = 8
INW = 5632
RX = 2080
NE = 22
TABW = NE * 64
EPS = 1e-6


class Sched:
    ENGS = ["pe", "act", "dve", "pool", "sp"]

    def __init__(self):
        self.ops = {e: [] for e in self.ENGS}
        self.res = {}
        self.ndma = {e: 0 for e in self.ENGS}
        self.pending = {e: set() for e in self.ENGS}
        self.extra = {}

    def _collect(self, eng, reads, writes):
        deps = set(self.pending[eng])
        self.pending[eng] = set()
        for k in reads:
            r = self.res.get(k)
            if r is not None and r[0] is not None:
                t = r[0]
                if t[0] == "c" and t[1] == eng and eng in ("pe", "sp"):
                    continue
                deps.add(t)
        for k in writes:
            if k in self.extra:
                deps |= self.extra.pop(k)
            r = self.res.get(k)
            if r is not None:
                for t in ([r[0]] if r[0] is not None else []) + list(r[1]):
                    if t[0] == "c" and t[1] == eng and eng == "pe":
                        continue
                    deps.add(t)
        return deps

    def _update(self, reads, writes, tok):
        for k in reads:
            r = self.res.setdefault(k, [None, []])
            r[1].append(tok)
        for k in writes:
            self.res[k] = [tok, []]

    def op(self, eng, fn, reads=(), writes=(), force_sig=False):
        deps = self._collect(eng, reads, writes)
        tok = ("c", eng, len(self.ops[eng]))
        self.ops[eng].append(dict(fn=fn, deps=deps, tok=tok, dma=None, force_sig=force_sig, cc=force_sig))
        self._update(reads, writes, tok)
        return tok

    def dma(self, eng, fn, reads=(), writes=()):
        deps = self._collect(eng, reads, writes)
        j = self.ndma[eng]
        self.ndma[eng] += 1
        if j >= NSLOT:
            deps.add(("d", eng, j - NSLOT))
        tok = ("d", eng, j)
        self.ops[eng].append(dict(fn=fn, deps=deps, tok=tok, dma=j))
        self._update(reads, writes, tok)
        return tok

    def barrier(self, skip_cc=False):
        toks = set()
        for e in self.ENGS:
            last = None
            for o in reversed(self.ops[e]):
                if skip_cc and o.get("cc"):
                    continue
                if o["dma"] is None and o["fn"] is not None:
                    last = o["tok"]
                    break
            if last is not None:
                toks.add(last)
        for q in self.ENGS:
            for j in range(max(0, self.ndma[q] - NSLOT), self.ndma[q]):
                toks.add(("d", q, j))
        for e in self.ENGS:
            self.pending[e] |= toks

    def wait_all(self, eng):
        self.barrier()
        deps = set(self.pending[eng])
        self.pending[eng] = set()
        self.ops[eng].append(dict(fn=None, deps=deps, tok=("c", eng, len(self.ops[eng])), dma=None))

    def plan(self):
        sig = {e: set() for e in self.ENGS}
        for e in self.ENGS:
            fc = {}
            fd = {}
            for idx, o in enumerate(self.ops[e]):
                waits = []
                for t in sorted(o["deps"], key=str):
                    if t[0] == "c":
                        _, e2, i2 = t
                        if e2 == e and i2 >= idx:
                            continue
                        if fc.get(e2, -1) >= i2:
                            continue
                        fc[e2] = i2
                        waits.append(t)
                        sig[e2].add(i2)
                    else:
                        q, j = t[1], t[2]
                        slot, val = (q, j % NSLOT), j // NSLOT
                        if fd.get(slot, -1) >= val:
                            continue
                        fd[slot] = val
                        waits.append(t)
                o["waits"] = waits
        self.ordinal = {}
        self.nsig = {}
        for e in self.ENGS:
            for idx, o in enumerate(self.ops[e]):
                if o.get("force_sig"):
                    sig[e].add(idx)
        for e in self.ENGS:
            for n, i in enumerate(sorted(sig[e])):
                self.ordinal[(e, i)] = n
            self.nsig[e] = len(sig[e])
        return self.nsig

    def emit(self, eng, h, esems, dsems):
        for idx, o in enumerate(self.ops[eng]):
            for t in o["waits"]:
                if t[0] == "c":
                    n = self.ordinal[(t[1], t[2])]
                    h.wait_ge(esems[t[1]][n // EPOCH], n % EPOCH + 1)
                else:
                    q, j = t[1], t[2]
                    h.wait_ge(dsems[q][j % NSLOT], 16 * (j // NSLOT + 1))
            if o["fn"] is None:
                continue
            ins = o["fn"](h)
            if o["dma"] is not None:
                ins.then_inc(dsems[eng][o["dma"] % NSLOT], 16)
            elif (eng, idx) in self.ordinal:
                n = self.ordinal[(eng, idx)]
                ins.then_inc(esems[eng][n // EPOCH], 1)


def _bf(a):
    return np.ascontiguousarray(a.astype(ml_dtypes.bfloat16))


def host_constants():
    C = {}
    C["ident"] = _bf(np.eye(128, dtype=np.float32))
    r = np.arange(128)[:, None].astype(np.float64)
    k1 = np.arange(128)[None, :].astype(np.float64)
    ang = 2 * np.pi * r * k1 / 128.0
    C["cs128"] = _bf(np.concatenate([np.cos(ang), -np.sin(ang)], axis=1))
    norm = 1.0 / np.sqrt(8192.0 * 64.0)
    p = np.arange(128)
    chp = 2 * (p % 64) + (p // 64)
    co = np.arange(128)
    same = (chp[:, None] // 64) == (co[None, :] // 64)
    a3 = 2 * np.pi * (chp[:, None] % 64) * (co[None, :] % 64) / 64.0
    C["ch3"] = _bf(np.stack([np.cos(a3) * same * norm, np.sin(a3) * same * norm], 0).transpose(1, 0, 2))
    same2 = (p[:, None] // 64) == (co[None, :] // 64)
    a4 = 2 * np.pi * (p[:, None] % 64) * (co[None, :] % 64) / 64.0
    C["chc"] = _bf(np.stack([np.cos(a4) * same2, np.sin(a4) * same2], 0).transpose(1, 0, 2))
    n = np.arange(256)[:, None].astype(np.float64)
    k = np.arange(256)[None, :].astype(np.float64)
    a5 = 2 * np.pi * n * k / 256.0
    nc_ = 1.0 / np.sqrt(256.0 * 64.0)
    t = np.stack([np.cos(a5) * nc_, -np.sin(a5) * nc_], 0)
    C["dft256"] = _bf(t.reshape(2, 2, 128, 256).transpose(2, 0, 1, 3))
    pp = np.arange(128)
    krl = pp // 64
    kc = pp % 64
    e = np.arange(NE)
    qc = np.arange(64)
    dr = 17 - e[None, :, None] + krl[:, None, None]
    cs = np.clip(qc - 8, 0, 48)
    colin = (kc[:, None, None] >= cs[None, None, :]) & (kc[:, None, None] < cs[None, None, :] + 16)
    m = (dr >= 0) & (dr <= 14) & colin
    C["maskF"] = _bf(m.reshape(128, TABW).astype(np.float32))
    mi = m & (dr >= 3) & (dr <= 10)
    C["maskI"] = _bf(mi[:, 7:16, :].reshape(128, 9 * 64).astype(np.float32))
    C["ones"] = _bf(np.ones((128, 128), np.float32))
    return C


def core_constants(half):
    C = {}
    p = np.arange(128)
    c = (p % 64)[:, None, None].astype(np.float64)
    k1 = np.arange(128)[None, :, None].astype(np.float64)
    k2 = (32 * half + np.arange(32))[None, None, :].astype(np.float64)
    ang = 2 * np.pi * c * (k1 + 128.0 * k2) / 8192.0
    gr, gi = np.cos(ang), -np.sin(ang)
    C["gtab"] = _bf(np.concatenate([-gi, gr, gi], axis=2))
    krl = p // 64
    rm = np.zeros((128, NBLK, 8, 8), np.float32)
    for i in range(NBLK):
        for j in range(8):
            for q in range(8):
                rq = 64 * half + 8 * i + q
                kr = 64 * half + 8 * i - 4 + 2 * j + krl
                rs = min(max(rq - 4, 0), 120)
                rm[:, i, j, q] = ((kr >= 0) & (kr < 128) & (kr >= rs) & (kr < rs + 8)).astype(np.float32)
    C["rowmask"] = _bf(rm)
    zf = np.zeros((128, 2), np.float32)
    zf[:, 0] = 0.0 if half == 0 else 1.0
    zf[:, 1] = 0.0 if half == 1 else 1.0
    C["zflag"] = zf
    return C


def gather_relbias(rel_bias):
    p = np.arange(128)
    krl = (p // 64)[:, None, None]
    kc = (p % 64)[:, None, None]
    e = np.arange(NE)[None, :, None]
    qc = np.arange(64)[None, None, :]
    dr = np.clip(17 - e + krl, 0, 14) + 0 * qc
    dc = np.clip(kc - qc, -15, 15) + 15 + 0 * e
    t = rel_bias[:, :, dr, dc]
    return np.ascontiguousarray(t.reshape(rel_bias.shape[0], rel_bias.shape[1], 128, TABW).astype(np.float32))


def build_nc(stage=99, debug=(), ncores=NCORES):
    nc = bass.Bass("TRN2", target_bir_lowering=False)
    S = Sched()
    D = {}

    def din(name, shape, dt=F32):
        D[name] = nc.dram_tensor(name, list(shape), dt, kind="ExternalInput").ap()

    def dscr(name, shape, dt):
        D[name] = nc.dram_tensor(name, list(shape), dt).ap()

    def dout(name, shape, dt=F32):
        D[name] = nc.dram_tensor(name, list(shape), dt, kind="ExternalOutput").ap()

    din("x", [TOK, DM]); din("ctx", [256, DM]); din("cvecT", [128, 16])
    din("ada_w", [2, DM, 6 * DM]); din("ada_bT", [2, 128, 48]); din("ada_b", [2, 6 * DM])
    din("n1gT", [2, 128, 8]); din("n2gT", [2, 128, 8]); din("final_g", [1, DM])
    din("w_in", [2, DM, INW]); din("conv_wT", [2, 128, 6]); din("rbt", [2, 8, 128, TABW])
    din("w_fourier", [2, 256, DM]); din("w_conv", [2, 256, DM]); din("w_attn", [2, 512, DM])
    din("w_o", [2, DM, DM]); din("mlp_w1", [2, DM, 4 * DM]); din("mlp_w2", [2, 4 * DM, DM])
    din("ident", [128, 128], BF16); din("cs128", [128, 256], BF16); din("ch3", [128, 2, 128], BF16)
    din("chc", [128, 2, 128], BF16); din("dft256", [128, 2, 2, 256], BF16); din("maskF", [128, TABW], BF16); din("maskI", [128, 576], BF16)
    din("ones", [128, 128], BF16); din("gtab", [128, 128, 96], BF16); din("rowmask", [128, NBLK, 8, 8], BF16)
    din("zflag", [128, 2])
    dout("out", [TOK, DM])
    dscr("x1", [TOK, DM], F32)
    dscr("hx_s", [NBLK, 128, 8, 512], BF16)
    dscr("xiu", [TOK, 256], BF16)
    dscr("xou", [2 * TOK, 256], BF16)
    dscr("exch_in", [RX, 256], BF16)
    dscr("exch_out", [2 * RX, 256], BF16)
    dscr("k_s", [128, 4, 72 * 64], BF16)
    dscr("v_s", [36, 128, 512], BF16)
    dscr("z_s", [128, 2, TOK + 32], BF16)
    dscr("y_s", [128, 2, TOK], BF16)
    dscr("gbc", [2, 2, 2, 128, DM], F32)
    WSHP = {"w_in": [DM, INW], "w_fourier": [256, DM], "w_conv": [256, DM], "w_attn": [512, DM], "w_o": [DM, DM],
            "mlp_w1": [DM, 4 * DM], "mlp_w2": [4 * DM, DM]}
    for nm_, shp_ in WSHP.items():
        dscr("b_" + nm_, [2] + shp_, BF16)

    dbg_layer = 1 if "L1" in debug else 0
    debug = [d_ for d_ in debug if d_ != "L1"]
    if "mix" in debug:
        dout("dbg_at", [128, 4, 512], BF16); dout("dbg_cv", [128, 2, 512], BF16); dout("dbg_m", [128, 8, 512], BF16)
        dout("dbg_xmix", [128, 4, DM], F32)
        dout("dbg_h2", [128, 8, 512], BF16); dout("dbg_h1", [128, 32, 512], BF16)
    ARENA = 212000
    arena = nc.alloc_sbuf_tensor("arena", [128, ARENA // 2], BF16)
    ps = [nc.alloc_psum_tensor(f"ps{i}", [128, 512], F32) for i in range(8)]

    class Region:
        def __init__(self, base, size):
            self.base, self.size, self.cur = base, size, base

        def reset(self):
            self.cur = self.base

        def alloc(self, shape, dt):
            n = int(np.prod(shape))
            nb = n * (4 if dt == F32 else 2)
            off = (self.cur + 63) // 64 * 64
            assert off + nb <= self.base + self.size, ("arena overflow", shape, off + nb - self.base, self.size)
            self.cur = off + nb
            ap = arena[:, off // 2:(off + nb) // 2]
            if dt == F32:
                ap = ap.bitcast(F32)
            if len(shape) == 2:
                ap = ap.rearrange("p (a b) -> p a b", b=int(shape[1]))
            elif len(shape) == 3:
                ap = ap.rearrange("p (a b c) -> p a b c", b=int(shape[1]), c=int(shape[2]))
            elif len(shape) == 4:
                ap = ap.rearrange("p (a b c d) -> p a b c d", b=int(shape[1]), c=int(shape[2]), d=int(shape[3]))
            return ap

    PERS = Region(0, 58000)
    TMP = Region(58000, ARENA - 58000)

    bank_ctr = [0]

    def nbank():
        b = bank_ctr[0] % 8
        bank_ctr[0] += 1
        return b

    def PK(b):
        return ("ps", b)

    def mm(out, lhsT, rhs, start, stop, reads, writes, tp=None):
        if tp is None:
            S.op("pe", lambda e: e.matmul(out, lhsT, rhs, start=start, stop=stop), reads, writes)
        else:
            S.op("pe", lambda e: e.matmul(out, lhsT, rhs, start=start, stop=stop, tile_position=tp), reads, writes)

    def act(out, in_, func, reads, writes, **kw):
        S.op("act", lambda e: e.activation(out, in_, func, **kw), reads, writes)

    def tt(out, in0, in1, op, reads, writes, eng="dve"):
        S.op(eng, lambda e: e.tensor_tensor(out, in0, in1, op), reads, writes)

    def ld(out, in_, reads, writes, eng="sp", **kw):
        S.dma(eng, lambda e: e.dma_start(out=out, in_=in_, **kw), reads, writes)

    def ldw(out, in_, reads, writes):
        S.dma("pool", lambda e: e.dma_start(out=out, in_=in_), reads, writes)

    def wview(name, l, r0, r1, c0, c1):
        return D[name][l, r0:r1, c0:c1].rearrange("(k p) n -> p k n", p=128)

    WMODE = {"cast_store": False}

    def ldwb(out, name, l, r0, r1, c0, c1, wkey):
        if WMODE["cast_store"]:
            ldw(out, wview(name, l, r0, r1, c0, c1), [], [wkey])
            S.dma("sp", lambda e: e.dma_start(out=D["b_" + name][l, r0:r1, c0:c1].rearrange("(k p) n -> p k n", p=128), in_=out),
                  [wkey], [("wbpart", name, l, r0, c0)])
            return
        S.dma("sp", lambda e: e.dma_start(out=out, in_=D["b_" + name][l, r0:r1, c0:c1].rearrange("(k p) n -> p k n", p=128)),
              [("wb", name, l)], [wkey])

    CONV_PIECES = [("w_in", 0, DM, 512, 768), ("w_in", 0, DM, 1024, 1536), ("w_in", 0, DM, 2560, 3584), ("w_in", 0, DM, 3584, 4608),
                   ("w_in", 0, DM, 4608, 5632), ("w_fourier", 0, 256, 0, DM), ("w_conv", 0, 256, 0, DM), ("w_attn", 0, 512, 0, DM),
                   ("w_o", 0, DM, 0, DM)] + [("mlp_w1", 0, DM, c * 1024, (c + 1) * 1024) for c in range(4)] + \
                  [("mlp_w2", r * 1024, (r + 1) * 1024, 0, DM) for r in range(4)]

    def convert_pieces(l, pieces, after_key):
        for (nm_, r0, r1, c0, c1) in pieces:
            S.dma("pool", lambda e, nm_=nm_, r0=r0, r1=r1, c0=c0, c1=c1: e.dma_start(
                out=D["b_" + nm_][l, r0:r1, c0:c1], in_=D[nm_][l, r0:r1, c0:c1]), [after_key], [("wb", nm_, l)])

    ident = PERS.alloc([128], BF16)
    ones = PERS.alloc([128], BF16)
    modc = PERS.alloc([2, 48, 2], F32)
    acol = PERS.alloc([2, 2, 4, 8], F32)
    convw = PERS.alloc([2, 6], F32)
    zflag = PERS.alloc([2], F32)
    rowmask = PERS.alloc([NBLK, 8, 8], BF16)
    tab = PERS.alloc([8, TABW], BF16)
    tabI = PERS.alloc([8, 576], BF16)
    kc_sb = PERS.alloc([4, 256], BF16)
    vc_sb = PERS.alloc([2, 512], BF16)
    ctx_sb = PERS.alloc([2, DM], F32)
    g1bc = PERS.alloc([DM], F32)
    g2bc = PERS.alloc([DM], F32)

    ld(ident, D["ident"], [], ["ident"])
    ld(ones, D["ones"], [], ["ones"])
    ld(convw, D["conv_wT"].rearrange("l p k -> p l k"), [], ["convw"])
    ld(zflag, D["zflag"], [], ["zflag"])
    ld(rowmask, D["rowmask"], [], ["rowmask"])
    ld(ctx_sb, D["ctx"].rearrange("(t p) d -> p t d", p=128), [], ["ctx_sb"])

    def phase0(layers):
        TMP.reset()
        cT = TMP.alloc([16], F32)
        sT = TMP.alloc([16], F32)
        sl = TMP.alloc([16, 128], BF16)
        sb = TMP.alloc([16], BF16)
        abT = TMP.alloc([2, 48], F32)
        ngT = TMP.alloc([2, 2, 8], F32)
        abb = TMP.alloc([2, DM], F32)
        W = [TMP.alloc([8, 1024], BF16) for _ in range(6)]
        gt = [TMP.alloc([512], F32) for _ in range(2)]
        ld(cT, D["cvecT"], [], ["cT"])
        ld(abT, D["ada_bT"].rearrange("l p c -> p l c"), [], ["abT"])
        ld(ngT[:, :, 0, :], D["n1gT"].rearrange("l p c -> p l c"), [], ["ngT0"])
        ld(ngT[:, :, 1, :], D["n2gT"].rearrange("l p c -> p l c"), [], ["ngT1"])
        act(sT, cT, AF.Silu, ["cT"], ["sT"])
        S.op("dve", lambda e: e.tensor_copy(sb, sT), ["sT"], ["sb"])
        S.op("dve", lambda e: e.tensor_copy(sl, sT.unsqueeze(2).to_broadcast([128, 16, 128])), ["sT"], ["sl"])
        for l in layers:
            for s in range(6):
                ldw(W[s], wview("ada_w", l, 0, DM, s * 1024, (s + 1) * 1024), [], [("p0w", s)])
            for gi, c0 in enumerate((2048, 5120)):
                ld(abb[:, gi], D["ada_b"][l:l + 1, c0:c0 + 1024].partition_broadcast(128), [("gbcw", l, gi)], [("abb", gi)])
            for s in range(6):
                w = W[s]
                wk = ("p0w", s)
                b = nbank()
                for fc in range(8):
                    for k in range(8):
                        mm(ps[b][:, 2 * fc:2 * fc + 2], w[:, k, fc * 128:(fc + 1) * 128], sb[:, 2 * k:2 * k + 2],
                           k == 0, k == 7, [wk, "sb"], [PK(b)])
                tt(modc[:, l, s * 8:(s + 1) * 8, :], ps[b][:, 0:16].rearrange("p (c t) -> p c t", t=2),
                   abT[:, l, s * 8:(s + 1) * 8].unsqueeze(2).to_broadcast([128, 8, 2]), ALU.add,
                   [PK(b), "abT"], [("modc", l)])
                if s in (2, 5):
                    gi = 0 if s == 2 else 1
                    for t in range(2):
                        for hh in range(2):
                            b = nbank()
                            for k in range(8):
                                mm(ps[b][:, :], sl[:, 2 * k + t, :], w[:, k, hh * 512:(hh + 1) * 512],
                                   k == 0, k == 7, [wk, "sl"], [PK(b)])
                            g = gt[(t * 2 + hh) % 2]
                            gk = ("gt", (t * 2 + hh) % 2)
                            tt(g, ps[b][:, :], abb[:, gi, hh * 512:(hh + 1) * 512], ALU.add, [PK(b), ("abb", gi)], [gk])
                            ld(D["gbc"][l, t, gi, :, hh * 512:(hh + 1) * 512], g, [gk], [("gbcw", l, gi), ("gbc", l, t)])
            for t in range(2):
                for (vi, sc_c, ng_i) in ((0, 8, 0), (2, 32, 1)):
                    S.op("dve", lambda e, l=l, t=t, vi=vi, sc_c=sc_c, ng_i=ng_i: e.scalar_tensor_tensor(
                        acol[:, l, t, vi, :], modc[:, l, sc_c:sc_c + 8, t], 1.0, ngT[:, l, ng_i, :], ALU.add, ALU.mult),
                        [("modc", l), "ngT0", "ngT1"], [("acol", l)])
                for (vi, sh_c) in ((1, 0), (3, 24)):
                    S.op("dve", lambda e, l=l, t=t, vi=vi, sh_c=sh_c: e.tensor_copy(
                        acol[:, l, t, vi, :], modc[:, l, sh_c:sh_c + 8, t]), [("modc", l)], [("acol", l)])
        S.barrier()

    phase0([0])

    def norm_transpose(xt, ntile, A, B, hT, keys_x, key_h, scr):
        junk, ss, xn = scr
        for t in range(ntile):
            act(xn[:, t, :], xt[:, t, :], AF.Square, keys_x, [("xn", t), ("ss", t)], accum_out=ss[:, t:t + 1])
            S.op("dve", lambda e, t=t: e.tensor_scalar(ss[:, t:t + 1], ss[:, t:t + 1], 1.0 / DM, EPS, ALU.mult, ALU.add),
                 [("ss", t)], [("ss", t)])
            act(ss[:, t:t + 1], ss[:, t:t + 1], AF.Sqrt, [("ss", t)], [("ss", t)])
            S.op("dve", lambda e, t=t: e.reciprocal(ss[:, t:t + 1], ss[:, t:t + 1]), [("ss", t)], [("ss", t)])
            act(xn[:, t, :], xt[:, t, :], AF.Copy, keys_x + [("ss", t)], [("xn", t)], scale=ss[:, t:t + 1])
        for k in range(8):
            b = nbank()
            pb = ps[b].bitcast(BF16)
            for t in range(ntile):
                S.op("pe", lambda e, t=t, k=k, pb=pb: e.transpose(pb[:, t * 128:(t + 1) * 128], xn[:, t, k * 128:(k + 1) * 128], ident),
                     [("xn", t), "ident"], [PK(b)])
            act(hT[:, k, 0:ntile * 128], pb[:, 0:ntile * 128], AF.Identity, [PK(b)], [key_h],
                scale=A[:, k:k + 1], bias=B[:, k:k + 1])

    def proj_fm(hT, T, W, wkey, c0, nchunk, sink, hkey):
        for c in range(nchunk):
            b = nbank()
            for k in range(8):
                mm(ps[b][:, 0:T], W[:, k, c0 + c * 128:c0 + (c + 1) * 128], hT[:, k, 0:T], k == 0, k == 7,
                   [wkey, hkey], [PK(b)])
            sink(c, b)

    def alloc_scr():
        return (TMP.alloc([DM], BF16), TMP.alloc([8], F32), TMP.alloc([4, DM], BF16))

    class NS:
        pass

    U1K = ["kwin", "vwin", "zwin", "ywin", ("rden", 0), ("rden", 1)] + [("cacc", c) for c in range(2)] + \
          [("E", n) for n in range(4)] + [("P", n) for n in range(4)] + [("sg", n) for n in range(2)] + [("mtmp", n) for n in range(2)]

    def alloc_mixer():
        M = NS()
        M.WS = [TMP.alloc([4096], BF16) for _ in range(4)]
        M.ws_ctr = 0
        u0 = TMP.cur
        M.kwin = TMP.alloc([4, 1024], BF16)
        M.vwin = TMP.alloc([8, 512], BF16)
        M.Et = [TMP.alloc([512], BF16) for _ in range(4)]
        M.Pt = [TMP.alloc([512], BF16) for _ in range(4)]
        M.sg = [TMP.alloc([512], F32) for _ in range(2)]
        M.mtmp = [TMP.alloc([512], F32) for _ in range(2)]
        M.cacc = TMP.alloc([2, 512], F32)
        M.zwin = TMP.alloc([2, 514], BF16)
        M.ywin = TMP.alloc([2, 512], BF16)
        M.rden = TMP.alloc([512], F32)
        u1 = TMP.cur
        TMP.cur = u0
        M.h1 = TMP.alloc([32, 512], BF16)
        TMP.cur = max(TMP.cur, u1)
        M.atT = TMP.alloc([4, 512], BF16)
        M.cvT = TMP.alloc([2, 512], BF16)
        M.q_sb = TMP.alloc([4, 2, 512], BF16)
        S.op("dve", lambda e, q=M.q_sb: e.memset(q, 0.0), [], ["q_sb"])
        M.mT = TMP.alloc([8, 512], BF16)
        M.rr = [TMP.alloc([512], BF16) for _ in range(2)]
        M.otmp = [TMP.alloc([512], F32) for _ in range(2)]
        M.ep_ctr = 0
        M.sg_ctr = 0
        M.h1_free = set()
        return M

    def layer(l):
        last = (l == 1)
        xin = D["x"] if l == 0 else D["x1"]
        xout = D["x1"] if l == 0 else D["out"]
        xkin = "xin0" if l == 0 else "x1"
        xkout = "x1" if l == 0 else "outk"
        A1 = acol[:, l, 0, 0, :]; B1 = acol[:, l, 0, 1, :]; A2 = acol[:, l, 0, 2, :]; B2 = acol[:, l, 0, 3, :]
        cA1 = acol[:, l, 1, 0, :]; cB1 = acol[:, l, 1, 1, :]; cA2 = acol[:, l, 1, 2, :]; cB2 = acol[:, l, 1, 3, :]

        def load_wk():
            wk1 = TMP.alloc([8, 768], BF16)
            wk2 = TMP.alloc([8, 1024], BF16)
            ldw(wk1[:, :, 0:512], wview("w_in", l, 0, DM, 0, 512), [], ["wk1"])
            ldw(wk1[:, :, 512:768], wview("w_in", l, 0, DM, 768, 1024), [], ["wk1"])
            ldw(wk2, wview("w_in", l, 0, DM, 1536, 2560), [], ["wk2"])
            return wk1, wk2

        TMP.reset()
        maskF = TMP.alloc([TABW], BF16)
        ld(maskF, D["maskF"], [], ["maskF"])
        maskI = TMP.alloc([576], BF16)
        ld(maskI, D["maskI"], [], ["maskI"])
        rb = [TMP.alloc([TABW], F32) for _ in range(2)]
        eb = [TMP.alloc([TABW], BF16) for _ in range(2)]
        for h in range(8):
            ld(rb[h % 2], D["rbt"][l, h], [], [("rb", h % 2)])
            act(eb[h % 2], rb[h % 2], AF.Exp, [("rb", h % 2)], [("eb", h % 2)])
            tt(tab[:, h, :], eb[h % 2], maskF, ALU.mult, [("eb", h % 2), "maskF"], ["tab"])
            tt(tabI[:, h, :], eb[h % 2][:, 7 * 64:16 * 64], maskI, ALU.mult, [("eb", h % 2), "maskI"], ["tab"])
        S.barrier()

        def slab(M):
            n = M.ws_ctr % 4
            M.ws_ctr += 1
            return M.WS[n], ("ws", n)

        def mixer_block(M, scr, T, hT, hkey, xres, xkey, yT, ykey, zw, zkey, keychunks, Am, Bm, pre_wo=None):
            nt = T // 128
            cacc, q_sb, atT, cvT, mT, h1 = M.cacc, M.q_sb, M.atT, M.cvT, M.mT, M.h1
            for c in range(2):
                S.op("dve", lambda e, c=c: e.tensor_scalar(cacc[:, c, 0:T], zw[:, c, 0:T], convw[:, l, 3 * c:3 * c + 1], None, ALU.mult),
                     [zkey, "convw"], [("cacc", c)])
                for kk in (1, 2):
                    S.op("dve", lambda e, c=c, kk=kk: e.scalar_tensor_tensor(cacc[:, c, 0:T], zw[:, c, kk:kk + T],
                                                                            convw[:, l, 3 * c + kk:3 * c + kk + 1], cacc[:, c, 0:T], ALU.mult, ALU.add),
                         [zkey, "convw", ("cacc", c)], [("cacc", c)])
            w, wk = slab(M)
            wv = w.rearrange("p (k n) -> p k n", k=8)
            ldwb(wv[:, :, 0:256], "w_in", l, 0, DM, 512, 768, wk)

            def sink_cb(c, b):
                tt(cvT[:, c, 0:T], ps[b][:, 0:T], cacc[:, c, 0:T], ALU.mult, [PK(b), ("cacc", c)], ["cvT"])
            proj_fm(hT, T, wv, wk, 0, 2, sink_cb, hkey)
            w, wk = slab(M)
            wv = w.rearrange("p (k n) -> p k n", k=8)
            ldwb(wv, "w_in", l, 0, DM, 1024, 1536, wk)

            def sink_q(c, b):
                act(q_sb[0:64, c, 0, 0:T], ps[b][0:64, 0:T], AF.Copy, [PK(b)], ["q_sb"])
                act(q_sb[64:128, c, 1, 0:T], ps[b][64:128, 0:T], AF.Copy, [PK(b)], ["q_sb"])
            proj_fm(hT, T, wv, wk, 0, 4, sink_q, hkey)
            nkc = len(keychunks)
            sctr = 0
            for hp in range(4):
                OB, DB = (3, 5), (4, 6)
                order = [nkc - 2] + list(range(nkc - 2)) + [nkc - 1]
                items = [(hh, ci) for hh in range(2) for ci in order]
                sbanks = {}

                def crange(ci):
                    lo, hi = keychunks[ci][6], keychunks[ci][7]
                    return (0, T) if lo is None else (lo * 64, (hi + 1) * 64)

                def emit_S(it):
                    nonlocal sctr
                    hh, ci = it
                    kT_ap, kkey = keychunks[ci][0], keychunks[ci][4]
                    c0, c1 = crange(ci)
                    b = sctr % 3
                    sctr += 1
                    sbanks[it] = b
                    mm(ps[b][:, c0:c1], kT_ap(hp), q_sb[:, hp, hh, c0:c1], True, True, [kkey, "q_sb"], [PK(b)])

                def emit_PV(it):
                    hh, ci = it
                    _, v_ap, tspec, rmask, kkey, vkey, lo, hi = keychunks[ci]
                    c0, c1 = crange(ci)
                    b = sbanks[it]
                    n = M.ep_ctr % 4
                    M.ep_ctr += 1
                    E, P = M.Et[n], M.Pt[n]
                    act(E[:, c0:c1], ps[b][:, c0:c1], AF.Exp, [PK(b)], [("E", n)], scale=0.125)
                    src, skey = E, ("E", n)
                    h = 2 * hp + hh
                    if tspec is not None:
                        kind, e0 = tspec
                        if kind == "int":
                            tt(P[:, c0:c1], E[:, c0:c1], tabI[:, h, e0 * 64:e0 * 64 + (c1 - c0)], ALU.mult, [("E", n), "tab"], [("P", n)])
                        else:
                            tt(P[:, 0:T], E[:, 0:T], tab[:, h, e0 * 64:e0 * 64 + T], ALU.mult, [("E", n), "tab"], [("P", n)])
                            Pv = P[:, 0:T].rearrange("p (a b) -> p a b", b=64)
                            tt(Pv, Pv, rmask.unsqueeze(2).to_broadcast([128, 8, 64]), ALU.mult, [("P", n), "rowmask"], [("P", n)], eng="pool")
                        src, skey = P, ("P", n)
                    first, lastc = (ci == order[0]), (ci == order[-1])
                    mm(ps[OB[hh]][:, c0:c1], v_ap(hp), src[:, c0:c1], first, lastc, [skey, vkey], [PK(OB[hh])])
                    mm(ps[DB[hh]][:, c0:c1], ones, src[:, c0:c1], first, lastc, [skey, "ones"], [PK(DB[hh])])

                LA = 2
                for n_, it in enumerate(items):
                    emit_S(it)
                    if n_ >= LA:
                        emit_PV(items[n_ - LA])
                for it in items[len(items) - LA:]:
                    emit_PV(it)
                for hh in range(2):
                    pr = slice(64 * hh, 64 * hh + 64)
                    S.op("dve", lambda e, hh=hh, pr=pr: e.reciprocal(M.rden[pr, 0:T], ps[DB[hh]][pr, 0:T]), [PK(DB[hh])], [("rden", hh)])
                    tt(atT[pr, hp, 0:T], ps[OB[hh]][pr, 0:T], M.rden[pr, 0:T], ALU.mult, [PK(OB[hh]), ("rden", hh)], ["atT"])
            branches = ((2560, "w_fourier", 256, yT, ykey, 2), (3584, "w_conv", 256, cvT, "cvT", 2), (4608, "w_attn", 512, atT, "atT", 4))
            for bi, (gc0, wname, wrows, src, skey, nk) in enumerate(branches):
                wb, wbk = slab(M)
                wbv = wb.rearrange("p (k n) -> p k n", n=1024)
                ldwb(wbv[:, 0:nk, :], wname, l, 0, wrows, 0, DM, wbk)
                for half in range(2):
                    wg, wgk = slab(M)
                    wgv = wg.rearrange("p (k n) -> p k n", k=8)
                    ldwb(wgv, "w_in", l, 0, DM, gc0 + half * 512, gc0 + (half + 1) * 512, wgk)
                    for o4 in range(4):
                        oc = half * 4 + o4
                        bg = nbank()
                        for k in range(8):
                            mm(ps[bg][:, 0:T], wgv[:, k, o4 * 128:(o4 + 1) * 128], hT[:, k, 0:T], k == 0, k == 7, [wgk, hkey], [PK(bg)])
                        n = M.sg_ctr % 2
                        M.sg_ctr += 1
                        act(M.sg[n][:, 0:T], ps[bg][:, 0:T], AF.Sigmoid, [PK(bg)], [("sg", n)])
                        bp = nbank()
                        for k in range(nk):
                            mm(ps[bp][:, 0:T], wbv[:, k, oc * 128:(oc + 1) * 128], src[:, k, 0:T], k == 0, k == nk - 1,
                               [wbk, skey], [PK(bp)])
                        if bi == 0:
                            tt(mT[:, oc, 0:T], ps[bp][:, 0:T], M.sg[n][:, 0:T], ALU.mult, [PK(bp), ("sg", n)], [("mT", oc)])
                        else:
                            tt(M.mtmp[n][:, 0:T], ps[bp][:, 0:T], M.sg[n][:, 0:T], ALU.mult, [PK(bp), ("sg", n)], [("mtmp", n)])
                            tt(mT[:, oc, 0:T], mT[:, oc, 0:T], M.mtmp[n][:, 0:T], ALU.add, [("mT", oc), ("mtmp", n)], [("mT", oc)])
            mTk = [("mT", oc) for oc in range(8)]
            if getattr(M, "dbg", False) and T == 512:
                ld(D["dbg_at"], atT, ["atT"], [])
                ld(D["dbg_cv"], cvT, ["cvT"], [])
                ld(D["dbg_m"], mT, mTk, [])
            if pre_wo is not None:
                pre_wo()
            for hh in range(2):
                wo, wok = slab(M)
                wov = wo.rearrange("p (k n) -> p k n", k=8)
                ldwb(wov, "w_o", l, 0, DM, hh * 512, (hh + 1) * 512, wok)
                for t in range(nt):
                    b = nbank()
                    for k in range(8):
                        mm(ps[b][:, :], mT[:, k, t * 128:(t + 1) * 128], wov[:, k, :], k == 0, k == 7, mTk + [wok], [PK(b)])
                    n = M.sg_ctr % 2
                    M.sg_ctr += 1
                    tt(M.otmp[n], ps[b][:, :], g1bc[:, hh * 512:(hh + 1) * 512], ALU.mult, [PK(b), "g1bc"], [("otmp", n)])
                    tt(xres[:, t, hh * 512:(hh + 1) * 512], xres[:, t, hh * 512:(hh + 1) * 512], M.otmp[n], ALU.add,
                       [xkey, ("otmp", n)], [xkey])
            if getattr(M, "dbg", False) and T == 512:
                ld(D["dbg_xmix"], xres, [xkey], [])
            norm_transpose(xres, nt, Am, Bm, hT, [xkey], hkey, scr)
            first = True
            for s in range(8):
                w1, w1k = slab(M)
                w1v = w1.rearrange("p (k n) -> p k n", k=8)
                ldwb(w1v, "mlp_w1", l, 0, DM, s * 512, (s + 1) * 512, w1k)
                for c in range(4):
                    b = nbank()
                    for k in range(8):
                        mm(ps[b][:, 0:T], w1v[:, k, c * 128:(c + 1) * 128], hT[:, k, 0:T], k == 0, k == 7, [w1k, hkey], [PK(b)])
                    n = M.sg_ctr % 2
                    M.sg_ctr += 1
                    act(M.rr[n][:, 0:T], ps[b][:, 0:T], AF.Relu, [PK(b)], [("rr", n)])
                    tt(h1[:, s * 4 + c, 0:T], M.rr[n][:, 0:T], M.rr[n][:, 0:T], ALU.mult, [("rr", n)],
                       ["h1"] + (U1K if first else []))
                    first = False
            if getattr(M, "dbg", False) and T == 512:
                ld(D["dbg_h2"], hT, [hkey], [])
                ld(D["dbg_h1"], h1, ["h1"], [])
                M.dbg = False
            obanks = [(t, hh, nbank()) for t in range(nt) for hh in range(2)]
            lasttok = None
            for s in range(8):
                w2, w2k = slab(M)
                w2v = w2.rearrange("p (k n) -> p k n", k=4)
                ldwb(w2v, "mlp_w2", l, s * 512, (s + 1) * 512, 0, DM, w2k)
                for (t, hh, b) in obanks:
                    for k in range(4):
                        mm(ps[b][:, :], h1[:, s * 4 + k, t * 128:(t + 1) * 128], w2v[:, k, hh * 512:(hh + 1) * 512],
                           s == 0 and k == 0, s == 7 and k == 3, ["h1", w2k], [PK(b)])
            lasttok = ("c", "pe", len(S.ops["pe"]) - 1)
            for k_ in U1K:
                S.extra.setdefault(k_, set()).add(lasttok)
            for (t, hh, b) in obanks:
                n = M.sg_ctr % 2
                M.sg_ctr += 1
                tt(M.otmp[n], ps[b][:, :], g2bc[:, hh * 512:(hh + 1) * 512], ALU.mult, [PK(b), "g2bc"], [("otmp", n)])
                tt(xres[:, t, hh * 512:(hh + 1) * 512], xres[:, t, hh * 512:(hh + 1) * 512], M.otmp[n], ALU.add,
                   [xkey, ("otmp", n)], [xkey])

        def ctx_keychunks():
            kch = []
            for ci in range(2):
                kch.append((lambda hp, ci=ci: kc_sb[:, hp, ci * 128:(ci + 1) * 128],
                            lambda hp, ci=ci: vc_sb[:, ci, hp * 128:(hp + 1) * 128], None, None, "kc_sb", "vc_sb", None, None))
            return kch

        TMP.reset()
        scr = alloc_scr()
        wk1, wk2 = load_wk()
        xt = TMP.alloc([4, DM], F32)
        hxT = TMP.alloc([8, 512], BF16)
        cust = [TMP.alloc([512], F32) for _ in range(2)]
        kst = [TMP.alloc([512], BF16) for _ in range(4)]
        p1ctr = [0]
        XU = D["xiu"][0:TOK, :].rearrange("(r x) n -> r (x n)", x=64).rearrange("r (h c) -> r h c", c=64)

        def stg():
            n = p1ctr[0] % 4
            p1ctr[0] += 1
            return kst[n], ("kst", n)

        for i in range(NBLK):
            ld(xt, xin[i * 512:(i + 1) * 512, :].rearrange("(t p) d -> p t d", p=128), [xkin], ["xt"])
            norm_transpose(xt, 4, A1, B1, hxT, ["xt"], "hxT", scr)
            ld(D["hx_s"][i], hxT, ["hxT"], [("hx_s", i)])
            def sink_u(c, b, i=i):
                st_, sk_ = stg()
                S.op("dve", lambda e, st_=st_, b=b: e.tensor_copy(st_, ps[b][:, :]), [PK(b)], [sk_])
                ld(XU[8 * i:8 * i + 8, c * 128:(c + 1) * 128, :].rearrange("r h c -> h r c"),
                   st_.rearrange("p (r c) -> p r c", c=64), [sk_], ["xiu"])
            proj_fm(hxT, 512, wk1, "wk1", 0, 2, sink_u, "hxT")

            def sink_z(c, b, i=i):
                if c < 2:
                    act(cust[c], ps[b][:, :], AF.Copy, [PK(b)], [("cust", c)])
                else:
                    st_, sk_ = stg()
                    tt(st_, ps[b][:, :], cust[c - 2], ALU.mult, [PK(b), ("cust", c - 2)], [sk_])
                    ld(D["z_s"][:, c - 2, 16 + i * 512:16 + (i + 1) * 512], st_, [sk_], ["z_s"])
            proj_fm(hxT, 512, wk1, "wk1", 256, 4, sink_z, "hxT")

            def sink_k(c, b, i=i):
                st_, sk_ = stg()
                S.op("dve", lambda e, st_=st_, b=b: e.tensor_copy(st_, ps[b][:, :]), [PK(b)], [sk_])
                ld(D["k_s"][:, c, (4 + 8 * i) * 64:(4 + 8 * i) * 64 + 512], st_, [sk_], ["k_s"])
            proj_fm(hxT, 512, wk2, "wk2", 0, 4, sink_k, "hxT")
            for t in range(4):
                b = nbank()
                for k in range(8):
                    mm(ps[b][:, :], hxT[:, k, t * 128:(t + 1) * 128], wk2[:, k, 512:1024], k == 0, k == 7, ["wk2", "hxT"], [PK(b)])
                st_, sk_ = stg()
                S.op("dve", lambda e, st_=st_, b=b: e.tensor_copy(st_, ps[b][:, :]), [PK(b)], [sk_])
                ld(D["v_s"][2 + 4 * i + t], st_, [sk_], ["v_s"])

        XI, XO = D["exch_in"], D["exch_out"]
        ld(XI[0:512, :].rearrange("(p c) n -> p c n", c=4), D["k_s"][:, :, 256:512], ["k_s"], ["exch_in"])
        ld(XI[512:1024, :].rearrange("(p c) n -> p c n", c=4), D["k_s"][:, :, 64 * 64:68 * 64], ["k_s"], ["exch_in"])
        ld(XI[1024:1536, :].rearrange("(c p h) n -> c p (h n)", c=2, p=128), D["v_s"][2:4], ["v_s"], ["exch_in"])
        ld(XI[1536:2048, :].rearrange("(c p h) n -> c p (h n)", c=2, p=128), D["v_s"][32:34], ["v_s"], ["exch_in"])
        ld(XI[2048:2064, :].rearrange("r (pp c t) -> (r pp) c t", pp=8, c=2, t=16), D["z_s"][:, :, 16:32], ["z_s"], ["exch_in"])
        ld(XI[2064:2080, :].rearrange("r (pp c t) -> (r pp) c t", pp=8, c=2, t=16), D["z_s"][:, :, TOK:TOK + 16], ["z_s"], ["exch_in"])
        RG = [[2 * g, 2 * g + 1] for g in range(ncores // 2)]
        S.op("pool", lambda e: e.collective_compute("AllGather", ALU.bypass, replica_groups=RG, ins=[D["xiu"]], outs=[D["xou"]]),
             ["xiu"], ["xou"], force_sig=True)
        S.op("pool", lambda e: e.collective_compute("AllGather", ALU.bypass, replica_groups=RG, ins=[XI], outs=[XO]),
             ["exch_in"], ["exch_out"], force_sig=True)
        S.barrier(skip_cc=True)
        if l == 0:
            phase0([1])
        TMP.reset()
        scr = alloc_scr()
        hcT = TMP.alloc([8, 256], BF16)
        if not last:
            ucT = TMP.alloc([2, 256], BF16)
            zc = TMP.alloc([2, 258], BF16)
            cuc = TMP.alloc([2, 256], F32)
        markA = TMP.cur
        wk1, wk2 = load_wk()
        norm_transpose(ctx_sb, 2, cA1, cB1, hcT, ["ctx_sb"], "hcT", scr)

        def sink_kc(c, b):
            act(kc_sb[:, c, :], ps[b][:, 0:256], AF.Copy, [PK(b)], ["kc_sb"])
        proj_fm(hcT, 256, wk2, "wk2", 0, 4, sink_kc, "hcT")
        for t in range(2):
            b = nbank()
            for k in range(8):
                mm(ps[b][:, :], hcT[:, k, t * 128:(t + 1) * 128], wk2[:, k, 512:1024], k == 0, k == 7, ["wk2", "hcT"], [PK(b)])
            act(vc_sb[:, t, :], ps[b][:, :], AF.Copy, [PK(b)], ["vc_sb"])
        if not last:
            S.op("dve", lambda e: e.memset(zc, 0.0), [], ["zc"])

            def sink_c1(c, b):
                if c < 2:
                    act(ucT[:, c, :], ps[b][:, 0:256], AF.Copy, [PK(b)], ["ucT"])
                elif c < 4:
                    act(cuc[:, c - 2, :], ps[b][:, 0:256], AF.Copy, [PK(b)], [("cuc", c - 2)])
                else:
                    tt(zc[:, c - 4, 1:257], ps[b][:, 0:256], cuc[:, c - 4, :], ALU.mult, [PK(b), ("cuc", c - 4)], ["zc"])
            proj_fm(hcT, 256, wk1, "wk1", 0, 6, sink_c1, "hcT")
        S.barrier()
        if not last:
            TMP.cur = markA
            M = alloc_mixer()
            ld(g1bc, D["gbc"][l, 1, 0], [("gbc", l, 1)], ["g1bc"])
            ld(g2bc, D["gbc"][l, 1, 1], [("gbc", l, 1)], ["g2bc"])
            chc = TMP.alloc([2, 128], BF16)
            d256 = TMP.alloc([2, 2, 256], BF16)
            zri = TMP.alloc([2, 2, 256], BF16)
            ycT = TMP.alloc([2, 256], BF16)
            ld(chc, D["chc"], [], ["chc"])
            ld(d256, D["dft256"], [], ["d256"])
            for ri in range(2):
                for tc in range(2):
                    b = nbank()
                    for cc in range(2):
                        mm(ps[b][:, cc * 128:(cc + 1) * 128], ucT[:, cc, tc * 128:(tc + 1) * 128], chc[:, ri, :], True, True,
                           ["ucT", "chc"], [PK(b)])
                    act(zri[:, ri, tc, :], ps[b][:, 0:256], AF.Copy, [PK(b)], ["zri"])
            for cc in range(2):
                b = nbank()
                n_ = 0
                for ri in range(2):
                    for tc in range(2):
                        mm(ps[b][:, 0:256], zri[:, ri, tc, cc * 128:(cc + 1) * 128], d256[:, ri, tc, :], n_ == 0, n_ == 3,
                           ["zri", "d256"], [PK(b)])
                        n_ += 1
                act(ycT[:, cc, :], ps[b][:, 0:256], AF.Copy, [PK(b)], ["ycT"])
            WMODE["cast_store"] = True
            mixer_block(M, scr, 256, hcT, "hcT", ctx_sb, "ctx_sb", ycT, "ycT", zc, "zc", ctx_keychunks(), cA2, cB2)
            WMODE["cast_store"] = False
            S.barrier()

        ld(D["k_s"][:, :, 0:256], XO[512:1024, :].rearrange("(p c) n -> p c n", c=4), ["exch_out"], ["k_s"])
        ld(D["k_s"][:, :, 68 * 64:72 * 64], XO[RX:RX + 512, :].rearrange("(p c) n -> p c n", c=4), ["exch_out"], ["k_s"])
        ld(D["v_s"][0:2], XO[1536:2048, :].rearrange("(c p h) n -> c p (h n)", c=2, p=128), ["exch_out"], ["v_s"])
        ld(D["v_s"][34:36], XO[RX + 1024:RX + 1536, :].rearrange("(c p h) n -> c p (h n)", c=2, p=128), ["exch_out"], ["v_s"])
        ld(D["z_s"][:, :, 0:16], XO[2064:2080, :].rearrange("r (pp c t) -> (r pp) c t", pp=8, c=2, t=16), ["exch_out"], ["z_s"])
        ld(D["z_s"][:, :, TOK + 16:TOK + 32], XO[RX + 2048:RX + 2064, :].rearrange("r (pp c t) -> (r pp) c t", pp=8, c=2, t=16), ["exch_out"], ["z_s"])
        S.barrier()
        if stage <= 2:
            return

        TMP.reset()
        cs128 = TMP.alloc([256], BF16)
        ch3 = TMP.alloc([2, 128], BF16)
        gtab = TMP.alloc([128, 96], BF16)
        U = TMP.alloc([128, 64], BF16)
        Asb = TMP.alloc([64, 256], BF16)
        Xsb = TMP.alloc([2, TOK], BF16)
        yst = [TMP.alloc([512], BF16) for _ in range(2)]
        ld(cs128, D["cs128"], [], ["cs128"])
        ld(ch3, D["ch3"], [], ["ch3"])
        ld(gtab, D["gtab"], [], ["gtab"])
        Uf = U.rearrange("p h c -> p (h c)")
        Xv = Xsb.rearrange("p r (k2 k1) -> p r k2 k1", k1=128)
        for cc in range(2):
            XOU0 = D["xou"][0:TOK, :].rearrange("(r x) n -> r (x n)", x=64).rearrange("r (h c) -> r h c", c=64)
            XOU1 = D["xou"][TOK:2 * TOK, :].rearrange("(r x) n -> r (x n)", x=64).rearrange("r (h c) -> r h c", c=64)
            ld(U[0:64], XOU0[:, cc * 128:(cc + 1) * 128, :], ["xou"], ["U"])
            ld(U[64:128], XOU1[:, cc * 128:(cc + 1) * 128, :], ["xou"], ["U"])
            for qq in range(32):
                b = nbank()
                for j in range(2):
                    q = 2 * qq + j
                    mm(ps[b][:, j * 256:(j + 1) * 256], Uf[:, 128 * q:128 * q + 128], cs128, True, True, ["U", "cs128"], [PK(b)])
                src = ps[b][:, :].rearrange("p (a b) -> p a b", a=2)
                if qq % 2 == 0:
                    act(Asb[:, 2 * qq:2 * qq + 2, :], src, AF.Copy, [PK(b)], ["Asb"])
                else:
                    S.op("dve", lambda e, qq=qq, src=src: e.tensor_copy(Asb[:, 2 * qq:2 * qq + 2, :], src), [PK(b)], ["Asb"])
            for g8 in range(16):
                b = nbank()
                for kl in range(8):
                    k1 = g8 * 8 + kl
                    for j in range(2):
                        pr = slice(64 * j, 64 * j + 64)
                        mm(ps[b][pr, kl * 64:(kl + 1) * 64], Asb[pr, :, k1], gtab[pr, k1, 32:96], True, False,
                           ["Asb", "gtab"], [PK(b)], tp=(64 * j, 64 * j))
                        mm(ps[b][pr, kl * 64:(kl + 1) * 64], Asb[pr, :, 128 + k1], gtab[pr, k1, 0:64], False, True,
                           ["Asb", "gtab"], [PK(b)], tp=(64 * j, 64 * j))
                src = ps[b][:, :].rearrange("p (kl r k2) -> p r k2 kl", kl=8, r=2, k2=32)
                dst = Xv[:, :, :, g8 * 8:(g8 + 1) * 8]
                if g8 % 2 == 0:
                    act(dst, src, AF.Copy, [PK(b)], ["Xsb"])
                else:
                    S.op("dve", lambda e, dst=dst, src=src: e.tensor_copy(dst, src), [PK(b)], ["Xsb"])
            for i in range(NBLK):
                b = nbank()
                mm(ps[b][:, :], ch3[:, 0, :], Xsb[:, 0, i * 512:(i + 1) * 512], True, False, ["Xsb", "ch3"], [PK(b)])
                mm(ps[b][:, :], ch3[:, 1, :], Xsb[:, 1, i * 512:(i + 1) * 512], False, True, ["Xsb", "ch3"], [PK(b)])
                y = yst[i % 2]
                act(y, ps[b][:, :], AF.Copy, [PK(b)], [("yst", i % 2)])
                ld(D["y_s"][:, cc, i * 512:(i + 1) * 512], y, [("yst", i % 2)], ["y_s"])
        S.barrier()
        if stage <= 3:
            return

        TMP.reset()
        scr = alloc_scr()
        junk, ss, xn = scr
        M = alloc_mixer()
        M.dbg = ("mix" in debug) and l == dbg_layer
        hx2 = [TMP.alloc([8, 512], BF16) for _ in range(2)]
        xt2 = TMP.alloc([4, DM], F32)
        fgb = TMP.alloc([DM], F32)
        ld(g1bc, D["gbc"][l, 0, 0], [("gbc", l, 0)], ["g1bc"])
        ld(g2bc, D["gbc"][l, 0, 1], [("gbc", l, 0)], ["g2bc"])
        if last:
            ld(fgb, D["final_g"].partition_broadcast(128), [], ["fgb"])
        nblk2 = NBLK if stage > 4 else 1
        ld(hx2[0], D["hx_s"][0], [("hx_s", 0)], [("hxT", 0)])
        for i in range(nblk2):
            hxT, hkey_ = hx2[i % 2], ("hxT", i % 2)
            if i + 1 < nblk2:
                ld(hx2[(i + 1) % 2], D["hx_s"][i + 1], [("hx_s", i + 1)], [("hxT", (i + 1) % 2)])
            ld(M.zwin, D["z_s"][:, :, 15 + i * 512:15 + i * 512 + 514], ["z_s"], ["zwin"])
            ld(M.ywin, D["y_s"][:, :, i * 512:(i + 1) * 512], ["y_s"], ["ywin"])
            ld(M.kwin, D["k_s"][:, :, 8 * i * 64:(8 * i + 16) * 64], ["k_s"], ["kwin"])
            ld(M.vwin, D["v_s"][4 * i:4 * i + 8].rearrange("c p n -> p c n"), ["v_s"], ["vwin"])

            def pre_wo(i=i):
                ld(xt2, xin[i * 512:(i + 1) * 512, :].rearrange("(t p) d -> p t d", p=128), [xkin], ["xt2"])
            if i == 0:
                S.op("dve", lambda e: e.tensor_scalar(M.zwin[:, :, 0:1], M.zwin[:, :, 0:1], zflag[:, 0:1], None, ALU.mult),
                     ["zwin", "zflag"], ["zwin"])
            if i == NBLK - 1:
                S.op("dve", lambda e: e.tensor_scalar(M.zwin[:, :, 513:514], M.zwin[:, :, 513:514], zflag[:, 1:2], None, ALU.mult),
                     ["zwin", "zflag"], ["zwin"])
            kch = []
            RNG = [(0, 1), (0, 3), (0, 5), (0, 7), (1, 7), (3, 7), (5, 7), (7, 7)]
            for j in range(8):
                if 0 < i < NBLK - 1:
                    lo, hi = RNG[j]
                    kch.append((lambda hp, j=j: M.kwin[:, hp, j * 128:(j + 1) * 128],
                                lambda hp, j=j: M.vwin[:, j, hp * 128:(hp + 1) * 128], ("int", 7 - 2 * j + lo), None, "kwin", "vwin", lo, hi))
                else:
                    kch.append((lambda hp, j=j: M.kwin[:, hp, j * 128:(j + 1) * 128],
                                lambda hp, j=j: M.vwin[:, j, hp * 128:(hp + 1) * 128], ("full", 14 - 2 * j), rowmask[:, i, j, :], "kwin", "vwin", None, None))
            kch += ctx_keychunks()
            mixer_block(M, scr, 512, hxT, hkey_, xt2, "xt2", M.ywin, "ywin", M.zwin, "zwin", kch, A2, B2, pre_wo=pre_wo)
            if last:
                for t in range(4):
                    act(xn[:, t, :], xt2[:, t, :], AF.Square, ["xt2"], [("xn", t), ("ss", t)], accum_out=ss[:, t:t + 1])
                    S.op("dve", lambda e, t=t: e.tensor_scalar(ss[:, t:t + 1], ss[:, t:t + 1], 1.0 / DM, EPS, ALU.mult, ALU.add),
                         [("ss", t)], [("ss", t)])
                    act(ss[:, t:t + 1], ss[:, t:t + 1], AF.Sqrt, [("ss", t)], [("ss", t)])
                    S.op("dve", lambda e, t=t: e.reciprocal(ss[:, t:t + 1], ss[:, t:t + 1]), [("ss", t)], [("ss", t)])
                    S.op("dve", lambda e, t=t: e.scalar_tensor_tensor(xt2[:, t, :], xt2[:, t, :], ss[:, t:t + 1], fgb, ALU.mult, ALU.mult),
                         ["xt2", ("ss", t), "fgb"], ["xt2"])
            ld(xout[i * 512:(i + 1) * 512, :].rearrange("(t p) d -> p t d", p=128), xt2, ["xt2"], [xkout, ("blkdone", l, i)])
            if not last and stage > 5:
                pcs = [pc for n_, pc in enumerate(CONV_PIECES) if n_ % NBLK == i]
                convert_pieces(1, pcs, ("blkdone", l, i))
        S.barrier()

    for l in range(2):
        layer(l)
        if stage <= 5:
            break

    S.barrier()
    DUMPS = {"kc_sb": ([128, 4, 256], BF16, kc_sb), "vc_sb": ([128, 2, 512], BF16, vc_sb), "ctx_sb": ([128, 2, DM], F32, ctx_sb),
             "acol": ([128, 2, 2, 4, 8], F32, acol), "modc": ([128, 2, 48, 2], F32, modc), "tab": ([128, 8, TABW], BF16, tab)}
    for nm in debug:
        if nm == "mix":
            continue
        if nm in DUMPS:
            shp, dt, src = DUMPS[nm]
        else:
            src = D[nm]
            shp, dt = list(src.shape), src.dtype
        dout("dbg_" + nm, shp, dt)
        ld(D["dbg_" + nm], src, [], [])
    S.wait_all("sp")

    nsig = S.plan()
    import contextlib
    with contextlib.ExitStack() as st:
        esems = {e: [st.enter_context(nc.semaphore(f"s_{e}_{n}")) for n in range(nsig[e] // EPOCH + 1)] for e in Sched.ENGS}
        dsems = {q: [st.enter_context(nc.semaphore(f"d_{q}_{n}")) for n in range(NSLOT)] for q in ("sp", "pool")}
        block = st.enter_context(nc.Block())

        @block.tensor
        def _(h):
            S.emit("pe", h, esems, dsems)

        @block.scalar
        def _(h):
            S.emit("act", h, esems, dsems)

        @block.vector
        def _(h):
            S.emit("dve", h, esems, dsems)

        @block.gpsimd
        def _(h):
            S.emit("pool", h, esems, dsems)

        @block.sync
        def _(h):
            S.emit("sp", h, esems, dsems)
    return nc


def make_in_maps(inp):
    f32 = lambda a: np.ascontiguousarray(np.asarray(a, dtype=np.float32))
    HC = host_constants()
    CC = [core_constants(h) for h in range(2)]
    shared = {
        "ada_w": f32(inp["ada_w"]), "ada_b": f32(inp["ada_b"]),
        "ada_bT": f32(np.asarray(inp["ada_b"]).reshape(2, 48, 128).transpose(0, 2, 1)),
        "n1gT": f32(np.asarray(inp["norm1_g"]).reshape(2, 8, 128).transpose(0, 2, 1)),
        "n2gT": f32(np.asarray(inp["norm2_g"]).reshape(2, 8, 128).transpose(0, 2, 1)),
        "final_g": f32(np.asarray(inp["final_g"]).reshape(1, DM)),
        "w_in": f32(inp["w_in"]),
        "conv_wT": f32(np.asarray(inp["conv_w"]).reshape(2, 3, 2, 128).transpose(0, 3, 2, 1).reshape(2, 128, 6)),
        "rbt": gather_relbias(np.asarray(inp["rel_bias"], dtype=np.float32)),
        "w_fourier": f32(inp["w_fourier"]), "w_conv": f32(inp["w_conv"]), "w_attn": f32(inp["w_attn"]),
        "w_o": f32(inp["w_o"]), "mlp_w1": f32(inp["mlp_w1"]), "mlp_w2": f32(inp["mlp_w2"]),
    }
    shared.update(HC)
    maps = []
    x = np.asarray(inp["x"]); c = np.asarray(inp["c"]); ctx = np.asarray(inp["ctx"]); c_ctx = np.asarray(inp["c_ctx"])
    for core in range(NCORES):
        b, half = core // 2, core % 2
        m = dict(shared)
        m.update(CC[half])
        m["x"] = f32(x[b, half * TOK:(half + 1) * TOK])
        m["ctx"] = f32(ctx[b])
        cv = np.stack([c[b], c_ctx], axis=1).reshape(8, 128, 2).transpose(1, 0, 2).reshape(128, 16)
        m["cvecT"] = f32(cv)
        maps.append(m)
    return maps


_NC_CACHE = {}


def kernel(**inputs):
    if "nc" not in _NC_CACHE:
        _NC_CACHE["nc"] = build_nc()
    nc = _NC_CACHE["nc"]
    maps = make_in_maps(inputs)
    res = run_bass_kernel_spmd(nc, maps, core_ids=list(range(NCORES)))
    out = np.empty((4, 8192, DM), np.float32)
    for core in range(NCORES):
        b, half = core // 2, core % 2
        out[b, half * TOK:(half + 1) * TOK] = res.results[core]["out"]
    return out
```

```python
import numpy as np
import ml_dtypes
import concourse.bass as bass
import concourse.mybir as mybir
from concourse.bass_utils import run_bass_kernel_spmd

F32 = mybir.dt.float32
BF16 = mybir.dt.bfloat16
AF = mybir.ActivationFunctionType
ALU = mybir.AluOpType

NSLOT = 16
EPOCH = 8000
NCORES = 8
DM = 1024
TOK = 4096
NBLK = 8
INW = 5632
RX = 2080
NE = 22
TABW = NE * 64
EPS = 1e-6


class Sched:
    ENGS = ["pe", "act", "dve", "pool", "sp"]

    def __init__(self):
        self.ops = {e: [] for e in self.ENGS}
        self.res = {}
        self.ndma = {e: 0 for e in self.ENGS}
        self.pending = {e: set() for e in self.ENGS}
        self.extra = {}

    def _collect(self, eng, reads, writes):
        deps = set(self.pending[eng])
        self.pending[eng] = set()
        for k in reads:
            r = self.res.get(k)
            if r is not None and r[0] is not None:
                t = r[0]
                if t[0] == "c" and t[1] == eng and eng in ("pe", "sp"):
                    continue
                deps.add(t)
        for k in writes:
            if k in self.extra:
                deps |= self.extra.pop(k)
            r = self.res.get(k)
            if r is not None:
                for t in ([r[0]] if r[0] is not None else []) + list(r[1]):
                    if t[0] == "c" and t[1] == eng and eng == "pe":
                        continue
                    deps.add(t)
        return deps

    def _update(self, reads, writes, tok):
        for k in reads:
            r = self.res.setdefault(k, [None, []])
            r[1].append(tok)
        for k in writes:
            self.res[k] = [tok, []]

    def op(self, eng, fn, reads=(), writes=(), force_sig=False):
        deps = self._collect(eng, reads, writes)
        tok = ("c", eng, len(self.ops[eng]))
        self.ops[eng].append(dict(fn=fn, deps=deps, tok=tok, dma=None, force_sig=force_sig, cc=force_sig))
        self._update(reads, writes, tok)
        return tok

    def dma(self, eng, fn, reads=(), writes=()):
        deps = self._collect(eng, reads, writes)
        j = self.ndma[eng]
        self.ndma[eng] += 1
        if j >= NSLOT:
            deps.add(("d", eng, j - NSLOT))
        tok = ("d", eng, j)
        self.ops[eng].append(dict(fn=fn, deps=deps, tok=tok, dma=j))
        self._update(reads, writes, tok)
        return tok

    def barrier(self, skip_cc=False):
        toks = set()
        for e in self.ENGS:
            last = None
            for o in reversed(self.ops[e]):
                if skip_cc and o.get("cc"):
                    continue
                if o["dma"] is None and o["fn"] is not None:
                    last = o["tok"]
                    break
            if last is not None:
                toks.add(last)
        for q in self.ENGS:
            for j in range(max(0, self.ndma[q] - NSLOT), self.ndma[q]):
                toks.add(("d", q, j))
        for e in self.ENGS:
            self.pending[e] |= toks

    def wait_all(self, eng):
        self.barrier()
        deps = set(self.pending[eng])
        self.pending[eng] = set()
        self.ops[eng].append(dict(fn=None, deps=deps, tok=("c", eng, len(self.ops[eng])), dma=None))

    def plan(self):
        sig = {e: set() for e in self.ENGS}
        for e in self.ENGS:
            fc = {}
            fd = {}
            for idx, o in enumerate(self.ops[e]):
                waits = []
                for t in sorted(o["deps"], key=str):
                    if t[0] == "c":
                        _, e2, i2 = t
                        if e2 == e and i2 >= idx:
                            continue
                        if fc.get(e2, -1) >= i2:
                            continue
                        fc[e2] = i2
                        waits.append(t)
                        sig[e2].add(i2)
                    else:
                        q, j = t[1], t[2]
                        slot, val = (q, j % NSLOT), j // NSLOT
                        if fd.get(slot, -1) >= val:
                            continue
                        fd[slot] = val
                        waits.append(t)
                o["waits"] = waits
        self.ordinal = {}
        self.nsig = {}
        for e in self.ENGS:
            for idx, o in enumerate(self.ops[e]):
                if o.get("force_sig"):
                    sig[e].add(idx)
        for e in self.ENGS:
            for n, i in enumerate(sorted(sig[e])):
                self.ordinal[(e, i)] = n
            self.nsig[e] = len(sig[e])
        return self.nsig

    def emit(self, eng, h, esems, dsems):
        for idx, o in enumerate(self.ops[eng]):
            for t in o["waits"]:
                if t[0] == "c":
                    n = self.ordinal[(t[1], t[2])]
                    h.wait_ge(esems[t[1]][n // EPOCH], n % EPOCH + 1)
                else:
                    q, j = t[1], t[2]
                    h.wait_ge(dsems[q][j % NSLOT], 16 * (j // NSLOT + 1))
            if o["fn"] is None:
                continue
            ins = o["fn"](h)
            if o["dma"] is not None:
                ins.then_inc(dsems[eng][o["dma"] % NSLOT], 16)
            elif (eng, idx) in self.ordinal:
                n = self.ordinal[(eng, idx)]
                ins.then_inc(esems[eng][n // EPOCH], 1)


def _bf(a):
    return np.ascontiguousarray(a.astype(ml_dtypes.bfloat16))


def host_constants():
    C = {}
    C["ident"] = _bf(np.eye(128, dtype=np.float32))
    r = np.arange(128)[:, None].astype(np.float64)
    k1 = np.arange(128)[None, :].astype(np.float64)
    ang = 2 * np.pi * r * k1 / 128.0
    C["cs128"] = _bf(np.concatenate([np.cos(ang), -np.sin(ang)], axis=1))
    norm = 1.0 / np.sqrt(8192.0 * 64.0)
    p = np.arange(128)
    chp = 2 * (p % 64) + (p // 64)
    co = np.arange(128)
    same = (chp[:, None] // 64) == (co[None, :] // 64)
    a3 = 2 * np.pi * (chp[:, None] % 64) * (co[None, :] % 64) / 64.0
    C["ch3"] = _bf(np.stack([np.cos(a3) * same * norm, np.sin(a3) * same * norm], 0).transpose(1, 0, 2))
    same2 = (p[:, None] // 64) == (co[None, :] // 64)
    a4 = 2 * np.pi * (p[:, None] % 64) * (co[None, :] % 64) / 64.0
    C["chc"] = _bf(np.stack([np.cos(a4) * same2, np.sin(a4) * same2], 0).transpose(1, 0, 2))
    n = np.arange(256)[:, None].astype(np.float64)
    k = np.arange(256)[None, :].astype(np.float64)
    a5 = 2 * np.pi * n * k / 256.0
    nc_ = 1.0 / np.sqrt(256.0 * 64.0)
    t = np.stack([np.cos(a5) * nc_, -np.sin(a5) * nc_], 0)
    C["dft256"] = _bf(t.reshape(2, 2, 128, 256).transpose(2, 0, 1, 3))
    pp = np.arange(128)
    krl = pp // 64
    kc = pp % 64
    e = np.arange(NE)
    qc = np.arange(64)
    dr = 17 - e[None, :, None] + krl[:, None, None]
    cs = np.clip(qc - 8, 0, 48)
    colin = (kc[:, None, None] >= cs[None, None, :]) & (kc[:, None, None] < cs[None, None, :] + 16)
    m = (dr >= 0) & (dr <= 14) & colin
    C["maskF"] = _bf(m.reshape(128, TABW).astype(np.float32))
    mi = m & (dr >= 3) & (dr <= 10)
    C["maskI"] = _bf(mi[:, 7:16, :].reshape(128, 9 * 64).astype(np.float32))
    C["ones"] = _bf(np.ones((128, 128), np.float32))
    return C


def core_constants(half):
    C = {}
    p = np.arange(128)
    c = (p % 64)[:, None, None].astype(np.float64)
    k1 = np.arange(128)[None, :, None].astype(np.float64)
    k2 = (32 * half + np.arange(32))[None, None, :].astype(np.float64)
    ang = 2 * np.pi * c * (k1 + 128.0 * k2) / 8192.0
    gr, gi = np.cos(ang), -np.sin(ang)
    C["gtab"] = _bf(np.concatenate([-gi, gr, gi], axis=2))
    krl = p // 64
    rm = np.zeros((128, NBLK, 8, 8), np.float32)
    for i in range(NBLK):
        for j in range(8):
            for q in range(8):
                rq = 64 * half + 8 * i + q
                kr = 64 * half + 8 * i - 4 + 2 * j + krl
                rs = min(max(rq - 4, 0), 120)
                rm[:, i, j, q] = ((kr >= 0) & (kr < 128) & (kr >= rs) & (kr < rs + 8)).astype(np.float32)
    C["rowmask"] = _bf(rm)
    zf = np.zeros((128, 2), np.float32)
    zf[:, 0] = 0.0 if half == 0 else 1.0
    zf[:, 1] = 0.0 if half == 1 else 1.0
    C["zflag"] = zf
    return C


def gather_relbias(rel_bias):
    p = np.arange(128)
    krl = (p // 64)[:, None, None]
    kc = (p % 64)[:, None, None]
    e = np.arange(NE)[None, :, None]
    qc = np.arange(64)[None, None, :]
    dr = np.clip(17 - e + krl, 0, 14) + 0 * qc
    dc = np.clip(kc - qc, -15, 15) + 15 + 0 * e
    t = rel_bias[:, :, dr, dc]
    return np.ascontiguousarray(t.reshape(rel_bias.shape[0], rel_bias.shape[1], 128, TABW).astype(np.float32))


def build_nc(stage=99, debug=(), ncores=NCORES):
    nc = bass.Bass("TRN2", target_bir_lowering=False)
    S = Sched()
    D = {}

    def din(name, shape, dt=F32):
        D[name] = nc.dram_tensor(name, list(shape), dt, kind="ExternalInput").ap()

    def dscr(name, shape, dt):
        D[name] = nc.dram_tensor(name, list(shape), dt).ap()

    def dout(name, shape, dt=F32):
        D[name] = nc.dram_tensor(name, list(shape), dt, kind="ExternalOutput").ap()

    din("x", [TOK, DM]); din("ctx", [256, DM]); din("cvecT", [128, 16])
    din("ada_w", [2, DM, 6 * DM]); din("ada_bT", [2, 128, 48]); din("ada_b", [2, 6 * DM])
    din("n1gT", [2, 128, 8]); din("n2gT", [2, 128, 8]); din("final_g", [1, DM])
    din("w_in", [2, DM, INW]); din("conv_wT", [2, 128, 6]); din("rbt", [2, 8, 128, TABW])
    din("w_fourier", [2, 256, DM]); din("w_conv", [2, 256, DM]); din("w_attn", [2, 512, DM])
    din("w_o", [2, DM, DM]); din("mlp_w1", [2, DM, 4 * DM]); din("mlp_w2", [2, 4 * DM, DM])
    din("ident", [128, 128], BF16); din("cs128", [128, 256], BF16); din("ch3", [128, 2, 128], BF16)
    din("chc", [128, 2, 128], BF16); din("dft256", [128, 2, 2, 256], BF16); din("maskF", [128, TABW], BF16); din("maskI", [128, 576], BF16)
    din("ones", [128, 128], BF16); din("gtab", [128, 128, 96], BF16); din("rowmask", [128, NBLK, 8, 8], BF16)
    din("zflag", [128, 2])
    dout("out", [TOK, DM])
    dscr("x1", [TOK, DM], F32)
    dscr("hx_s", [NBLK, 128, 8, 512], BF16)
    dscr("xiu", [TOK, 256], BF16)
    dscr("xou", [2 * TOK, 256], BF16)
    dscr("exch_in", [RX, 256], BF16)
    dscr("exch_out", [2 * RX, 256], BF16)
    dscr("k_s", [128, 4, 72 * 64], BF16)
    dscr("v_s", [36, 128, 512], BF16)
    dscr("z_s", [128, 2, TOK + 32], BF16)
    dscr("y_s", [128, 2, TOK], BF16)
    dscr("gbc", [2, 2, 2, 128, DM], F32)
    WSHP = {"w_in": [DM, INW], "w_fourier": [256, DM], "w_conv": [256, DM], "w_attn": [512, DM], "w_o": [DM, DM],
            "mlp_w1": [DM, 4 * DM], "mlp_w2": [4 * DM, DM]}
    for nm_, shp_ in WSHP.items():
        dscr("b_" + nm_, [2] + shp_, BF16)

    dbg_layer = 1 if "L1" in debug else 0
    debug = [d_ for d_ in debug if d_ != "L1"]
    if "mix" in debug:
        dout("dbg_at", [128, 4, 512], BF16); dout("dbg_cv", [128, 2, 512], BF16); dout("dbg_m", [128, 8, 512], BF16)
        dout("dbg_xmix", [128, 4, DM], F32)
        dout("dbg_h2", [128, 8, 512], BF16); dout("dbg_h1", [128, 32, 512], BF16)
    ARENA = 212000
    arena = nc.alloc_sbuf_tensor("arena", [128, ARENA // 2], BF16)
    ps = [nc.alloc_psum_tensor(f"ps{i}", [128, 512], F32) for i in range(8)]

    class Region:
        def __init__(self, base, size):
            self.base, self.size, self.cur = base, size, base

        def reset(self):
            self.cur = self.base

        def alloc(self, shape, dt):
            n = int(np.prod(shape))
            nb = n * (4 if dt == F32 else 2)
            off = (self.cur + 63) // 64 * 64
            assert off + nb <= self.base + self.size, ("arena overflow", shape, off + nb - self.base, self.size)
            self.cur = off + nb
            ap = arena[:, off // 2:(off + nb) // 2]
            if dt == F32:
                ap = ap.bitcast(F32)
            if len(shape) == 2:
                ap = ap.rearrange("p (a b) -> p a b", b=int(shape[1]))
            elif len(shape) == 3:
                ap = ap.rearrange("p (a b c) -> p a b c", b=int(shape[1]), c=int(shape[2]))
            elif len(shape) == 4:
                ap = ap.rearrange("p (a b c d) -> p a b c d", b=int(shape[1]), c=int(shape[2]), d=int(shape[3]))
            return ap

    PERS = Region(0, 58000)
    TMP = Region(58000, ARENA - 58000)

    bank_ctr = [0]

    def nbank():
        b = bank_ctr[0] % 8
        bank_ctr[0] += 1
        return b

    def PK(b):
        return ("ps", b)

    def mm(out, lhsT, rhs, start, stop, reads, writes, tp=None):
        if tp is None:
            S.op("pe", lambda e: e.matmul(out, lhsT, rhs, start=start, stop=stop), reads, writes)
        else:
            S.op("pe", lambda e: e.matmul(out, lhsT, rhs, start=start, stop=stop, tile_position=tp), reads, writes)

    def act(out, in_, func, reads, writes, **kw):
        S.op("act", lambda e: e.activation(out, in_, func, **kw), reads, writes)

    def tt(out, in0, in1, op, reads, writes, eng="dve"):
        S.op(eng, lambda e: e.tensor_tensor(out, in0, in1, op), reads, writes)

    def ld(out, in_, reads, writes, eng="sp", **kw):
        S.dma(eng, lambda e: e.dma_start(out=out, in_=in_, **kw), reads, writes)

    def ldw(out, in_, reads, writes):
        S.dma("pool", lambda e: e.dma_start(out=out, in_=in_), reads, writes)

    def wview(name, l, r0, r1, c0, c1):
        return D[name][l, r0:r1, c0:c1].rearrange("(k p) n -> p k n", p=128)

    WMODE = {"cast_store": False}

    def ldwb(out, name, l, r0, r1, c0, c1, wkey):
        if WMODE["cast_store"]:
            ldw(out, wview(name, l, r0, r1, c0, c1), [], [wkey])
            S.dma("sp", lambda e: e.dma_start(out=D["b_" + name][l, r0:r1, c0:c1].rearrange("(k p) n -> p k n", p=128), in_=out),
                  [wkey], [("wb", name, l, r0, c0)])
            return
        S.dma("sp", lambda e: e.dma_start(out=out, in_=D["b_" + name][l, r0:r1, c0:c1].rearrange("(k p) n -> p k n", p=128)),
              [("wb", name, l, r0, c0)], [wkey])

    CONV_PIECES = [("w_in", 0, DM, 512, 768), ("w_in", 0, DM, 1024, 1536), ("w_in", 0, DM, 2560, 3584), ("w_in", 0, DM, 3584, 4608),
                   ("w_in", 0, DM, 4608, 5632), ("w_fourier", 0, 256, 0, DM), ("w_conv", 0, 256, 0, DM), ("w_attn", 0, 512, 0, DM),
                   ("w_o", 0, DM, 0, DM)] + [("mlp_w1", 0, DM, c * 1024, (c + 1) * 1024) for c in range(4)] + \
                  [("mlp_w2", r * 1024, (r + 1) * 1024, 0, DM) for r in range(4)]

    def convert_pieces(l, pieces, after_key):
        for (nm_, r0, r1, c0, c1) in pieces:
            S.dma("pool", lambda e, nm_=nm_, r0=r0, r1=r1, c0=c0, c1=c1: e.dma_start(
                out=D["b_" + nm_][l, r0:r1, c0:c1], in_=D[nm_][l, r0:r1, c0:c1]), [after_key], [("wb", nm_, l)])

    ident = PERS.alloc([128], BF16)
    ones = PERS.alloc([128], BF16)
    modc = PERS.alloc([2, 48, 2], F32)
    acol = PERS.alloc([2, 2, 4, 8], F32)
    convw = PERS.alloc([2, 6], F32)
    zflag = PERS.alloc([2], F32)
    rowmask = PERS.alloc([NBLK, 8, 8], BF16)
    tab = PERS.alloc([8, TABW], BF16)
    tabI = PERS.alloc([8, 576], BF16)
    kc_sb = PERS.alloc([4, 256], BF16)
    vc_sb = PERS.alloc([2, 512], BF16)
    ctx_sb = PERS.alloc([2, DM], F32)
    g1bc = PERS.alloc([DM], F32)
    g2bc = PERS.alloc([DM], F32)

    ld(ident, D["ident"], [], ["ident"])
    ld(ones, D["ones"], [], ["ones"])
    ld(convw, D["conv_wT"].rearrange("l p k -> p l k"), [], ["convw"])
    ld(zflag, D["zflag"], [], ["zflag"])
    ld(rowmask, D["rowmask"], [], ["rowmask"])
    ld(ctx_sb, D["ctx"].rearrange("(t p) d -> p t d", p=128), [], ["ctx_sb"])

    def phase0(layers):
        TMP.reset()
        cT = TMP.alloc([16], F32)
        sT = TMP.alloc([16], F32)
        sl = TMP.alloc([16, 128], BF16)
        sb = TMP.alloc([16], BF16)
        abT = TMP.alloc([2, 48], F32)
        ngT = TMP.alloc([2, 2, 8], F32)
        abb = TMP.alloc([2, DM], F32)
        W = [TMP.alloc([8, 1024], BF16) for _ in range(6)]
        gt = [TMP.alloc([512], F32) for _ in range(2)]
        ld(cT, D["cvecT"], [], ["cT"])
        ld(abT, D["ada_bT"].rearrange("l p c -> p l c"), [], ["abT"])
        ld(ngT[:, :, 0, :], D["n1gT"].rearrange("l p c -> p l c"), [], ["ngT0"])
        ld(ngT[:, :, 1, :], D["n2gT"].rearrange("l p c -> p l c"), [], ["ngT1"])
        act(sT, cT, AF.Silu, ["cT"], ["sT"])
        S.op("dve", lambda e: e.tensor_copy(sb, sT), ["sT"], ["sb"])
        S.op("dve", lambda e: e.tensor_copy(sl, sT.unsqueeze(2).to_broadcast([128, 16, 128])), ["sT"], ["sl"])
        for l in layers:
            for s in range(6):
                ldw(W[s], wview("ada_w", l, 0, DM, s * 1024, (s + 1) * 1024), [], [("p0w", s)])
            for gi, c0 in enumerate((2048, 5120)):
                ld(abb[:, gi], D["ada_b"][l:l + 1, c0:c0 + 1024].partition_broadcast(128), [("gbcw", l, gi)], [("abb", gi)])
            for s in range(6):
                w = W[s]
                wk = ("p0w", s)
                b = nbank()
                for fc in range(8):
                    for k in range(8):
                        mm(ps[b][:, 2 * fc:2 * fc + 2], w[:, k, fc * 128:(fc + 1) * 128], sb[:, 2 * k:2 * k + 2],
                           k == 0, k == 7, [wk, "sb"], [PK(b)])
                tt(modc[:, l, s * 8:(s + 1) * 8, :], ps[b][:, 0:16].rearrange("p (c t) -> p c t", t=2),
                   abT[:, l, s * 8:(s + 1) * 8].unsqueeze(2).to_broadcast([128, 8, 2]), ALU.add,
                   [PK(b), "abT"], [("modc", l)])
                if s in (2, 5):
                    gi = 0 if s == 2 else 1
                    for t in range(2):
                        for hh in range(2):
                            b = nbank()
                            for k in range(8):
                                mm(ps[b][:, :], sl[:, 2 * k + t, :], w[:, k, hh * 512:(hh + 1) * 512],
                                   k == 0, k == 7, [wk, "sl"], [PK(b)])
                            g = gt[(t * 2 + hh) % 2]
                            gk = ("gt", (t * 2 + hh) % 2)
                            tt(g, ps[b][:, :], abb[:, gi, hh * 512:(hh + 1) * 512], ALU.add, [PK(b), ("abb", gi)], [gk])
                            ld(D["gbc"][l, t, gi, :, hh * 512:(hh + 1) * 512], g, [gk], [("gbcw", l, gi), ("gbc", l, t)])
            for t in range(2):
                for (vi, sc_c, ng_i) in ((0, 8, 0), (2, 32, 1)):
                    S.op("dve", lambda e, l=l, t=t, vi=vi, sc_c=sc_c, ng_i=ng_i: e.scalar_tensor_tensor(
                        acol[:, l, t, vi, :], modc[:, l, sc_c:sc_c + 8, t], 1.0, ngT[:, l, ng_i, :], ALU.add, ALU.mult),
                        [("modc", l), "ngT0", "ngT1"], [("acol", l)])
                for (vi, sh_c) in ((1, 0), (3, 24)):
                    S.op("dve", lambda e, l=l, t=t, vi=vi, sh_c=sh_c: e.tensor_copy(
                        acol[:, l, t, vi, :], modc[:, l, sh_c:sh_c + 8, t]), [("modc", l)], [("acol", l)])
        S.barrier()

    phase0([0])

    def norm_transpose(xt, ntile, A, B, hT, keys_x, key_h, scr):
        junk, ss, xn = scr
        for t in range(ntile):
            act(xn[:, t, :], xt[:, t, :], AF.Square, keys_x, [("xn", t), ("ss", t)], accum_out=ss[:, t:t + 1])
            S.op("dve", lambda e, t=t: e.tensor_scalar(ss[:, t:t + 1], ss[:, t:t + 1], 1.0 / DM, EPS, ALU.mult, ALU.add),
                 [("ss", t)], [("ss", t)])
            act(ss[:, t:t + 1], ss[:, t:t + 1], AF.Sqrt, [("ss", t)], [("ss", t)])
            S.op("dve", lambda e, t=t: e.reciprocal(ss[:, t:t + 1], ss[:, t:t + 1]), [("ss", t)], [("ss", t)])
            act(xn[:, t, :], xt[:, t, :], AF.Copy, keys_x + [("ss", t)], [("xn", t)], scale=ss[:, t:t + 1])
        for k in range(8):
            b = nbank()
            pb = ps[b].bitcast(BF16)
            for t in range(ntile):
                S.op("pe", lambda e, t=t, k=k, pb=pb: e.transpose(pb[:, t * 128:(t + 1) * 128], xn[:, t, k * 128:(k + 1) * 128], ident),
                     [("xn", t), "ident"], [PK(b)])
            act(hT[:, k, 0:ntile * 128], pb[:, 0:ntile * 128], AF.Identity, [PK(b)], [key_h],
                scale=A[:, k:k + 1], bias=B[:, k:k + 1])

    def proj_fm(hT, T, W, wkey, c0, nchunk, sink, hkey):
        for c in range(nchunk):
            b = nbank()
            for k in range(8):
                mm(ps[b][:, 0:T], W[:, k, c0 + c * 128:c0 + (c + 1) * 128], hT[:, k, 0:T], k == 0, k == 7,
                   [wkey, hkey], [PK(b)])
            sink(c, b)

    def alloc_scr():
        return (TMP.alloc([DM], BF16), TMP.alloc([8], F32), TMP.alloc([4, DM], BF16))

    class NS:
        pass

    U1K = ["kwin", "vwin", "zwin", "ywin", ("rden", 0), ("rden", 1)] + [("cacc", c) for c in range(2)] + \
          [("E", n) for n in range(4)] + [("P", n) for n in range(4)] + [("sg", n) for n in range(2)] + [("mtmp", n) for n in range(2)]

    def alloc_mixer():
        M = NS()
        M.WS = [TMP.alloc([4096], BF16) for _ in range(4)]
        M.ws_ctr = 0
        u0 = TMP.cur
        M.kwin = TMP.alloc([4, 1024], BF16)
        M.vwin = TMP.alloc([8, 512], BF16)
        M.Et = [TMP.alloc([512], BF16) for _ in range(4)]
        M.Pt = [TMP.alloc([512], BF16) for _ in range(4)]
        M.sg = [TMP.alloc([512], F32) for _ in range(2)]
        M.mtmp = [TMP.alloc([512], F32) for _ in range(2)]
        M.cacc = TMP.alloc([2, 512], F32)
        M.zwin = TMP.alloc([2, 514], BF16)
        M.ywin = TMP.alloc([2, 512], BF16)
        M.rden = TMP.alloc([512], F32)
        u1 = TMP.cur
        TMP.cur = u0
        M.h1 = TMP.alloc([32, 512], BF16)
        TMP.cur = max(TMP.cur, u1)
        M.atT = TMP.alloc([4, 512], BF16)
        M.cvT = TMP.alloc([2, 512], BF16)
        M.q_sb = TMP.alloc([4, 2, 512], BF16)
        S.op("dve", lambda e, q=M.q_sb: e.memset(q, 0.0), [], ["q_sb"])
        M.mT = TMP.alloc([8, 512], BF16)
        M.rr = [TMP.alloc([512], BF16) for _ in range(2)]
        M.otmp = [TMP.alloc([512], F32) for _ in range(2)]
        M.ep_ctr = 0
        M.sg_ctr = 0
        M.h1_free = set()
        return M

    def layer(l):
        last = (l == 1)
        xin = D["x"] if l == 0 else D["x1"]
        xout = D["x1"] if l == 0 else D["out"]
        xkin = "xin0" if l == 0 else "x1"
        xkout = "x1" if l == 0 else "outk"
        A1 = acol[:, l, 0, 0, :]; B1 = acol[:, l, 0, 1, :]; A2 = acol[:, l, 0, 2, :]; B2 = acol[:, l, 0, 3, :]
        cA1 = acol[:, l, 1, 0, :]; cB1 = acol[:, l, 1, 1, :]; cA2 = acol[:, l, 1, 2, :]; cB2 = acol[:, l, 1, 3, :]

        def load_wk():
            wk1 = TMP.alloc([8, 768], BF16)
            wk2 = TMP.alloc([8, 1024], BF16)
            ldw(wk1[:, :, 0:512], wview("w_in", l, 0, DM, 0, 512), [], ["wk1"])
            ldw(wk1[:, :, 512:768], wview("w_in", l, 0, DM, 768, 1024), [], ["wk1"])
            ldw(wk2, wview("w_in", l, 0, DM, 1536, 2560), [], ["wk2"])
            return wk1, wk2

        TMP.reset()
        maskF = TMP.alloc([TABW], BF16)
        ld(maskF, D["maskF"], [], ["maskF"])
        maskI = TMP.alloc([576], BF16)
        ld(maskI, D["maskI"], [], ["maskI"])
        rb = [TMP.alloc([TABW], F32) for _ in range(2)]
        eb = [TMP.alloc([TABW], BF16) for _ in range(2)]
        for h in range(8):
            ld(rb[h % 2], D["rbt"][l, h], [], [("rb", h % 2)])
            act(eb[h % 2], rb[h % 2], AF.Exp, [("rb", h % 2)], [("eb", h % 2)])
            tt(tab[:, h, :], eb[h % 2], maskF, ALU.mult, [("eb", h % 2), "maskF"], ["tab"])
            tt(tabI[:, h, :], eb[h % 2][:, 7 * 64:16 * 64], maskI, ALU.mult, [("eb", h % 2), "maskI"], ["tab"])
        S.barrier()

        def slab(M):
            n = M.ws_ctr % 4
            M.ws_ctr += 1
            return M.WS[n], ("ws", n)

        def mixer_block(M, scr, T, hT, hkey, xres, xkey, yT, ykey, zw, zkey, keychunks, Am, Bm, pre_wo=None):
            nt = T // 128
            cacc, q_sb, atT, cvT, mT, h1 = M.cacc, M.q_sb, M.atT, M.cvT, M.mT, M.h1
            for c in range(2):
                S.op("dve", lambda e, c=c: e.tensor_scalar(cacc[:, c, 0:T], zw[:, c, 0:T], convw[:, l, 3 * c:3 * c + 1], None, ALU.mult),
                     [zkey, "convw"], [("cacc", c)])
                for kk in (1, 2):
                    S.op("dve", lambda e, c=c, kk=kk: e.scalar_tensor_tensor(cacc[:, c, 0:T], zw[:, c, kk:kk + T],
                                                                            convw[:, l, 3 * c + kk:3 * c + kk + 1], cacc[:, c, 0:T], ALU.mult, ALU.add),
                         [zkey, "convw", ("cacc", c)], [("cacc", c)])
            w, wk = slab(M)
            wv = w.rearrange("p (k n) -> p k n", k=8)
            ldwb(wv[:, :, 0:256], "w_in", l, 0, DM, 512, 768, wk)

            def sink_cb(c, b):
                tt(cvT[:, c, 0:T], ps[b][:, 0:T], cacc[:, c, 0:T], ALU.mult, [PK(b), ("cacc", c)], ["cvT"])
            proj_fm(hT, T, wv, wk, 0, 2, sink_cb, hkey)
            w, wk = slab(M)
            wv = w.rearrange("p (k n) -> p k n", k=8)
            ldwb(wv, "w_in", l, 0, DM, 1024, 1536, wk)

            def sink_q(c, b):
                act(q_sb[0:64, c, 0, 0:T], ps[b][0:64, 0:T], AF.Copy, [PK(b)], ["q_sb"])
                act(q_sb[64:128, c, 1, 0:T], ps[b][64:128, 0:T], AF.Copy, [PK(b)], ["q_sb"])
            proj_fm(hT, T, wv, wk, 0, 4, sink_q, hkey)
            nkc = len(keychunks)
            sctr = 0
            for hp in range(4):
                OB, DB = (3, 5), (4, 6)
                order = [nkc - 2] + list(range(nkc - 2)) + [nkc - 1]
                items = [(hh, ci) for hh in range(2) for ci in order]
                sbanks = {}

                def crange(ci):
                    lo, hi = keychunks[ci][6], keychunks[ci][7]
                    return (0, T) if lo is None else (lo * 64, (hi + 1) * 64)

                def emit_S(it):
                    nonlocal sctr
                    hh, ci = it
                    kT_ap, kkey = keychunks[ci][0], keychunks[ci][4]
                    c0, c1 = crange(ci)
                    b = sctr % 3
                    sctr += 1
                    sbanks[it] = b
                    mm(ps[b][:, c0:c1], kT_ap(hp), q_sb[:, hp, hh, c0:c1], True, True, [kkey, "q_sb"], [PK(b)])

                def emit_PV(it):
                    hh, ci = it
                    _, v_ap, tspec, rmask, kkey, vkey, lo, hi = keychunks[ci]
                    c0, c1 = crange(ci)
                    b = sbanks[it]
                    n = M.ep_ctr % 4
                    M.ep_ctr += 1
                    E, P = M.Et[n], M.Pt[n]
                    act(E[:, c0:c1], ps[b][:, c0:c1], AF.Exp, [PK(b)], [("E", n)], scale=0.125)
                    src, skey = E, ("E", n)
                    h = 2 * hp + hh
                    if tspec is not None:
                        kind, e0 = tspec
                        if kind == "int":
                            tt(P[:, c0:c1], E[:, c0:c1], tabI[:, h, e0 * 64:e0 * 64 + (c1 - c0)], ALU.mult, [("E", n), "tab"], [("P", n)])
                        else:
                            tt(P[:, 0:T], E[:, 0:T], tab[:, h, e0 * 64:e0 * 64 + T], ALU.mult, [("E", n), "tab"], [("P", n)])
                            Pv = P[:, 0:T].rearrange("p (a b) -> p a b", b=64)
                            tt(Pv, Pv, rmask.unsqueeze(2).to_broadcast([128, 8, 64]), ALU.mult, [("P", n), "rowmask"], [("P", n)], eng="pool")
                        src, skey = P, ("P", n)
                    first, lastc = (ci == order[0]), (ci == order[-1])
                    mm(ps[OB[hh]][:, c0:c1], v_ap(hp), src[:, c0:c1], first, lastc, [skey, vkey], [PK(OB[hh])])
                    mm(ps[DB[hh]][:, c0:c1], ones, src[:, c0:c1], first, lastc, [skey, "ones"], [PK(DB[hh])])

                LA = 2
                for n_, it in enumerate(items):
                    emit_S(it)
                    if n_ >= LA:
                        emit_PV(items[n_ - LA])
                for it in items[len(items) - LA:]:
                    emit_PV(it)
                for hh in range(2):
                    pr = slice(64 * hh, 64 * hh + 64)
                    S.op("dve", lambda e, hh=hh, pr=pr: e.reciprocal(M.rden[pr, 0:T], ps[DB[hh]][pr, 0:T]), [PK(DB[hh])], [("rden", hh)])
                    tt(atT[pr, hp, 0:T], ps[OB[hh]][pr, 0:T], M.rden[pr, 0:T], ALU.mult, [PK(OB[hh]), ("rden", hh)], ["atT"])
            branches = ((2560, "w_fourier", 256, yT, ykey, 2), (3584, "w_conv", 256, cvT, "cvT", 2), (4608, "w_attn", 512, atT, "atT", 4))
            for bi, (gc0, wname, wrows, src, skey, nk) in enumerate(branches):
                wb, wbk = slab(M)
                wbv = wb.rearrange("p (k n) -> p k n", n=1024)
                ldwb(wbv[:, 0:nk, :], wname, l, 0, wrows, 0, DM, wbk)
                for half in range(2):
                    wg, wgk = slab(M)
                    wgv = wg.rearrange("p (k n) -> p k n", k=8)
                    ldwb(wgv, "w_in", l, 0, DM, gc0 + half * 512, gc0 + (half + 1) * 512, wgk)
                    for o4 in range(4):
                        oc = half * 4 + o4
                        bg = nbank()
                        for k in range(8):
                            mm(ps[bg][:, 0:T], wgv[:, k, o4 * 128:(o4 + 1) * 128], hT[:, k, 0:T], k == 0, k == 7, [wgk, hkey], [PK(bg)])
                        n = M.sg_ctr % 2
                        M.sg_ctr += 1
                        act(M.sg[n][:, 0:T], ps[bg][:, 0:T], AF.Sigmoid, [PK(bg)], [("sg", n)])
                        bp = nbank()
                        for k in range(nk):
                            mm(ps[bp][:, 0:T], wbv[:, k, oc * 128:(oc + 1) * 128], src[:, k, 0:T], k == 0, k == nk - 1,
                               [wbk, skey], [PK(bp)])
                        if bi == 0:
                            tt(mT[:, oc, 0:T], ps[bp][:, 0:T], M.sg[n][:, 0:T], ALU.mult, [PK(bp), ("sg", n)], [("mT", oc)])
                        else:
                            tt(M.mtmp[n][:, 0:T], ps[bp][:, 0:T], M.sg[n][:, 0:T], ALU.mult, [PK(bp), ("sg", n)], [("mtmp", n)])
                            tt(mT[:, oc, 0:T], mT[:, oc, 0:T], M.mtmp[n][:, 0:T], ALU.add, [("mT", oc), ("mtmp", n)], [("mT", oc)])
            mTk = [("mT", oc) for oc in range(8)]
            if getattr(M, "dbg", False) and T == 512:
                ld(D["dbg_at"], atT, ["atT"], [])
                ld(D["dbg_cv"], cvT, ["cvT"], [])
                ld(D["dbg_m"], mT, mTk, [])
            if pre_wo is not None:
                pre_wo()
            for hh in range(2):
                wo, wok = slab(M)
                wov = wo.rearrange("p (k n) -> p k n", k=8)
                ldwb(wov, "w_o", l, 0, DM, hh * 512, (hh + 1) * 512, wok)
                for t in range(nt):
                    b = nbank()
                    for k in range(8):
                        mm(ps[b][:, :], mT[:, k, t * 128:(t + 1) * 128], wov[:, k, :], k == 0, k == 7, mTk + [wok], [PK(b)])
                    n = M.sg_ctr % 2
                    M.sg_ctr += 1
                    tt(M.otmp[n], ps[b][:, :], g1bc[:, hh * 512:(hh + 1) * 512], ALU.mult, [PK(b), "g1bc"], [("otmp", n)])
                    tt(xres[:, t, hh * 512:(hh + 1) * 512], xres[:, t, hh * 512:(hh + 1) * 512], M.otmp[n], ALU.add,
                       [xkey, ("otmp", n)], [xkey])
            if getattr(M, "dbg", False) and T == 512:
                ld(D["dbg_xmix"], xres, [xkey], [])
            norm_transpose(xres, nt, Am, Bm, hT, [xkey], hkey, scr)
            first = True
            for s in range(8):
                w1, w1k = slab(M)
                w1v = w1.rearrange("p (k n) -> p k n", k=8)
                ldwb(w1v, "mlp_w1", l, 0, DM, s * 512, (s + 1) * 512, w1k)
                for c in range(4):
                    b = nbank()
                    for k in range(8):
                        mm(ps[b][:, 0:T], w1v[:, k, c * 128:(c + 1) * 128], hT[:, k, 0:T], k == 0, k == 7, [w1k, hkey], [PK(b)])
                    n = M.sg_ctr % 2
                    M.sg_ctr += 1
                    act(M.rr[n][:, 0:T], ps[b][:, 0:T], AF.Relu, [PK(b)], [("rr", n)])
                    tt(h1[:, s * 4 + c, 0:T], M.rr[n][:, 0:T], M.rr[n][:, 0:T], ALU.mult, [("rr", n)],
                       ["h1"] + (U1K if first else []))
                    first = False
            if getattr(M, "dbg", False) and T == 512:
                ld(D["dbg_h2"], hT, [hkey], [])
                ld(D["dbg_h1"], h1, ["h1"], [])
                M.dbg = False
            obanks = [(t, hh, nbank()) for t in range(nt) for hh in range(2)]
            lasttok = None
            for s in range(8):
                w2, w2k = slab(M)
                w2v = w2.rearrange("p (k n) -> p k n", k=4)
                ldwb(w2v, "mlp_w2", l, s * 512, (s + 1) * 512, 0, DM, w2k)
                for (t, hh, b) in obanks:
                    for k in range(4):
                        mm(ps[b][:, :], h1[:, s * 4 + k, t * 128:(t + 1) * 128], w2v[:, k, hh * 512:(hh + 1) * 512],
                           s == 0 and k == 0, s == 7 and k == 3, ["h1", w2k], [PK(b)])
            lasttok = ("c", "pe", len(S.ops["pe"]) - 1)
            for k_ in U1K:
                S.extra.setdefault(k_, set()).add(lasttok)
            for (t, hh, b) in obanks:
                n = M.sg_ctr % 2
                M.sg_ctr += 1
                tt(M.otmp[n], ps[b][:, :], g2bc[:, hh * 512:(hh + 1) * 512], ALU.mult, [PK(b), "g2bc"], [("otmp", n)])
                tt(xres[:, t, hh * 512:(hh + 1) * 512], xres[:, t, hh * 512:(hh + 1) * 512], M.otmp[n], ALU.add,
                   [xkey, ("otmp", n)], [xkey])

        def ctx_keychunks():
            kch = []
            for ci in range(2):
                kch.append((lambda hp, ci=ci: kc_sb[:, hp, ci * 128:(ci + 1) * 128],
                            lambda hp, ci=ci: vc_sb[:, ci, hp * 128:(hp + 1) * 128], None, None, "kc_sb", "vc_sb", None, None))
            return kch

        TMP.reset()
        scr = alloc_scr()
        wk1, wk2 = load_wk()
        xt = TMP.alloc([4, DM], F32)
        hxT = TMP.alloc([8, 512], BF16)
        cust = [TMP.alloc([512], F32) for _ in range(2)]
        kst = [TMP.alloc([512], BF16) for _ in range(4)]
        p1ctr = [0]
        XU = D["xiu"][0:TOK, :].rearrange("(r x) n -> r (x n)", x=64).rearrange("r (h c) -> r h c", c=64)

        def stg():
            n = p1ctr[0] % 4
            p1ctr[0] += 1
            return kst[n], ("kst", n)

        for i in range(NBLK):
            ld(xt, xin[i * 512:(i + 1) * 512, :].rearrange("(t p) d -> p t d", p=128), [xkin], ["xt"])
            norm_transpose(xt, 4, A1, B1, hxT, ["xt"], "hxT", scr)
            ld(D["hx_s"][i], hxT, ["hxT"], [("hx_s", i)])
            def sink_u(c, b, i=i):
                st_, sk_ = stg()
                S.op("dve", lambda e, st_=st_, b=b: e.tensor_copy(st_, ps[b][:, :]), [PK(b)], [sk_])
                ld(XU[8 * i:8 * i + 8, c * 128:(c + 1) * 128, :].rearrange("r h c -> h r c"),
                   st_.rearrange("p (r c) -> p r c", c=64), [sk_], ["xiu"])
            proj_fm(hxT, 512, wk1, "wk1", 0, 2, sink_u, "hxT")

            def sink_z(c, b, i=i):
                if c < 2:
                    act(cust[c], ps[b][:, :], AF.Copy, [PK(b)], [("cust", c)])
                else:
                    st_, sk_ = stg()
                    tt(st_, ps[b][:, :], cust[c - 2], ALU.mult, [PK(b), ("cust", c - 2)], [sk_])
                    ld(D["z_s"][:, c - 2, 16 + i * 512:16 + (i + 1) * 512], st_, [sk_], ["z_s"])
            proj_fm(hxT, 512, wk1, "wk1", 256, 4, sink_z, "hxT")

            def sink_k(c, b, i=i):
                st_, sk_ = stg()
                S.op("dve", lambda e, st_=st_, b=b: e.tensor_copy(st_, ps[b][:, :]), [PK(b)], [sk_])
                ld(D["k_s"][:, c, (4 + 8 * i) * 64:(4 + 8 * i) * 64 + 512], st_, [sk_], ["k_s"])
            proj_fm(hxT, 512, wk2, "wk2", 0, 4, sink_k, "hxT")
            for t in range(4):
                b = nbank()
                for k in range(8):
                    mm(ps[b][:, :], hxT[:, k, t * 128:(t + 1) * 128], wk2[:, k, 512:1024], k == 0, k == 7, ["wk2", "hxT"], [PK(b)])
                st_, sk_ = stg()
                S.op("dve", lambda e, st_=st_, b=b: e.tensor_copy(st_, ps[b][:, :]), [PK(b)], [sk_])
                ld(D["v_s"][2 + 4 * i + t], st_, [sk_], ["v_s"])

        XI, XO = D["exch_in"], D["exch_out"]
        ld(XI[0:512, :].rearrange("(p c) n -> p c n", c=4), D["k_s"][:, :, 256:512], ["k_s"], ["exch_in"])
        ld(XI[512:1024, :].rearrange("(p c) n -> p c n", c=4), D["k_s"][:, :, 64 * 64:68 * 64], ["k_s"], ["exch_in"])
        ld(XI[1024:1536, :].rearrange("(c p h) n -> c p (h n)", c=2, p=128), D["v_s"][2:4], ["v_s"], ["exch_in"])
        ld(XI[1536:2048, :].rearrange("(c p h) n -> c p (h n)", c=2, p=128), D["v_s"][32:34], ["v_s"], ["exch_in"])
        ld(XI[2048:2064, :].rearrange("r (pp c t) -> (r pp) c t", pp=8, c=2, t=16), D["z_s"][:, :, 16:32], ["z_s"], ["exch_in"])
        ld(XI[2064:2080, :].rearrange("r (pp c t) -> (r pp) c t", pp=8, c=2, t=16), D["z_s"][:, :, TOK:TOK + 16], ["z_s"], ["exch_in"])
        RG = [[2 * g, 2 * g + 1] for g in range(ncores // 2)]
        S.op("pool", lambda e: e.collective_compute("AllGather", ALU.bypass, replica_groups=RG, ins=[D["xiu"]], outs=[D["xou"]]),
             ["xiu"], ["xou"], force_sig=True)
        S.op("pool", lambda e: e.collective_compute("AllGather", ALU.bypass, replica_groups=RG, ins=[XI], outs=[XO]),
             ["exch_in"], ["exch_out"], force_sig=True)
        S.barrier(skip_cc=True)
        if l == 0:
            phase0([1])
        TMP.reset()
        scr = alloc_scr()
        hcT = TMP.alloc([8, 256], BF16)
        if not last:
            ucT = TMP.alloc([2, 256], BF16)
            zc = TMP.alloc([2, 258], BF16)
            cuc = TMP.alloc([2, 256], F32)
        markA = TMP.cur
        wk1, wk2 = load_wk()
        norm_transpose(ctx_sb, 2, cA1, cB1, hcT, ["ctx_sb"], "hcT", scr)

        def sink_kc(c, b):
            act(kc_sb[:, c, :], ps[b][:, 0:256], AF.Copy, [PK(b)], ["kc_sb"])
        proj_fm(hcT, 256, wk2, "wk2", 0, 4, sink_kc, "hcT")
        for t in range(2):
            b = nbank()
            for k in range(8):
                mm(ps[b][:, :], hcT[:, k, t * 128:(t + 1) * 128], wk2[:, k, 512:1024], k == 0, k == 7, ["wk2", "hcT"], [PK(b)])
            act(vc_sb[:, t, :], ps[b][:, :], AF.Copy, [PK(b)], ["vc_sb"])
        if not last:
            S.op("dve", lambda e: e.memset(zc, 0.0), [], ["zc"])

            def sink_c1(c, b):
                if c < 2:
                    act(ucT[:, c, :], ps[b][:, 0:256], AF.Copy, [PK(b)], ["ucT"])
                elif c < 4:
                    act(cuc[:, c - 2, :], ps[b][:, 0:256], AF.Copy, [PK(b)], [("cuc", c - 2)])
                else:
                    tt(zc[:, c - 4, 1:257], ps[b][:, 0:256], cuc[:, c - 4, :], ALU.mult, [PK(b), ("cuc", c - 4)], ["zc"])
            proj_fm(hcT, 256, wk1, "wk1", 0, 6, sink_c1, "hcT")
        S.barrier()
        if not last:
            TMP.cur = markA
            M = alloc_mixer()
            ld(g1bc, D["gbc"][l, 1, 0], [("gbc", l, 1)], ["g1bc"])
            ld(g2bc, D["gbc"][l, 1, 1], [("gbc", l, 1)], ["g2bc"])
            chc = TMP.alloc([2, 128], BF16)
            d256 = TMP.alloc([2, 2, 256], BF16)
            zri = TMP.alloc([2, 2, 256], BF16)
            ycT = TMP.alloc([2, 256], BF16)
            ld(chc, D["chc"], [], ["chc"])
            ld(d256, D["dft256"], [], ["d256"])
            for ri in range(2):
                for tc in range(2):
                    b = nbank()
                    for cc in range(2):
                        mm(ps[b][:, cc * 128:(cc + 1) * 128], ucT[:, cc, tc * 128:(tc + 1) * 128], chc[:, ri, :], True, True,
                           ["ucT", "chc"], [PK(b)])
                    act(zri[:, ri, tc, :], ps[b][:, 0:256], AF.Copy, [PK(b)], ["zri"])
            for cc in range(2):
                b = nbank()
                n_ = 0
                for ri in range(2):
                    for tc in range(2):
                        mm(ps[b][:, 0:256], zri[:, ri, tc, cc * 128:(cc + 1) * 128], d256[:, ri, tc, :], n_ == 0, n_ == 3,
                           ["zri", "d256"], [PK(b)])
                        n_ += 1
                act(ycT[:, cc, :], ps[b][:, 0:256], AF.Copy, [PK(b)], ["ycT"])
            WMODE["cast_store"] = True
            mixer_block(M, scr, 256, hcT, "hcT", ctx_sb, "ctx_sb", ycT, "ycT", zc, "zc", ctx_keychunks(), cA2, cB2)
            WMODE["cast_store"] = False
            S.barrier()

        ld(D["k_s"][:, :, 0:256], XO[512:1024, :].rearrange("(p c) n -> p c n", c=4), ["exch_out"], ["k_s"])
        ld(D["k_s"][:, :, 68 * 64:72 * 64], XO[RX:RX + 512, :].rearrange("(p c) n -> p c n", c=4), ["exch_out"], ["k_s"])
        ld(D["v_s"][0:2], XO[1536:2048, :].rearrange("(c p h) n -> c p (h n)", c=2, p=128), ["exch_out"], ["v_s"])
        ld(D["v_s"][34:36], XO[RX + 1024:RX + 1536, :].rearrange("(c p h) n -> c p (h n)", c=2, p=128), ["exch_out"], ["v_s"])
        ld(D["z_s"][:, :, 0:16], XO[2064:2080, :].rearrange("r (pp c t) -> (r pp) c t", pp=8, c=2, t=16), ["exch_out"], ["z_s"])
        ld(D["z_s"][:, :, TOK + 16:TOK + 32], XO[RX + 2048:RX + 2064, :].rearrange("r (pp c t) -> (r pp) c t", pp=8, c=2, t=16), ["exch_out"], ["z_s"])
        S.barrier()
        if stage <= 2:
            return

        TMP.reset()
        cs128 = TMP.alloc([256], BF16)
        ch3 = TMP.alloc([2, 128], BF16)
        gtab = TMP.alloc([128, 96], BF16)
        U = TMP.alloc([128, 64], BF16)
        Asb = TMP.alloc([64, 256], BF16)
        Xsb = TMP.alloc([2, TOK], BF16)
        yst = [TMP.alloc([512], BF16) for _ in range(2)]
        ld(cs128, D["cs128"], [], ["cs128"])
        ld(ch3, D["ch3"], [], ["ch3"])
        ld(gtab, D["gtab"], [], ["gtab"])
        Uf = U.rearrange("p h c -> p (h c)")
        Xv = Xsb.rearrange("p r (k2 k1) -> p r k2 k1", k1=128)
        for cc in range(2):
            XOU0 = D["xou"][0:TOK, :].rearrange("(r x) n -> r (x n)", x=64).rearrange("r (h c) -> r h c", c=64)
            XOU1 = D["xou"][TOK:2 * TOK, :].rearrange("(r x) n -> r (x n)", x=64).rearrange("r (h c) -> r h c", c=64)
            ld(U[0:64], XOU0[:, cc * 128:(cc + 1) * 128, :], ["xou"], ["U"])
            ld(U[64:128], XOU1[:, cc * 128:(cc + 1) * 128, :], ["xou"], ["U"])
            for qq in range(32):
                b = nbank()
                for j in range(2):
                    q = 2 * qq + j
                    mm(ps[b][:, j * 256:(j + 1) * 256], Uf[:, 128 * q:128 * q + 128], cs128, True, True, ["U", "cs128"], [PK(b)])
                src = ps[b][:, :].rearrange("p (a b) -> p a b", a=2)
                if qq % 2 == 0:
                    act(Asb[:, 2 * qq:2 * qq + 2, :], src, AF.Copy, [PK(b)], ["Asb"])
                else:
                    S.op("dve", lambda e, qq=qq, src=src: e.tensor_copy(Asb[:, 2 * qq:2 * qq + 2, :], src), [PK(b)], ["Asb"])
            for g8 in range(16):
                b = nbank()
                for kl in range(8):
                    k1 = g8 * 8 + kl
                    for j in range(2):
                        pr = slice(64 * j, 64 * j + 64)
                        mm(ps[b][pr, kl * 64:(kl + 1) * 64], Asb[pr, :, k1], gtab[pr, k1, 32:96], True, False,
                           ["Asb", "gtab"], [PK(b)], tp=(64 * j, 64 * j))
                        mm(ps[b][pr, kl * 64:(kl + 1) * 64], Asb[pr, :, 128 + k1], gtab[pr, k1, 0:64], False, True,
                           ["Asb", "gtab"], [PK(b)], tp=(64 * j, 64 * j))
                src = ps[b][:, :].rearrange("p (kl r k2) -> p r k2 kl", kl=8, r=2, k2=32)
                dst = Xv[:, :, :, g8 * 8:(g8 + 1) * 8]
                if g8 % 2 == 0:
                    act(dst, src, AF.Copy, [PK(b)], ["Xsb"])
                else:
                    S.op("dve", lambda e, dst=dst, src=src: e.tensor_copy(dst, src), [PK(b)], ["Xsb"])
            for i in range(NBLK):
                b = nbank()
                mm(ps[b][:, :], ch3[:, 0, :], Xsb[:, 0, i * 512:(i + 1) * 512], True, False, ["Xsb", "ch3"], [PK(b)])
                mm(ps[b][:, :], ch3[:, 1, :], Xsb[:, 1, i * 512:(i + 1) * 512], False, True, ["Xsb", "ch3"], [PK(b)])
                y = yst[i % 2]
                act(y, ps[b][:, :], AF.Copy, [PK(b)], [("yst", i % 2)])
                ld(D["y_s"][:, cc, i * 512:(i + 1) * 512], y, [("yst", i % 2)], ["y_s"])
        S.barrier()
        if stage <= 3:
            return

        TMP.reset()
        scr = alloc_scr()
        junk, ss, xn = scr
        M = alloc_mixer()
        M.dbg = ("mix" in debug) and l == dbg_layer
        hx2 = [TMP.alloc([8, 512], BF16) for _ in range(2)]
        xt2 = TMP.alloc([4, DM], F32)
        fgb = TMP.alloc([DM], F32)
        ld(g1bc, D["gbc"][l, 0, 0], [("gbc", l, 0)], ["g1bc"])
        ld(g2bc, D["gbc"][l, 0, 1], [("gbc", l, 0)], ["g2bc"])
        if last:
            ld(fgb, D["final_g"].partition_broadcast(128), [], ["fgb"])
        nblk2 = NBLK if stage > 4 else 1
        ld(hx2[0], D["hx_s"][0], [("hx_s", 0)], [("hxT", 0)])
        for i in range(nblk2):
            hxT, hkey_ = hx2[i % 2], ("hxT", i % 2)
            if i + 1 < nblk2:
                ld(hx2[(i + 1) % 2], D["hx_s"][i + 1], [("hx_s", i + 1)], [("hxT", (i + 1) % 2)])
            ld(M.zwin, D["z_s"][:, :, 15 + i * 512:15 + i * 512 + 514], ["z_s"], ["zwin"])
            ld(M.ywin, D["y_s"][:, :, i * 512:(i + 1) * 512], ["y_s"], ["ywin"])
            ld(M.kwin, D["k_s"][:, :, 8 * i * 64:(8 * i + 16) * 64], ["k_s"], ["kwin"])
            ld(M.vwin, D["v_s"][4 * i:4 * i + 8].rearrange("c p n -> p c n"), ["v_s"], ["vwin"])

            def pre_wo(i=i):
                ld(xt2, xin[i * 512:(i + 1) * 512, :].rearrange("(t p) d -> p t d", p=128), [xkin], ["xt2"])
            if i == 0:
                S.op("dve", lambda e: e.tensor_scalar(M.zwin[:, :, 0:1], M.zwin[:, :, 0:1], zflag[:, 0:1], None, ALU.mult),
                     ["zwin", "zflag"], ["zwin"])
            if i == NBLK - 1:
                S.op("dve", lambda e: e.tensor_scalar(M.zwin[:, :, 513:514], M.zwin[:, :, 513:514], zflag[:, 1:2], None, ALU.mult),
                     ["zwin", "zflag"], ["zwin"])
            kch = []
            RNG = [(0, 1), (0, 3), (0, 5), (0, 7), (1, 7), (3, 7), (5, 7), (7, 7)]
            for j in range(8):
                if 0 < i < NBLK - 1:
                    lo, hi = RNG[j]
                    kch.append((lambda hp, j=j: M.kwin[:, hp, j * 128:(j + 1) * 128],
                                lambda hp, j=j: M.vwin[:, j, hp * 128:(hp + 1) * 128], ("int", 7 - 2 * j + lo), None, "kwin", "vwin", lo, hi))
                else:
                    kch.append((lambda hp, j=j: M.kwin[:, hp, j * 128:(j + 1) * 128],
                                lambda hp, j=j: M.vwin[:, j, hp * 128:(hp + 1) * 128], ("full", 14 - 2 * j), rowmask[:, i, j, :], "kwin", "vwin", None, None))
            kch += ctx_keychunks()
            WMODE["cast_store"] = (last and i == 0)
            mixer_block(M, scr, 512, hxT, hkey_, xt2, "xt2", M.ywin, "ywin", M.zwin, "zwin", kch, A2, B2, pre_wo=pre_wo)
            WMODE["cast_store"] = False
            if last:
                for t in range(4):
                    act(xn[:, t, :], xt2[:, t, :], AF.Square, ["xt2"], [("xn", t), ("ss", t)], accum_out=ss[:, t:t + 1])
                    S.op("dve", lambda e, t=t: e.tensor_scalar(ss[:, t:t + 1], ss[:, t:t + 1], 1.0 / DM, EPS, ALU.mult, ALU.add),
                         [("ss", t)], [("ss", t)])
                    act(ss[:, t:t + 1], ss[:, t:t + 1], AF.Sqrt, [("ss", t)], [("ss", t)])
                    S.op("dve", lambda e, t=t: e.reciprocal(ss[:, t:t + 1], ss[:, t:t + 1]), [("ss", t)], [("ss", t)])
                    S.op("dve", lambda e, t=t: e.scalar_tensor_tensor(xt2[:, t, :], xt2[:, t, :], ss[:, t:t + 1], fgb, ALU.mult, ALU.mult),
                         ["xt2", ("ss", t), "fgb"], ["xt2"])
            ld(xout[i * 512:(i + 1) * 512, :].rearrange("(t p) d -> p t d", p=128), xt2, ["xt2"], [xkout, ("blkdone", l, i)])
        S.barrier()

    for l in range(2):
        layer(l)
        if stage <= 5:
            break

    S.barrier()
    DUMPS = {"kc_sb": ([128, 4, 256], BF16, kc_sb), "vc_sb": ([128, 2, 512], BF16, vc_sb), "ctx_sb": ([128, 2, DM], F32, ctx_sb),
             "acol": ([128, 2, 2, 4, 8], F32, acol), "modc": ([128, 2, 48, 2], F32, modc), "tab": ([128, 8, TABW], BF16, tab)}
    for nm in debug:
        if nm == "mix":
            continue
        if nm in DUMPS:
            shp, dt, src = DUMPS[nm]
        else:
            src = D[nm]
            shp, dt = list(src.shape), src.dtype
        dout("dbg_" + nm, shp, dt)
        ld(D["dbg_" + nm], src, [], [])
    S.wait_all("sp")

    nsig = S.plan()
    import contextlib
    with contextlib.ExitStack() as st:
        esems = {e: [st.enter_context(nc.semaphore(f"s_{e}_{n}")) for n in range(nsig[e] // EPOCH + 1)] for e in Sched.ENGS}
        dsems = {q: [st.enter_context(nc.semaphore(f"d_{q}_{n}")) for n in range(NSLOT)] for q in ("sp", "pool")}
        block = st.enter_context(nc.Block())

        @block.tensor
        def _(h):
            S.emit("pe", h, esems, dsems)

        @block.scalar
        def _(h):
            S.emit("act", h, esems, dsems)

        @block.vector
        def _(h):
            S.emit("dve", h, esems, dsems)

        @block.gpsimd
        def _(h):
            S.emit("pool", h, esems, dsems)

        @block.sync
        def _(h):
            S.emit("sp", h, esems, dsems)
    return nc


def make_in_maps(inp):
    f32 = lambda a: np.ascontiguousarray(np.asarray(a, dtype=np.float32))
    HC = host_constants()
    CC = [core_constants(h) for h in range(2)]
    shared = {
        "ada_w": f32(inp["ada_w"]), "ada_b": f32(inp["ada_b"]),
        "ada_bT": f32(np.asarray(inp["ada_b"]).reshape(2, 48, 128).transpose(0, 2, 1)),
        "n1gT": f32(np.asarray(inp["norm1_g"]).reshape(2, 8, 128).transpose(0, 2, 1)),
        "n2gT": f32(np.asarray(inp["norm2_g"]).reshape(2, 8, 128).transpose(0, 2, 1)),
        "final_g": f32(np.asarray(inp["final_g"]).reshape(1, DM)),
        "w_in": f32(inp["w_in"]),
        "conv_wT": f32(np.asarray(inp["conv_w"]).reshape(2, 3, 2, 128).transpose(0, 3, 2, 1).reshape(2, 128, 6)),
        "rbt": gather_relbias(np.asarray(inp["rel_bias"], dtype=np.float32)),
        "w_fourier": f32(inp["w_fourier"]), "w_conv": f32(inp["w_conv"]), "w_attn": f32(inp["w_attn"]),
        "w_o": f32(inp["w_o"]), "mlp_w1": f32(inp["mlp_w1"]), "mlp_w2": f32(inp["mlp_w2"]),
    }
    shared.update(HC)
    maps = []
    x = np.asarray(inp["x"]); c = np.asarray(inp["c"]); ctx = np.asarray(inp["ctx"]); c_ctx = np.asarray(inp["c_ctx"])
    for core in range(NCORES):
        b, half = core // 2, core % 2
        m = dict(shared)
        m.update(CC[half])
        m["x"] = f32(x[b, half * TOK:(half + 1) * TOK])
        m["ctx"] = f32(ctx[b])
        cv = np.stack([c[b], c_ctx], axis=1).reshape(8, 128, 2).transpose(1, 0, 2).reshape(128, 16)
        m["cvecT"] = f32(cv)
        maps.append(m)
    return maps


_NC_CACHE = {}


def kernel(**inputs):
    if "nc" not in _NC_CACHE:
        _NC_CACHE["nc"] = build_nc()
    nc = _NC_CACHE["nc"]
    maps = make_in_maps(inputs)
    res = run_bass_kernel_spmd(nc, maps, core_ids=list(range(NCORES)))
    out = np.empty((4, 8192, DM), np.float32)
    for core in range(NCORES):
        b, half = core // 2, core % 2
        out[b, half * TOK:(half + 1) * TOK] = res.results[core]["out"]
    return out
```

```python
import numpy as np
import ml_dtypes
import concourse.bass as bass
import concourse.mybir as mybir
from concourse.bass_utils import run_bass_kernel_spmd

F32 = mybir.dt.float32
BF16 = mybir.dt.bfloat16
AF = mybir.ActivationFunctionType
ALU = mybir.AluOpType

NSLOT = 16
EPOCH = 8000
NCORES = 8
DM = 1024
TOK = 4096
NBLK = 8
INW = 5632
RX = 2080
NE = 22
TABW = NE * 64
EPS = 1e-6


class Sched:
    ENGS = ["pe", "act", "dve", "pool", "sp"]

    def __init__(self):
        self.ops = {e: [] for e in self.ENGS}
        self.res = {}
        self.ndma = {e: 0 for e in self.ENGS}
        self.pending = {e: set() for e in self.ENGS}
        self.extra = {}

    def _collect(self, eng, reads, writes):
        deps = set(self.pending[eng])
        self.pending[eng] = set()
        for k in reads:
            r = self.res.get(k)
            if r is not None and r[0] is not None:
                t = r[0]
                if t[0] == "c" and t[1] == eng and eng in ("pe", "sp"):
                    continue
                deps.add(t)
        for k in writes:
            if k in self.extra:
                deps |= self.extra.pop(k)
            r = self.res.get(k)
            if r is not None:
                for t in ([r[0]] if r[0] is not None else []) + list(r[1]):
                    if t[0] == "c" and t[1] == eng and eng == "pe":
                        continue
                    deps.add(t)
        return deps

    def _update(self, reads, writes, tok):
        for k in reads:
            r = self.res.setdefault(k, [None, []])
            r[1].append(tok)
        for k in writes:
            self.res[k] = [tok, []]

    def op(self, eng, fn, reads=(), writes=(), force_sig=False):
        deps = self._collect(eng, reads, writes)
        tok = ("c", eng, len(self.ops[eng]))
        self.ops[eng].append(dict(fn=fn, deps=deps, tok=tok, dma=None, force_sig=force_sig, cc=force_sig))
        self._update(reads, writes, tok)
        return tok

    def dma(self, eng, fn, reads=(), writes=()):
        deps = self._collect(eng, reads, writes)
        j = self.ndma[eng]
        self.ndma[eng] += 1
        if j >= NSLOT:
            deps.add(("d", eng, j - NSLOT))
        tok = ("d", eng, j)
        self.ops[eng].append(dict(fn=fn, deps=deps, tok=tok, dma=j))
        self._update(reads, writes, tok)
        return tok

    def barrier(self, skip_cc=False):
        toks = set()
        for e in self.ENGS:
            last = None
            for o in reversed(self.ops[e]):
                if skip_cc and o.get("cc"):
                    continue
                if o["dma"] is None and o["fn"] is not None:
                    last = o["tok"]
                    break
            if last is not None:
                toks.add(last)
        for q in self.ENGS:
            for j in range(max(0, self.ndma[q] - NSLOT), self.ndma[q]):
                toks.add(("d", q, j))
        for e in self.ENGS:
            self.pending[e] |= toks

    def wait_all(self, eng):
        self.barrier()
        deps = set(self.pending[eng])
        self.pending[eng] = set()
        self.ops[eng].append(dict(fn=None, deps=deps, tok=("c", eng, len(self.ops[eng])), dma=None))

    def plan(self):
        sig = {e: set() for e in self.ENGS}
        for e in self.ENGS:
            fc = {}
            fd = {}
            for idx, o in enumerate(self.ops[e]):
                waits = []
                for t in sorted(o["deps"], key=str):
                    if t[0] == "c":
                        _, e2, i2 = t
                        if e2 == e and i2 >= idx:
                            continue
                        if fc.get(e2, -1) >= i2:
                            continue
                        fc[e2] = i2
                        waits.append(t)
                        sig[e2].add(i2)
                    else:
                        q, j = t[1], t[2]
                        slot, val = (q, j % NSLOT), j // NSLOT
                        if fd.get(slot, -1) >= val:
                            continue
                        fd[slot] = val
                        waits.append(t)
                o["waits"] = waits
        self.ordinal = {}
        self.nsig = {}
        for e in self.ENGS:
            for idx, o in enumerate(self.ops[e]):
                if o.get("force_sig"):
                    sig[e].add(idx)
        for e in self.ENGS:
            for n, i in enumerate(sorted(sig[e])):
                self.ordinal[(e, i)] = n
            self.nsig[e] = len(sig[e])
        return self.nsig

    def emit(self, eng, h, esems, dsems):
        for idx, o in enumerate(self.ops[eng]):
            for t in o["waits"]:
                if t[0] == "c":
                    n = self.ordinal[(t[1], t[2])]
                    h.wait_ge(esems[t[1]][n // EPOCH], n % EPOCH + 1)
                else:
                    q, j = t[1], t[2]
                    h.wait_ge(dsems[q][j % NSLOT], 16 * (j // NSLOT + 1))
            if o["fn"] is None:
                continue
            ins = o["fn"](h)
            if o["dma"] is not None:
                ins.then_inc(dsems[eng][o["dma"] % NSLOT], 16)
            elif (eng, idx) in self.ordinal:
                n = self.ordinal[(eng, idx)]
                ins.then_inc(esems[eng][n // EPOCH], 1)


def _bf(a):
    return np.ascontiguousarray(a.astype(ml_dtypes.bfloat16))


def host_constants():
    C = {}
    C["ident"] = _bf(np.eye(128, dtype=np.float32))
    r = np.arange(128)[:, None].astype(np.float64)
    k1 = np.arange(128)[None, :].astype(np.float64)
    ang = 2 * np.pi * r * k1 / 128.0
    C["cs128"] = _bf(np.concatenate([np.cos(ang), -np.sin(ang)], axis=1))
    norm = 1.0 / np.sqrt(8192.0 * 64.0)
    p = np.arange(128)
    chp = 2 * (p % 64) + (p // 64)
    co = np.arange(128)
    same = (chp[:, None] // 64) == (co[None, :] // 64)
    a3 = 2 * np.pi * (chp[:, None] % 64) * (co[None, :] % 64) / 64.0
    C["ch3"] = _bf(np.stack([np.cos(a3) * same * norm, np.sin(a3) * same * norm], 0).transpose(1, 0, 2))
    same2 = (p[:, None] // 64) == (co[None, :] // 64)
    a4 = 2 * np.pi * (p[:, None] % 64) * (co[None, :] % 64) / 64.0
    C["chc"] = _bf(np.stack([np.cos(a4) * same2, np.sin(a4) * same2], 0).transpose(1, 0, 2))
    n = np.arange(256)[:, None].astype(np.float64)
    k = np.arange(256)[None, :].astype(np.float64)
    a5 = 2 * np.pi * n * k / 256.0
    nc_ = 1.0 / np.sqrt(256.0 * 64.0)
    t = np.stack([np.cos(a5) * nc_, -np.sin(a5) * nc_], 0)
    C["dft256"] = _bf(t.reshape(2, 2, 128, 256).transpose(2, 0, 1, 3))
    pp = np.arange(128)
    krl = pp // 64
    kc = pp % 64
    e = np.arange(NE)
    qc = np.arange(64)
    dr = 17 - e[None, :, None] + krl[:, None, None]
    cs = np.clip(qc - 8, 0, 48)
    colin = (kc[:, None, None] >= cs[None, None, :]) & (kc[:, None, None] < cs[None, None, :] + 16)
    m = (dr >= 0) & (dr <= 14) & colin
    C["maskF"] = _bf(m.reshape(128, TABW).astype(np.float32))
    mi = m & (dr >= 3) & (dr <= 10)
    C["maskI"] = _bf(mi[:, 7:16, :].reshape(128, 9 * 64).astype(np.float32))
    C["ones"] = _bf(np.ones((128, 128), np.float32))
    return C


def core_constants(half):
    C = {}
    p = np.arange(128)
    c = (p % 64)[:, None, None].astype(np.float64)
    k1 = np.arange(128)[None, :, None].astype(np.float64)
    k2 = (32 * half + np.arange(32))[None, None, :].astype(np.float64)
    ang = 2 * np.pi * c * (k1 + 128.0 * k2) / 8192.0
    gr, gi = np.cos(ang), -np.sin(ang)
    C["gtab"] = _bf(np.concatenate([-gi, gr, gi], axis=2))
    krl = p // 64
    rm = np.zeros((128, NBLK, 8, 8), np.float32)
    for i in range(NBLK):
        for j in range(8):
            for q in range(8):
                rq = 64 * half + 8 * i + q
                kr = 64 * half + 8 * i - 4 + 2 * j + krl
                rs = min(max(rq - 4, 0), 120)
                rm[:, i, j, q] = ((kr >= 0) & (kr < 128) & (kr >= rs) & (kr < rs + 8)).astype(np.float32)
    C["rowmask"] = _bf(rm)
    zf = np.zeros((128, 2), np.float32)
    zf[:, 0] = 0.0 if half == 0 else 1.0
    zf[:, 1] = 0.0 if half == 1 else 1.0
    C["zflag"] = zf
    return C


def gather_relbias(rel_bias):
    p = np.arange(128)
    krl = (p // 64)[:, None, None]
    kc = (p % 64)[:, None, None]
    e = np.arange(NE)[None, :, None]
    qc = np.arange(64)[None, None, :]
    dr = np.clip(17 - e + krl, 0, 14) + 0 * qc
    dc = np.clip(kc - qc, -15, 15) + 15 + 0 * e
    t = rel_bias[:, :, dr, dc]
    return np.ascontiguousarray(t.reshape(rel_bias.shape[0], rel_bias.shape[1], 128, TABW).astype(np.float32))


def build_nc(stage=99, debug=(), ncores=NCORES):
    nc = bass.Bass("TRN2", target_bir_lowering=False)
    S = Sched()
    D = {}

    def din(name, shape, dt=F32):
        D[name] = nc.dram_tensor(name, list(shape), dt, kind="ExternalInput").ap()

    def dscr(name, shape, dt):
        D[name] = nc.dram_tensor(name, list(shape), dt).ap()

    def dout(name, shape, dt=F32):
        D[name] = nc.dram_tensor(name, list(shape), dt, kind="ExternalOutput").ap()

    din("x", [TOK, DM]); din("ctx", [256, DM]); din("cvecT", [128, 16])
    din("ada_w", [2, DM, 6 * DM]); din("ada_bT", [2, 128, 48]); din("ada_b", [2, 6 * DM])
    din("n1gT", [2, 128, 8]); din("n2gT", [2, 128, 8]); din("final_g", [1, DM])
    din("w_in", [2, DM, INW]); din("conv_wT", [2, 128, 6]); din("rbt", [2, 8, 128, TABW])
    din("w_fourier", [2, 256, DM]); din("w_conv", [2, 256, DM]); din("w_attn", [2, 512, DM])
    din("w_o", [2, DM, DM]); din("mlp_w1", [2, DM, 4 * DM]); din("mlp_w2", [2, 4 * DM, DM])
    din("ident", [128, 128], BF16); din("cs128", [128, 256], BF16); din("ch3", [128, 2, 128], BF16)
    din("chc", [128, 2, 128], BF16); din("dft256", [128, 2, 2, 256], BF16); din("maskF", [128, TABW], BF16); din("maskI", [128, 576], BF16)
    din("ones", [128, 128], BF16); din("gtab", [128, 128, 96], BF16); din("rowmask", [128, NBLK, 8, 8], BF16)
    din("zflag", [128, 2])
    dout("out", [TOK, DM])
    dscr("x1", [TOK, DM], F32)
    dscr("hx_s", [NBLK, 128, 8, 512], BF16)
    dscr("xiu", [TOK, 256], BF16)
    dscr("xou", [2 * TOK, 256], BF16)
    dscr("exch_in", [RX, 256], BF16)
    dscr("exch_out", [2 * RX, 256], BF16)
    dscr("k_s", [128, 4, 72 * 64], BF16)
    dscr("v_s", [36, 128, 512], BF16)
    dscr("z_s", [128, 2, TOK + 32], BF16)
    dscr("y_s", [128, 2, TOK], BF16)
    dscr("gbc", [2, 2, 2, 128, DM], F32)
    WSHP = {"w_in": [DM, INW], "w_fourier": [256, DM], "w_conv": [256, DM], "w_attn": [512, DM], "w_o": [DM, DM],
            "mlp_w1": [DM, 4 * DM], "mlp_w2": [4 * DM, DM]}
    for nm_, shp_ in WSHP.items():
        dscr("b_" + nm_, [2] + shp_, BF16)

    dbg_layer = 1 if "L1" in debug else 0
    debug = [d_ for d_ in debug if d_ != "L1"]
    if "mix" in debug:
        dout("dbg_at", [128, 4, 512], BF16); dout("dbg_cv", [128, 2, 512], BF16); dout("dbg_m", [128, 8, 512], BF16)
        dout("dbg_xmix", [128, 4, DM], F32)
        dout("dbg_h2", [128, 8, 512], BF16); dout("dbg_h1", [128, 32, 512], BF16)
    ARENA = 212000
    arena = nc.alloc_sbuf_tensor("arena", [128, ARENA // 2], BF16)
    ps = [nc.alloc_psum_tensor(f"ps{i}", [128, 512], F32) for i in range(8)]

    class Region:
        def __init__(self, base, size):
            self.base, self.size, self.cur = base, size, base

        def reset(self):
            self.cur = self.base

        def alloc(self, shape, dt):
            n = int(np.prod(shape))
            nb = n * (4 if dt == F32 else 2)
            off = (self.cur + 63) // 64 * 64
            assert off + nb <= self.base + self.size, ("arena overflow", shape, off + nb - self.base, self.size)
            self.cur = off + nb
            ap = arena[:, off // 2:(off + nb) // 2]
            if dt == F32:
                ap = ap.bitcast(F32)
            if len(shape) == 2:
                ap = ap.rearrange("p (a b) -> p a b", b=int(shape[1]))
            elif len(shape) == 3:
                ap = ap.rearrange("p (a b c) -> p a b c", b=int(shape[1]), c=int(shape[2]))
            elif len(shape) == 4:
                ap = ap.rearrange("p (a b c d) -> p a b c d", b=int(shape[1]), c=int(shape[2]), d=int(shape[3]))
            return ap

    PERS = Region(0, 58000)
    TMP = Region(58000, ARENA - 58000)

    bank_ctr = [0]

    def nbank():
        b = bank_ctr[0] % 8
        bank_ctr[0] += 1
        return b

    def PK(b):
        return ("ps", b)

    def mm(out, lhsT, rhs, start, stop, reads, writes, tp=None):
        if tp is None:
            S.op("pe", lambda e: e.matmul(out, lhsT, rhs, start=start, stop=stop), reads, writes)
        else:
            S.op("pe", lambda e: e.matmul(out, lhsT, rhs, start=start, stop=stop, tile_position=tp), reads, writes)

    def act(out, in_, func, reads, writes, **kw):
        S.op("act", lambda e: e.activation(out, in_, func, **kw), reads, writes)

    def tt(out, in0, in1, op, reads, writes, eng="dve"):
        S.op(eng, lambda e: e.tensor_tensor(out, in0, in1, op), reads, writes)

    def ld(out, in_, reads, writes, eng="sp", **kw):
        S.dma(eng, lambda e: e.dma_start(out=out, in_=in_, **kw), reads, writes)

    def ldw(out, in_, reads, writes):
        S.dma("pool", lambda e: e.dma_start(out=out, in_=in_), reads, writes)

    def wview(name, l, r0, r1, c0, c1):
        return D[name][l, r0:r1, c0:c1].rearrange("(k p) n -> p k n", p=128)

    WMODE = {"cast_store": False}

    def ldwb(out, name, l, r0, r1, c0, c1, wkey):
        if WMODE["cast_store"]:
            ldw(out, wview(name, l, r0, r1, c0, c1), [], [wkey])
            S.dma("sp", lambda e: e.dma_start(out=D["b_" + name][l, r0:r1, c0:c1].rearrange("(k p) n -> p k n", p=128), in_=out),
                  [wkey], [("wb", name, l, r0, c0)])
            return
        S.dma("sp", lambda e: e.dma_start(out=out, in_=D["b_" + name][l, r0:r1, c0:c1].rearrange("(k p) n -> p k n", p=128)),
              [("wb", name, l, r0, c0)], [wkey])

    CONV_PIECES = [("w_in", 0, DM, 512, 768), ("w_in", 0, DM, 1024, 1536), ("w_in", 0, DM, 2560, 3584), ("w_in", 0, DM, 3584, 4608),
                   ("w_in", 0, DM, 4608, 5632), ("w_fourier", 0, 256, 0, DM), ("w_conv", 0, 256, 0, DM), ("w_attn", 0, 512, 0, DM),
                   ("w_o", 0, DM, 0, DM)] + [("mlp_w1", 0, DM, c * 1024, (c + 1) * 1024) for c in range(4)] + \
                  [("mlp_w2", r * 1024, (r + 1) * 1024, 0, DM) for r in range(4)]

    def convert_pieces(l, pieces, after_key):
        for (nm_, r0, r1, c0, c1) in pieces:
            S.dma("pool", lambda e, nm_=nm_, r0=r0, r1=r1, c0=c0, c1=c1: e.dma_start(
                out=D["b_" + nm_][l, r0:r1, c0:c1], in_=D[nm_][l, r0:r1, c0:c1]), [after_key], [("wb", nm_, l)])

    ident = PERS.alloc([128], BF16)
    ones = PERS.alloc([128], BF16)
    modc = PERS.alloc([2, 48, 2], F32)
    acol = PERS.alloc([2, 2, 4, 8], F32)
    convw = PERS.alloc([2, 6], F32)
    zflag = PERS.alloc([2], F32)
    rowmask = PERS.alloc([NBLK, 8, 8], BF16)
    tab = PERS.alloc([8, TABW], BF16)
    tabI = PERS.alloc([8, 576], BF16)
    kc_sb = PERS.alloc([4, 256], BF16)
    vc_sb = PERS.alloc([2, 512], BF16)
    ctx_sb = PERS.alloc([2, DM], F32)
    g1bc = PERS.alloc([DM], F32)
    g2bc = PERS.alloc([DM], F32)

    ld(ident, D["ident"], [], ["ident"])
    ld(ones, D["ones"], [], ["ones"])
    ld(convw, D["conv_wT"].rearrange("l p k -> p l k"), [], ["convw"])
    ld(zflag, D["zflag"], [], ["zflag"])
    ld(rowmask, D["rowmask"], [], ["rowmask"])
    ld(ctx_sb, D["ctx"].rearrange("(t p) d -> p t d", p=128), [], ["ctx_sb"])

    def phase0(layers):
        TMP.reset()
        cT = TMP.alloc([16], F32)
        sT = TMP.alloc([16], F32)
        sl = TMP.alloc([16, 128], BF16)
        sb = TMP.alloc([16], BF16)
        abT = TMP.alloc([2, 48], F32)
        ngT = TMP.alloc([2, 2, 8], F32)
        abb = TMP.alloc([2, DM], F32)
        W = [TMP.alloc([8, 1024], BF16) for _ in range(6)]
        gt = [TMP.alloc([512], F32) for _ in range(2)]
        ld(cT, D["cvecT"], [], ["cT"])
        ld(abT, D["ada_bT"].rearrange("l p c -> p l c"), [], ["abT"])
        ld(ngT[:, :, 0, :], D["n1gT"].rearrange("l p c -> p l c"), [], ["ngT0"])
        ld(ngT[:, :, 1, :], D["n2gT"].rearrange("l p c -> p l c"), [], ["ngT1"])
        act(sT, cT, AF.Silu, ["cT"], ["sT"])
        S.op("dve", lambda e: e.tensor_copy(sb, sT), ["sT"], ["sb"])
        S.op("dve", lambda e: e.tensor_copy(sl, sT.unsqueeze(2).to_broadcast([128, 16, 128])), ["sT"], ["sl"])
        for l in layers:
            for s in range(6):
                ldw(W[s], wview("ada_w", l, 0, DM, s * 1024, (s + 1) * 1024), [], [("p0w", s)])
            for gi, c0 in enumerate((2048, 5120)):
                ld(abb[:, gi], D["ada_b"][l:l + 1, c0:c0 + 1024].partition_broadcast(128), [("gbcw", l, gi)], [("abb", gi)])
            for s in range(6):
                w = W[s]
                wk = ("p0w", s)
                b = nbank()
                for fc in range(8):
                    for k in range(8):
                        mm(ps[b][:, 2 * fc:2 * fc + 2], w[:, k, fc * 128:(fc + 1) * 128], sb[:, 2 * k:2 * k + 2],
                           k == 0, k == 7, [wk, "sb"], [PK(b)])
                tt(modc[:, l, s * 8:(s + 1) * 8, :], ps[b][:, 0:16].rearrange("p (c t) -> p c t", t=2),
                   abT[:, l, s * 8:(s + 1) * 8].unsqueeze(2).to_broadcast([128, 8, 2]), ALU.add,
                   [PK(b), "abT"], [("modc", l)])
                if s in (2, 5):
                    gi = 0 if s == 2 else 1
                    for t in range(2):
                        for hh in range(2):
                            b = nbank()
                            for k in range(8):
                                mm(ps[b][:, :], sl[:, 2 * k + t, :], w[:, k, hh * 512:(hh + 1) * 512],
                                   k == 0, k == 7, [wk, "sl"], [PK(b)])
                            g = gt[(t * 2 + hh) % 2]
                            gk = ("gt", (t * 2 + hh) % 2)
                            tt(g, ps[b][:, :], abb[:, gi, hh * 512:(hh + 1) * 512], ALU.add, [PK(b), ("abb", gi)], [gk])
                            ld(D["gbc"][l, t, gi, :, hh * 512:(hh + 1) * 512], g, [gk], [("gbcw", l, gi), ("gbc", l, t)])
            for t in range(2):
                for (vi, sc_c, ng_i) in ((0, 8, 0), (2, 32, 1)):
                    S.op("dve", lambda e, l=l, t=t, vi=vi, sc_c=sc_c, ng_i=ng_i: e.scalar_tensor_tensor(
                        acol[:, l, t, vi, :], modc[:, l, sc_c:sc_c + 8, t], 1.0, ngT[:, l, ng_i, :], ALU.add, ALU.mult),
                        [("modc", l), "ngT0", "ngT1"], [("acol", l)])
                for (vi, sh_c) in ((1, 0), (3, 24)):
                    S.op("dve", lambda e, l=l, t=t, vi=vi, sh_c=sh_c: e.tensor_copy(
                        acol[:, l, t, vi, :], modc[:, l, sh_c:sh_c + 8, t]), [("modc", l)], [("acol", l)])
        S.barrier()

    phase0([0])

    def norm_transpose(xt, ntile, A, B, hT, keys_x, key_h, scr, sfx=0):
        junk, ss, xn = scr
        for t in range(ntile):
            act(xn[:, t, :], xt[:, t, :], AF.Square, keys_x, [("xn", sfx, t), ("ss", sfx, t)], accum_out=ss[:, t:t + 1])
            S.op("dve", lambda e, t=t: e.tensor_scalar(ss[:, t:t + 1], ss[:, t:t + 1], 1.0 / DM, EPS, ALU.mult, ALU.add),
                 [("ss", sfx, t)], [("ss", sfx, t)])
            act(ss[:, t:t + 1], ss[:, t:t + 1], AF.Sqrt, [("ss", sfx, t)], [("ss", sfx, t)])
            S.op("dve", lambda e, t=t: e.reciprocal(ss[:, t:t + 1], ss[:, t:t + 1]), [("ss", sfx, t)], [("ss", sfx, t)])
            act(xn[:, t, :], xt[:, t, :], AF.Copy, keys_x + [("ss", sfx, t)], [("xn", sfx, t)], scale=ss[:, t:t + 1])
        for k in range(8):
            b = nbank()
            pb = ps[b].bitcast(BF16)
            for t in range(ntile):
                S.op("pe", lambda e, t=t, k=k, pb=pb: e.transpose(pb[:, t * 128:(t + 1) * 128], xn[:, t, k * 128:(k + 1) * 128], ident),
                     [("xn", sfx, t), "ident"], [PK(b)])
            act(hT[:, k, 0:ntile * 128], pb[:, 0:ntile * 128], AF.Identity, [PK(b)], [key_h],
                scale=A[:, k:k + 1], bias=B[:, k:k + 1])

    def proj_fm(hT, T, W, wkey, c0, nchunk, sink, hkey):
        for c in range(nchunk):
            b = nbank()
            for k in range(8):
                mm(ps[b][:, 0:T], W[:, k, c0 + c * 128:c0 + (c + 1) * 128], hT[:, k, 0:T], k == 0, k == 7,
                   [wkey, hkey], [PK(b)])
            sink(c, b)

    def alloc_scr():
        return (TMP.alloc([DM], BF16), TMP.alloc([8], F32), TMP.alloc([4, DM], BF16))

    class NS:
        pass

    U1K = ["kwin", "vwin", "zwin", "ywin", ("rden", 0), ("rden", 1)] + [("cacc", c) for c in range(2)] + \
          [("E", n) for n in range(5)] + [("P", n) for n in range(5)] + [("sg", n) for n in range(2)] + [("mtmp", n) for n in range(2)]

    def alloc_mixer():
        M = NS()
        M.WS = [TMP.alloc([4096], BF16) for _ in range(4)]
        M.ws_ctr = 0
        u0 = TMP.cur
        M.kwin = TMP.alloc([4, 1024], BF16)
        M.vwin = TMP.alloc([8, 512], BF16)
        M.Et = [TMP.alloc([512], BF16) for _ in range(5)]
        M.Pt = [TMP.alloc([512], BF16) for _ in range(5)]
        M.sg = [TMP.alloc([512], F32) for _ in range(2)]
        M.mtmp = [TMP.alloc([512], F32) for _ in range(2)]
        M.cacc = TMP.alloc([2, 512], F32)
        M.zwin = TMP.alloc([2, 514], BF16)
        M.ywin = TMP.alloc([2, 512], BF16)
        M.rden = TMP.alloc([512], F32)
        u1 = TMP.cur
        TMP.cur = u0
        M.h1 = TMP.alloc([32, 512], BF16)
        TMP.cur = max(TMP.cur, u1)
        M.atT = TMP.alloc([4, 512], BF16)
        M.cvT = TMP.alloc([2, 512], BF16)
        M.q_sb = TMP.alloc([4, 2, 512], BF16)
        S.op("dve", lambda e, q=M.q_sb: e.memset(q, 0.0), [], ["q_sb"])
        M.mT = TMP.alloc([8, 512], BF16)
        M.rr = [TMP.alloc([512], BF16) for _ in range(2)]
        M.otmp = [TMP.alloc([512], F32) for _ in range(2)]
        M.ep_ctr = 0
        M.sg_ctr = 0
        M.h1_free = set()
        return M

    def layer(l):
        last = (l == 1)
        xin = D["x"] if l == 0 else D["x1"]
        xout = D["x1"] if l == 0 else D["out"]
        xkin = "xin0" if l == 0 else "x1"
        xkout = "x1" if l == 0 else "outk"
        A1 = acol[:, l, 0, 0, :]; B1 = acol[:, l, 0, 1, :]; A2 = acol[:, l, 0, 2, :]; B2 = acol[:, l, 0, 3, :]
        cA1 = acol[:, l, 1, 0, :]; cB1 = acol[:, l, 1, 1, :]; cA2 = acol[:, l, 1, 2, :]; cB2 = acol[:, l, 1, 3, :]

        def load_wk():
            wk1 = TMP.alloc([8, 768], BF16)
            wk2 = TMP.alloc([8, 1024], BF16)
            ldw(wk1[:, :, 0:512], wview("w_in", l, 0, DM, 0, 512), [], ["wk1"])
            ldw(wk1[:, :, 512:768], wview("w_in", l, 0, DM, 768, 1024), [], ["wk1"])
            ldw(wk2, wview("w_in", l, 0, DM, 1536, 2560), [], ["wk2"])
            return wk1, wk2

        TMP.reset()
        maskF = TMP.alloc([TABW], BF16)
        ld(maskF, D["maskF"], [], ["maskF"])
        maskI = TMP.alloc([576], BF16)
        ld(maskI, D["maskI"], [], ["maskI"])
        rb = [TMP.alloc([TABW], F32) for _ in range(2)]
        eb = [TMP.alloc([TABW], BF16) for _ in range(2)]
        for h in range(8):
            ld(rb[h % 2], D["rbt"][l, h], [], [("rb", h % 2)])
            act(eb[h % 2], rb[h % 2], AF.Exp, [("rb", h % 2)], [("eb", h % 2)])
            tt(tab[:, h, :], eb[h % 2], maskF, ALU.mult, [("eb", h % 2), "maskF"], ["tab"])
            tt(tabI[:, h, :], eb[h % 2][:, 7 * 64:16 * 64], maskI, ALU.mult, [("eb", h % 2), "maskI"], ["tab"])
        S.barrier()

        def slab(M):
            n = M.ws_ctr % 4
            M.ws_ctr += 1
            return M.WS[n], ("ws", n)

        def mixer_block(M, scr, T, hT, hkey, xres, xkey, yT, ykey, zw, zkey, keychunks, Am, Bm, pre_wo=None):
            nt = T // 128
            cacc, q_sb, atT, cvT, mT, h1 = M.cacc, M.q_sb, M.atT, M.cvT, M.mT, M.h1
            for c in range(2):
                S.op("dve", lambda e, c=c: e.tensor_scalar(cacc[:, c, 0:T], zw[:, c, 0:T], convw[:, l, 3 * c:3 * c + 1], None, ALU.mult),
                     [zkey, "convw"], [("cacc", c)])
                for kk in (1, 2):
                    S.op("dve", lambda e, c=c, kk=kk: e.scalar_tensor_tensor(cacc[:, c, 0:T], zw[:, c, kk:kk + T],
                                                                            convw[:, l, 3 * c + kk:3 * c + kk + 1], cacc[:, c, 0:T], ALU.mult, ALU.add),
                         [zkey, "convw", ("cacc", c)], [("cacc", c)])
            w, wk = slab(M)
            wv = w.rearrange("p (k n) -> p k n", k=8)
            ldwb(wv[:, :, 0:256], "w_in", l, 0, DM, 512, 768, wk)

            def sink_cb(c, b):
                tt(cvT[:, c, 0:T], ps[b][:, 0:T], cacc[:, c, 0:T], ALU.mult, [PK(b), ("cacc", c)], ["cvT"])
            proj_fm(hT, T, wv, wk, 0, 2, sink_cb, hkey)
            w, wk = slab(M)
            wv = w.rearrange("p (k n) -> p k n", k=8)
            ldwb(wv, "w_in", l, 0, DM, 1024, 1536, wk)

            def sink_q(c, b):
                act(q_sb[0:64, c, 0, 0:T], ps[b][0:64, 0:T], AF.Copy, [PK(b)], ["q_sb"])
                act(q_sb[64:128, c, 1, 0:T], ps[b][64:128, 0:T], AF.Copy, [PK(b)], ["q_sb"])
            proj_fm(hT, T, wv, wk, 0, 4, sink_q, hkey)
            nkc = len(keychunks)
            sctr = 0
            for hp in range(4):
                OB, DB = (3, 5), (4, 6)
                order = [nkc - 2] + list(range(nkc - 2)) + [nkc - 1]
                items = [(hh, ci) for hh in range(2) for ci in order]
                sbanks = {}

                def crange(ci):
                    lo, hi = keychunks[ci][6], keychunks[ci][7]
                    return (0, T) if lo is None else (lo * 64, (hi + 1) * 64)

                def emit_S(it):
                    nonlocal sctr
                    hh, ci = it
                    kT_ap, kkey = keychunks[ci][0], keychunks[ci][4]
                    c0, c1 = crange(ci)
                    b = (0, 1, 2, 7)[sctr % 4]
                    sctr += 1
                    sbanks[it] = b
                    mm(ps[b][:, c0:c1], kT_ap(hp), q_sb[:, hp, hh, c0:c1], True, True, [kkey, "q_sb"], [PK(b)])

                def emit_PV(it):
                    hh, ci = it
                    _, v_ap, tspec, rmask, kkey, vkey, lo, hi = keychunks[ci]
                    c0, c1 = crange(ci)
                    b = sbanks[it]
                    n = M.ep_ctr % 5
                    M.ep_ctr += 1
                    E, P = M.Et[n], M.Pt[n]
                    act(E[:, c0:c1], ps[b][:, c0:c1], AF.Exp, [PK(b)], [("E", n)], scale=0.125)
                    src, skey = E, ("E", n)
                    h = 2 * hp + hh
                    if tspec is not None:
                        kind, e0 = tspec
                        if kind == "int":
                            tt(P[:, c0:c1], E[:, c0:c1], tabI[:, h, e0 * 64:e0 * 64 + (c1 - c0)], ALU.mult, [("E", n), "tab"], [("P", n)])
                        else:
                            tt(P[:, 0:T], E[:, 0:T], tab[:, h, e0 * 64:e0 * 64 + T], ALU.mult, [("E", n), "tab"], [("P", n)])
                            Pv = P[:, 0:T].rearrange("p (a b) -> p a b", b=64)
                            tt(Pv, Pv, rmask.unsqueeze(2).to_broadcast([128, 8, 64]), ALU.mult, [("P", n), "rowmask"], [("P", n)], eng="pool")
                        src, skey = P, ("P", n)
                    first, lastc = (ci == order[0]), (ci == order[-1])
                    mm(ps[OB[hh]][:, c0:c1], v_ap(hp), src[:, c0:c1], first, lastc, [skey, vkey], [PK(OB[hh])])
                    mm(ps[DB[hh]][:, c0:c1], ones, src[:, c0:c1], first, lastc, [skey, "ones"], [PK(DB[hh])])

                LA = 3
                for n_, it in enumerate(items):
                    emit_S(it)
                    if n_ >= LA:
                        emit_PV(items[n_ - LA])
                for it in items[len(items) - LA:]:
                    emit_PV(it)
                for hh in range(2):
                    pr = slice(64 * hh, 64 * hh + 64)
                    S.op("dve", lambda e, hh=hh, pr=pr: e.reciprocal(M.rden[pr, 0:T], ps[DB[hh]][pr, 0:T]), [PK(DB[hh])], [("rden", hh)])
                    tt(atT[pr, hp, 0:T], ps[OB[hh]][pr, 0:T], M.rden[pr, 0:T], ALU.mult, [PK(OB[hh]), ("rden", hh)], ["atT"])
            branches = ((2560, "w_fourier", 256, yT, ykey, 2), (3584, "w_conv", 256, cvT, "cvT", 2), (4608, "w_attn", 512, atT, "atT", 4))
            for bi, (gc0, wname, wrows, src, skey, nk) in enumerate(branches):
                wb, wbk = slab(M)
                wbv = wb.rearrange("p (k n) -> p k n", n=1024)
                ldwb(wbv[:, 0:nk, :], wname, l, 0, wrows, 0, DM, wbk)
                for half in range(2):
                    wg, wgk = slab(M)
                    wgv = wg.rearrange("p (k n) -> p k n", k=8)
                    ldwb(wgv, "w_in", l, 0, DM, gc0 + half * 512, gc0 + (half + 1) * 512, wgk)
                    for o4 in range(4):
                        oc = half * 4 + o4
                        bg = nbank()
                        for k in range(8):
                            mm(ps[bg][:, 0:T], wgv[:, k, o4 * 128:(o4 + 1) * 128], hT[:, k, 0:T], k == 0, k == 7, [wgk, hkey], [PK(bg)])
                        n = M.sg_ctr % 2
                        M.sg_ctr += 1
                        act(M.sg[n][:, 0:T], ps[bg][:, 0:T], AF.Sigmoid, [PK(bg)], [("sg", n)])
                        bp = nbank()
                        for k in range(nk):
                            mm(ps[bp][:, 0:T], wbv[:, k, oc * 128:(oc + 1) * 128], src[:, k, 0:T], k == 0, k == nk - 1,
                               [wbk, skey], [PK(bp)])
                        if bi == 0:
                            tt(mT[:, oc, 0:T], ps[bp][:, 0:T], M.sg[n][:, 0:T], ALU.mult, [PK(bp), ("sg", n)], [("mT", oc)])
                        else:
                            tt(M.mtmp[n][:, 0:T], ps[bp][:, 0:T], M.sg[n][:, 0:T], ALU.mult, [PK(bp), ("sg", n)], [("mtmp", n)])
                            tt(mT[:, oc, 0:T], mT[:, oc, 0:T], M.mtmp[n][:, 0:T], ALU.add, [("mT", oc), ("mtmp", n)], [("mT", oc)])
            mTk = [("mT", oc) for oc in range(8)]
            if getattr(M, "dbg", False) and T == 512:
                ld(D["dbg_at"], atT, ["atT"], [])
                ld(D["dbg_cv"], cvT, ["cvT"], [])
                ld(D["dbg_m"], mT, mTk, [])
            if pre_wo is not None:
                pre_wo()
            for hh in range(2):
                wo, wok = slab(M)
                wov = wo.rearrange("p (k n) -> p k n", k=8)
                ldwb(wov, "w_o", l, 0, DM, hh * 512, (hh + 1) * 512, wok)
                for t in range(nt):
                    b = nbank()
                    for k in range(8):
                        mm(ps[b][:, :], mT[:, k, t * 128:(t + 1) * 128], wov[:, k, :], k == 0, k == 7, mTk + [wok], [PK(b)])
                    n = M.sg_ctr % 2
                    M.sg_ctr += 1
                    tt(M.otmp[n], ps[b][:, :], g1bc[:, hh * 512:(hh + 1) * 512], ALU.mult, [PK(b), "g1bc"], [("otmp", n)])
                    tt(xres[:, t, hh * 512:(hh + 1) * 512], xres[:, t, hh * 512:(hh + 1) * 512], M.otmp[n], ALU.add,
                       [xkey, ("otmp", n)], [xkey])
            if getattr(M, "dbg", False) and T == 512:
                ld(D["dbg_xmix"], xres, [xkey], [])
            norm_transpose(xres, nt, Am, Bm, hT, [xkey], hkey, scr)
            first = True
            for s in range(8):
                w1, w1k = slab(M)
                w1v = w1.rearrange("p (k n) -> p k n", k=8)
                ldwb(w1v, "mlp_w1", l, 0, DM, s * 512, (s + 1) * 512, w1k)
                for c in range(4):
                    b = nbank()
                    for k in range(8):
                        mm(ps[b][:, 0:T], w1v[:, k, c * 128:(c + 1) * 128], hT[:, k, 0:T], k == 0, k == 7, [w1k, hkey], [PK(b)])
                    n = M.sg_ctr % 2
                    M.sg_ctr += 1
                    act(M.rr[n][:, 0:T], ps[b][:, 0:T], AF.Relu, [PK(b)], [("rr", n)])
                    tt(h1[:, s * 4 + c, 0:T], M.rr[n][:, 0:T], M.rr[n][:, 0:T], ALU.mult, [("rr", n)],
                       ["h1"] + (U1K if first else []))
                    first = False
            if getattr(M, "dbg", False) and T == 512:
                ld(D["dbg_h2"], hT, [hkey], [])
                ld(D["dbg_h1"], h1, ["h1"], [])
                M.dbg = False
            obanks = [(t, hh, nbank()) for t in range(nt) for hh in range(2)]
            lasttok = None
            for s in range(8):
                w2, w2k = slab(M)
                w2v = w2.rearrange("p (k n) -> p k n", k=4)
                ldwb(w2v, "mlp_w2", l, s * 512, (s + 1) * 512, 0, DM, w2k)
                for (t, hh, b) in obanks:
                    for k in range(4):
                        mm(ps[b][:, :], h1[:, s * 4 + k, t * 128:(t + 1) * 128], w2v[:, k, hh * 512:(hh + 1) * 512],
                           s == 0 and k == 0, s == 7 and k == 3, ["h1", w2k], [PK(b)])
            lasttok = ("c", "pe", len(S.ops["pe"]) - 1)
            for k_ in U1K:
                S.extra.setdefault(k_, set()).add(lasttok)
            for (t, hh, b) in obanks:
                n = M.sg_ctr % 2
                M.sg_ctr += 1
                tt(M.otmp[n], ps[b][:, :], g2bc[:, hh * 512:(hh + 1) * 512], ALU.mult, [PK(b), "g2bc"], [("otmp", n)])
                tt(xres[:, t, hh * 512:(hh + 1) * 512], xres[:, t, hh * 512:(hh + 1) * 512], M.otmp[n], ALU.add,
                   [xkey, ("otmp", n)], [xkey])

        def ctx_keychunks():
            kch = []
            for ci in range(2):
                kch.append((lambda hp, ci=ci: kc_sb[:, hp, ci * 128:(ci + 1) * 128],
                            lambda hp, ci=ci: vc_sb[:, ci, hp * 128:(hp + 1) * 128], None, None, "kc_sb", "vc_sb", None, None))
            return kch

        TMP.reset()
        scr2 = [alloc_scr() for _ in range(2)]
        wk1, wk2 = load_wk()
        xt2b = [TMP.alloc([4, DM], F32) for _ in range(2)]
        hx2b = [TMP.alloc([8, 512], BF16) for _ in range(2)]
        cust = [TMP.alloc([512], F32) for _ in range(2)]
        kst = [TMP.alloc([512], BF16) for _ in range(4)]
        p1ctr = [0]
        XU = D["xiu"][0:TOK, :].rearrange("(r x) n -> r (x n)", x=64).rearrange("r (h c) -> r h c", c=64)

        def stg():
            n = p1ctr[0] % 4
            p1ctr[0] += 1
            return kst[n], ("kst", n)

        for i in range(NBLK):
            xt, hxT, scr = xt2b[i % 2], hx2b[i % 2], scr2[i % 2]
            xtk, hxk = ("xt", i % 2), ("hxT1", i % 2)
            ld(xt, xin[i * 512:(i + 1) * 512, :].rearrange("(t p) d -> p t d", p=128), [xkin], [xtk])
            norm_transpose(xt, 4, A1, B1, hxT, [xtk], hxk, scr, sfx=i % 2)
            ld(D["hx_s"][i], hxT, [hxk], [("hx_s", i)])
            def sink_u(c, b, i=i):
                st_, sk_ = stg()
                S.op("dve", lambda e, st_=st_, b=b: e.tensor_copy(st_, ps[b][:, :]), [PK(b)], [sk_])
                ld(XU[8 * i:8 * i + 8, c * 128:(c + 1) * 128, :].rearrange("r h c -> h r c"),
                   st_.rearrange("p (r c) -> p r c", c=64), [sk_], ["xiu"])
            proj_fm(hxT, 512, wk1, "wk1", 0, 2, sink_u, hxk)

            def sink_z(c, b, i=i):
                if c < 2:
                    act(cust[c], ps[b][:, :], AF.Copy, [PK(b)], [("cust", c)])
                else:
                    st_, sk_ = stg()
                    tt(st_, ps[b][:, :], cust[c - 2], ALU.mult, [PK(b), ("cust", c - 2)], [sk_])
                    ld(D["z_s"][:, c - 2, 16 + i * 512:16 + (i + 1) * 512], st_, [sk_], ["z_s"])
            proj_fm(hxT, 512, wk1, "wk1", 256, 4, sink_z, hxk)

            def sink_k(c, b, i=i):
                st_, sk_ = stg()
                S.op("dve", lambda e, st_=st_, b=b: e.tensor_copy(st_, ps[b][:, :]), [PK(b)], [sk_])
                ld(D["k_s"][:, c, (4 + 8 * i) * 64:(4 + 8 * i) * 64 + 512], st_, [sk_], ["k_s"])
            proj_fm(hxT, 512, wk2, "wk2", 0, 4, sink_k, hxk)
            for t in range(4):
                b = nbank()
                for k in range(8):
                    mm(ps[b][:, :], hxT[:, k, t * 128:(t + 1) * 128], wk2[:, k, 512:1024], k == 0, k == 7, ["wk2", hxk], [PK(b)])
                st_, sk_ = stg()
                S.op("dve", lambda e, st_=st_, b=b: e.tensor_copy(st_, ps[b][:, :]), [PK(b)], [sk_])
                ld(D["v_s"][2 + 4 * i + t], st_, [sk_], ["v_s"])

        XI, XO = D["exch_in"], D["exch_out"]
        ld(XI[0:512, :].rearrange("(p c) n -> p c n", c=4), D["k_s"][:, :, 256:512], ["k_s"], ["exch_in"])
        ld(XI[512:1024, :].rearrange("(p c) n -> p c n", c=4), D["k_s"][:, :, 64 * 64:68 * 64], ["k_s"], ["exch_in"])
        ld(XI[1024:1536, :].rearrange("(c p h) n -> c p (h n)", c=2, p=128), D["v_s"][2:4], ["v_s"], ["exch_in"])
        ld(XI[1536:2048, :].rearrange("(c p h) n -> c p (h n)", c=2, p=128), D["v_s"][32:34], ["v_s"], ["exch_in"])
        ld(XI[2048:2064, :].rearrange("r (pp c t) -> (r pp) c t", pp=8, c=2, t=16), D["z_s"][:, :, 16:32], ["z_s"], ["exch_in"])
        ld(XI[2064:2080, :].rearrange("r (pp c t) -> (r pp) c t", pp=8, c=2, t=16), D["z_s"][:, :, TOK:TOK + 16], ["z_s"], ["exch_in"])
        RG = [[2 * g, 2 * g + 1] for g in range(ncores // 2)]
        S.op("pool", lambda e: e.collective_compute("AllGather", ALU.bypass, replica_groups=RG, ins=[D["xiu"]], outs=[D["xou"]]),
             ["xiu"], ["xou"], force_sig=True)
        S.op("pool", lambda e: e.collective_compute("AllGather", ALU.bypass, replica_groups=RG, ins=[XI], outs=[XO]),
             ["exch_in"], ["exch_out"], force_sig=True)
        S.barrier(skip_cc=True)
        if l == 0:
            phase0([1])
        TMP.reset()
        scr = alloc_scr()
        hcT = TMP.alloc([8, 256], BF16)
        if not last:
            ucT = TMP.alloc([2, 256], BF16)
            zc = TMP.alloc([2, 258], BF16)
            cuc = TMP.alloc([2, 256], F32)
        markA = TMP.cur
        wk1, wk2 = load_wk()
        norm_transpose(ctx_sb, 2, cA1, cB1, hcT, ["ctx_sb"], "hcT", scr)

        def sink_kc(c, b):
            act(kc_sb[:, c, :], ps[b][:, 0:256], AF.Copy, [PK(b)], ["kc_sb"])
        proj_fm(hcT, 256, wk2, "wk2", 0, 4, sink_kc, "hcT")
        for t in range(2):
            b = nbank()
            for k in range(8):
                mm(ps[b][:, :], hcT[:, k, t * 128:(t + 1) * 128], wk2[:, k, 512:1024], k == 0, k == 7, ["wk2", "hcT"], [PK(b)])
            act(vc_sb[:, t, :], ps[b][:, :], AF.Copy, [PK(b)], ["vc_sb"])
        if not last:
            S.op("dve", lambda e: e.memset(zc, 0.0), [], ["zc"])

            def sink_c1(c, b):
                if c < 2:
                    act(ucT[:, c, :], ps[b][:, 0:256], AF.Copy, [PK(b)], ["ucT"])
                elif c < 4:
                    act(cuc[:, c - 2, :], ps[b][:, 0:256], AF.Copy, [PK(b)], [("cuc", c - 2)])
                else:
                    tt(zc[:, c - 4, 1:257], ps[b][:, 0:256], cuc[:, c - 4, :], ALU.mult, [PK(b), ("cuc", c - 4)], ["zc"])
            proj_fm(hcT, 256, wk1, "wk1", 0, 6, sink_c1, "hcT")
        S.barrier()
        if not last:
            TMP.cur = markA
            M = alloc_mixer()
            ld(g1bc, D["gbc"][l, 1, 0], [("gbc", l, 1)], ["g1bc"])
            ld(g2bc, D["gbc"][l, 1, 1], [("gbc", l, 1)], ["g2bc"])
            chc = TMP.alloc([2, 128], BF16)
            d256 = TMP.alloc([2, 2, 256], BF16)
            zri = TMP.alloc([2, 2, 256], BF16)
            ycT = TMP.alloc([2, 256], BF16)
            ld(chc, D["chc"], [], ["chc"])
            ld(d256, D["dft256"], [], ["d256"])
            for ri in range(2):
                for tc in range(2):
                    b = nbank()
                    for cc in range(2):
                        mm(ps[b][:, cc * 128:(cc + 1) * 128], ucT[:, cc, tc * 128:(tc + 1) * 128], chc[:, ri, :], True, True,
                           ["ucT", "chc"], [PK(b)])
                    act(zri[:, ri, tc, :], ps[b][:, 0:256], AF.Copy, [PK(b)], ["zri"])
            for cc in range(2):
                b = nbank()
                n_ = 0
                for ri in range(2):
                    for tc in range(2):
                        mm(ps[b][:, 0:256], zri[:, ri, tc, cc * 128:(cc + 1) * 128], d256[:, ri, tc, :], n_ == 0, n_ == 3,
                           ["zri", "d256"], [PK(b)])
                        n_ += 1
                act(ycT[:, cc, :], ps[b][:, 0:256], AF.Copy, [PK(b)], ["ycT"])
            WMODE["cast_store"] = True
            mixer_block(M, scr, 256, hcT, "hcT", ctx_sb, "ctx_sb", ycT, "ycT", zc, "zc", ctx_keychunks(), cA2, cB2)
            WMODE["cast_store"] = False
            S.barrier()

        ld(D["k_s"][:, :, 0:256], XO[512:1024, :].rearrange("(p c) n -> p c n", c=4), ["exch_out"], ["k_s"])
        ld(D["k_s"][:, :, 68 * 64:72 * 64], XO[RX:RX + 512, :].rearrange("(p c) n -> p c n", c=4), ["exch_out"], ["k_s"])
        ld(D["v_s"][0:2], XO[1536:2048, :].rearrange("(c p h) n -> c p (h n)", c=2, p=128), ["exch_out"], ["v_s"])
        ld(D["v_s"][34:36], XO[RX + 1024:RX + 1536, :].rearrange("(c p h) n -> c p (h n)", c=2, p=128), ["exch_out"], ["v_s"])
        ld(D["z_s"][:, :, 0:16], XO[2064:2080, :].rearrange("r (pp c t) -> (r pp) c t", pp=8, c=2, t=16), ["exch_out"], ["z_s"])
        ld(D["z_s"][:, :, TOK + 16:TOK + 32], XO[RX + 2048:RX + 2064, :].rearrange("r (pp c t) -> (r pp) c t", pp=8, c=2, t=16), ["exch_out"], ["z_s"])
        S.barrier()
        if stage <= 2:
            return

        TMP.reset()
        cs128 = TMP.alloc([256], BF16)
        ch3 = TMP.alloc([2, 128], BF16)
        gtab = TMP.alloc([128, 96], BF16)
        U = TMP.alloc([128, 64], BF16)
        Asb = TMP.alloc([64, 256], BF16)
        Xsb = TMP.alloc([2, TOK], BF16)
        yst = [TMP.alloc([512], BF16) for _ in range(2)]
        ld(cs128, D["cs128"], [], ["cs128"])
        ld(ch3, D["ch3"], [], ["ch3"])
        ld(gtab, D["gtab"], [], ["gtab"])
        Uf = U.rearrange("p h c -> p (h c)")
        Xv = Xsb.rearrange("p r (k2 k1) -> p r k2 k1", k1=128)
        for cc in range(2):
            XOU0 = D["xou"][0:TOK, :].rearrange("(r x) n -> r (x n)", x=64).rearrange("r (h c) -> r h c", c=64)
            XOU1 = D["xou"][TOK:2 * TOK, :].rearrange("(r x) n -> r (x n)", x=64).rearrange("r (h c) -> r h c", c=64)
            ld(U[0:64], XOU0[:, cc * 128:(cc + 1) * 128, :], ["xou"], ["U"])
            ld(U[64:128], XOU1[:, cc * 128:(cc + 1) * 128, :], ["xou"], ["U"])
            for qq in range(32):
                b = nbank()
                for j in range(2):
                    q = 2 * qq + j
                    mm(ps[b][:, j * 256:(j + 1) * 256], Uf[:, 128 * q:128 * q + 128], cs128, True, True, ["U", "cs128"], [PK(b)])
                src = ps[b][:, :].rearrange("p (a b) -> p a b", a=2)
                if qq % 2 == 0:
                    act(Asb[:, 2 * qq:2 * qq + 2, :], src, AF.Copy, [PK(b)], ["Asb"])
                else:
                    S.op("dve", lambda e, qq=qq, src=src: e.tensor_copy(Asb[:, 2 * qq:2 * qq + 2, :], src), [PK(b)], ["Asb"])
            for g8 in range(16):
                b = nbank()
                for kl in range(8):
                    k1 = g8 * 8 + kl
                    for j in range(2):
                        pr = slice(64 * j, 64 * j + 64)
                        mm(ps[b][pr, kl * 64:(kl + 1) * 64], Asb[pr, :, k1], gtab[pr, k1, 32:96], True, False,
                           ["Asb", "gtab"], [PK(b)], tp=(64 * j, 64 * j))
                        mm(ps[b][pr, kl * 64:(kl + 1) * 64], Asb[pr, :, 128 + k1], gtab[pr, k1, 0:64], False, True,
                           ["Asb", "gtab"], [PK(b)], tp=(64 * j, 64 * j))
                src = ps[b][:, :].rearrange("p (kl r k2) -> p r k2 kl", kl=8, r=2, k2=32)
                dst = Xv[:, :, :, g8 * 8:(g8 + 1) * 8]
                if g8 % 2 == 0:
                    act(dst, src, AF.Copy, [PK(b)], ["Xsb"])
                else:
                    S.op("dve", lambda e, dst=dst, src=src: e.tensor_copy(dst, src), [PK(b)], ["Xsb"])
            for i in range(NBLK):
                b = nbank()
                mm(ps[b][:, :], ch3[:, 0, :], Xsb[:, 0, i * 512:(i + 1) * 512], True, False, ["Xsb", "ch3"], [PK(b)])
                mm(ps[b][:, :], ch3[:, 1, :], Xsb[:, 1, i * 512:(i + 1) * 512], False, True, ["Xsb", "ch3"], [PK(b)])
                y = yst[i % 2]
                act(y, ps[b][:, :], AF.Copy, [PK(b)], [("yst", i % 2)])
                ld(D["y_s"][:, cc, i * 512:(i + 1) * 512], y, [("yst", i % 2)], ["y_s"])
        S.barrier()
        if stage <= 3:
            return

        TMP.reset()
        scr = alloc_scr()
        junk, ss, xn = scr
        M = alloc_mixer()
        M.dbg = ("mix" in debug) and l == dbg_layer
        hx2 = [TMP.alloc([8, 512], BF16) for _ in range(2)]
        xt2 = TMP.alloc([4, DM], F32)
        fgb = TMP.alloc([DM], F32)
        ld(g1bc, D["gbc"][l, 0, 0], [("gbc", l, 0)], ["g1bc"])
        ld(g2bc, D["gbc"][l, 0, 1], [("gbc", l, 0)], ["g2bc"])
        if last:
            ld(fgb, D["final_g"].partition_broadcast(128), [], ["fgb"])
        nblk2 = NBLK if stage > 4 else 1
        ld(hx2[0], D["hx_s"][0], [("hx_s", 0)], [("hxT", 0)])
        for i in range(nblk2):
            hxT, hkey_ = hx2[i % 2], ("hxT", i % 2)
            if i + 1 < nblk2:
                ld(hx2[(i + 1) % 2], D["hx_s"][i + 1], [("hx_s", i + 1)], [("hxT", (i + 1) % 2)])
            ld(M.zwin, D["z_s"][:, :, 15 + i * 512:15 + i * 512 + 514], ["z_s"], ["zwin"])
            ld(M.ywin, D["y_s"][:, :, i * 512:(i + 1) * 512], ["y_s"], ["ywin"])
            ld(M.kwin, D["k_s"][:, :, 8 * i * 64:(8 * i + 16) * 64], ["k_s"], ["kwin"])
            ld(M.vwin, D["v_s"][4 * i:4 * i + 8].rearrange("c p n -> p c n"), ["v_s"], ["vwin"])

            def pre_wo(i=i):
                ld(xt2, xin[i * 512:(i + 1) * 512, :].rearrange("(t p) d -> p t d", p=128), [xkin], ["xt2"])
            if i == 0:
                S.op("dve", lambda e: e.tensor_scalar(M.zwin[:, :, 0:1], M.zwin[:, :, 0:1], zflag[:, 0:1], None, ALU.mult),
                     ["zwin", "zflag"], ["zwin"])
            if i == NBLK - 1:
                S.op("dve", lambda e: e.tensor_scalar(M.zwin[:, :, 513:514], M.zwin[:, :, 513:514], zflag[:, 1:2], None, ALU.mult),
                     ["zwin", "zflag"], ["zwin"])
            kch = []
            RNG = [(0, 1), (0, 3), (0, 5), (0, 7), (1, 7), (3, 7), (5, 7), (7, 7)]
            for j in range(8):
                if 0 < i < NBLK - 1:
                    lo, hi = RNG[j]
                    kch.append((lambda hp, j=j: M.kwin[:, hp, j * 128:(j + 1) * 128],
                                lambda hp, j=j: M.vwin[:, j, hp * 128:(hp + 1) * 128], ("int", 7 - 2 * j + lo), None, "kwin", "vwin", lo, hi))
                else:
                    kch.append((lambda hp, j=j: M.kwin[:, hp, j * 128:(j + 1) * 128],
                                lambda hp, j=j: M.vwin[:, j, hp * 128:(hp + 1) * 128], ("full", 14 - 2 * j), rowmask[:, i, j, :], "kwin", "vwin", None, None))
            kch += ctx_keychunks()
            WMODE["cast_store"] = (last and i == 0)
            mixer_block(M, scr, 512, hxT, hkey_, xt2, "xt2", M.ywin, "ywin", M.zwin, "zwin", kch, A2, B2, pre_wo=pre_wo)
            WMODE["cast_store"] = False
            if last:
                for t in range(4):
                    act(xn[:, t, :], xt2[:, t, :], AF.Square, ["xt2"], [("xn", 0, t), ("ss", 0, t)], accum_out=ss[:, t:t + 1])
                    S.op("dve", lambda e, t=t: e.tensor_scalar(ss[:, t:t + 1], ss[:, t:t + 1], 1.0 / DM, EPS, ALU.mult, ALU.add),
                         [("ss", 0, t)], [("ss", 0, t)])
                    act(ss[:, t:t + 1], ss[:, t:t + 1], AF.Sqrt, [("ss", 0, t)], [("ss", 0, t)])
                    S.op("dve", lambda e, t=t: e.reciprocal(ss[:, t:t + 1], ss[:, t:t + 1]), [("ss", 0, t)], [("ss", 0, t)])
                    S.op("dve", lambda e, t=t: e.scalar_tensor_tensor(xt2[:, t, :], xt2[:, t, :], ss[:, t:t + 1], fgb, ALU.mult, ALU.mult),
                         ["xt2", ("ss", 0, t), "fgb"], ["xt2"])
            ld(xout[i * 512:(i + 1) * 512, :].rearrange("(t p) d -> p t d", p=128), xt2, ["xt2"], [xkout, ("blkdone", l, i)])
        S.barrier()

    for l in range(2):
        layer(l)
        if stage <= 5:
            break

    S.barrier()
    DUMPS = {"kc_sb": ([128, 4, 256], BF16, kc_sb), "vc_sb": ([128, 2, 512], BF16, vc_sb), "ctx_sb": ([128, 2, DM], F32, ctx_sb),
             "acol": ([128, 2, 2, 4, 8], F32, acol), "modc": ([128, 2, 48, 2], F32, modc), "tab": ([128, 8, TABW], BF16, tab)}
    for nm in debug:
        if nm == "mix":
            continue
        if nm in DUMPS:
            shp, dt, src = DUMPS[nm]
        else:
            src = D[nm]
            shp, dt = list(src.shape), src.dtype
        dout("dbg_" + nm, shp, dt)
        ld(D["dbg_" + nm], src, [], [])
    S.wait_all("sp")

    nsig = S.plan()
    import contextlib
    with contextlib.ExitStack() as st:
        esems = {e: [st.enter_context(nc.semaphore(f"s_{e}_{n}")) for n in range(nsig[e] // EPOCH + 1)] for e in Sched.ENGS}
        dsems = {q: [st.enter_context(nc.semaphore(f"d_{q}_{n}")) for n in range(NSLOT)] for q in ("sp", "pool")}
        block = st.enter_context(nc.Block())

        @block.tensor
        def _(h):
            S.emit("pe", h, esems, dsems)

        @block.scalar
        def _(h):
            S.emit("act", h, esems, dsems)

        @block.vector
        def _(h):
            S.emit("dve", h, esems, dsems)

        @block.gpsimd
        def _(h):
            S.emit("pool", h, esems, dsems)

        @block.sync
        def _(h):
            S.emit("sp", h, esems, dsems)
    return nc


def make_in_maps(inp):
    f32 = lambda a: np.ascontiguousarray(np.asarray(a, dtype=np.float32))
    HC = host_constants()
    CC = [core_constants(h) for h in range(2)]
    shared = {
        "ada_w": f32(inp["ada_w"]), "ada_b": f32(inp["ada_b"]),
        "ada_bT": f32(np.asarray(inp["ada_b"]).reshape(2, 48, 128).transpose(0, 2, 1)),
        "n1gT": f32(np.asarray(inp["norm1_g"]).reshape(2, 8, 128).transpose(0, 2, 1)),
        "n2gT": f32(np.asarray(inp["norm2_g"]).reshape(2, 8, 128).transpose(0, 2, 1)),
        "final_g": f32(np.asarray(inp["final_g"]).reshape(1, DM)),
        "w_in": f32(inp["w_in"]),
        "conv_wT": f32(np.asarray(inp["conv_w"]).reshape(2, 3, 2, 128).transpose(0, 3, 2, 1).reshape(2, 128, 6)),
        "rbt": gather_relbias(np.asarray(inp["rel_bias"], dtype=np.float32)),
        "w_fourier": f32(inp["w_fourier"]), "w_conv": f32(inp["w_conv"]), "w_attn": f32(inp["w_attn"]),
        "w_o": f32(inp["w_o"]), "mlp_w1": f32(inp["mlp_w1"]), "mlp_w2": f32(inp["mlp_w2"]),
    }
    shared.update(HC)
    maps = []
    x = np.asarray(inp["x"]); c = np.asarray(inp["c"]); ctx = np.asarray(inp["ctx"]); c_ctx = np.asarray(inp["c_ctx"])
    for core in range(NCORES):
        b, half = core // 2, core % 2
        m = dict(shared)
        m.update(CC[half])
        m["x"] = f32(x[b, half * TOK:(half + 1) * TOK])
        m["ctx"] = f32(ctx[b])
        cv = np.stack([c[b], c_ctx], axis=1).reshape(8, 128, 2).transpose(1, 0, 2).reshape(128, 16)
        m["cvecT"] = f32(cv)
        maps.append(m)
    return maps


_NC_CACHE = {}


def kernel(**inputs):
    if "nc" not in _NC_CACHE:
        _NC_CACHE["nc"] = build_nc()
    nc = _NC_CACHE["nc"]
    maps = make_in_maps(inputs)
    res = run_bass_kernel_spmd(nc, maps, core_ids=list(range(NCORES)))
    out = np.empty((4, 8192, DM), np.float32)
    for core in range(NCORES):
        b, half = core // 2, core % 2
        out[b, half * TOK:(half + 1) * TOK] = res.results[core]["out"]
    return out
```

```python
import numpy as np
import ml_dtypes
import concourse.bass as bass
import concourse.mybir as mybir
from concourse.bass_utils import run_bass_kernel_spmd

F32 = mybir.dt.float32
BF16 = mybir.dt.bfloat16
AF = mybir.ActivationFunctionType
ALU = mybir.AluOpType

NSLOT = 16
EPOCH = 8000
NCORES = 8
DM = 1024
TOK = 4096
NBLK = 8
INW = 5632
RX = 2080
NE = 22
TABW = NE * 64
EPS = 1e-6


class Sched:
    ENGS = ["pe", "act", "dve", "pool", "sp"]

    def __init__(self):
        self.ops = {e: [] for e in self.ENGS}
        self.res = {}
        self.ndma = {e: 0 for e in self.ENGS}
        self.pending = {e: set() for e in self.ENGS}
        self.extra = {}

    def _collect(self, eng, reads, writes):
        deps = set(self.pending[eng])
        self.pending[eng] = set()
        for k in reads:
            r = self.res.get(k)
            if r is not None and r[0] is not None:
                t = r[0]
                if t[0] == "c" and t[1] == eng and eng in ("pe", "sp"):
                    continue
                deps.add(t)
        for k in writes:
            if k in self.extra:
                deps |= self.extra.pop(k)
            r = self.res.get(k)
            if r is not None:
                for t in ([r[0]] if r[0] is not None else []) + list(r[1]):
                    if t[0] == "c" and t[1] == eng and eng == "pe":
                        continue
                    deps.add(t)
        return deps

    def _update(self, reads, writes, tok):
        for k in reads:
            r = self.res.setdefault(k, [None, []])
            r[1].append(tok)
        for k in writes:
            self.res[k] = [tok, []]

    def op(self, eng, fn, reads=(), writes=(), force_sig=False):
        deps = self._collect(eng, reads, writes)
        tok = ("c", eng, len(self.ops[eng]))
        self.ops[eng].append(dict(fn=fn, deps=deps, tok=tok, dma=None, force_sig=force_sig, cc=force_sig))
        self._update(reads, writes, tok)
        return tok

    def dma(self, eng, fn, reads=(), writes=()):
        deps = self._collect(eng, reads, writes)
        j = self.ndma[eng]
        self.ndma[eng] += 1
        if j >= NSLOT:
            deps.add(("d", eng, j - NSLOT))
        tok = ("d", eng, j)
        self.ops[eng].append(dict(fn=fn, deps=deps, tok=tok, dma=j))
        self._update(reads, writes, tok)
        return tok

    def barrier(self, skip_cc=False):
        toks = set()
        for e in self.ENGS:
            last = None
            for o in reversed(self.ops[e]):
                if skip_cc and o.get("cc"):
                    continue
                if o["dma"] is None and o["fn"] is not None:
                    last = o["tok"]
                    break
            if last is not None:
                toks.add(last)
        for q in self.ENGS:
            for j in range(max(0, self.ndma[q] - NSLOT), self.ndma[q]):
                toks.add(("d", q, j))
        for e in self.ENGS:
            self.pending[e] |= toks

    def wait_all(self, eng):
        self.barrier()
        deps = set(self.pending[eng])
        self.pending[eng] = set()
        self.ops[eng].append(dict(fn=None, deps=deps, tok=("c", eng, len(self.ops[eng])), dma=None))

    def plan(self):
        sig = {e: set() for e in self.ENGS}
        for e in self.ENGS:
            fc = {}
            fd = {}
            for idx, o in enumerate(self.ops[e]):
                waits = []
                for t in sorted(o["deps"], key=str):
                    if t[0] == "c":
                        _, e2, i2 = t
                        if e2 == e and i2 >= idx:
                            continue
                        if fc.get(e2, -1) >= i2:
                            continue
                        fc[e2] = i2
                        waits.append(t)
                        sig[e2].add(i2)
                    else:
                        q, j = t[1], t[2]
                        slot, val = (q, j % NSLOT), j // NSLOT
                        if fd.get(slot, -1) >= val:
                            continue
                        fd[slot] = val
                        waits.append(t)
                o["waits"] = waits
        self.ordinal = {}
        self.nsig = {}
        for e in self.ENGS:
            for idx, o in enumerate(self.ops[e]):
                if o.get("force_sig"):
                    sig[e].add(idx)
        for e in self.ENGS:
            for n, i in enumerate(sorted(sig[e])):
                self.ordinal[(e, i)] = n
            self.nsig[e] = len(sig[e])
        return self.nsig

    def emit(self, eng, h, esems, dsems):
        for idx, o in enumerate(self.ops[eng]):
            for t in o["waits"]:
                if t[0] == "c":
                    n = self.ordinal[(t[1], t[2])]
                    h.wait_ge(esems[t[1]][n // EPOCH], n % EPOCH + 1)
                else:
                    q, j = t[1], t[2]
                    h.wait_ge(dsems[q][j % NSLOT], 16 * (j // NSLOT + 1))
            if o["fn"] is None:
                continue
            ins = o["fn"](h)
            if o["dma"] is not None:
                ins.then_inc(dsems[eng][o["dma"] % NSLOT], 16)
            elif (eng, idx) in self.ordinal:
                n = self.ordinal[(eng, idx)]
                ins.then_inc(esems[eng][n // EPOCH], 1)


def _bf(a):
    return np.ascontiguousarray(a.astype(ml_dtypes.bfloat16))


def host_constants():
    C = {}
    C["ident"] = _bf(np.eye(128, dtype=np.float32))
    r = np.arange(128)[:, None].astype(np.float64)
    k1 = np.arange(128)[None, :].astype(np.float64)
    ang = 2 * np.pi * r * k1 / 128.0
    C["cs128"] = _bf(np.concatenate([np.cos(ang), -np.sin(ang)], axis=1))
    norm = 1.0 / np.sqrt(8192.0 * 64.0)
    p = np.arange(128)
    chp = 2 * (p % 64) + (p // 64)
    co = np.arange(128)
    same = (chp[:, None] // 64) == (co[None, :] // 64)
    a3 = 2 * np.pi * (chp[:, None] % 64) * (co[None, :] % 64) / 64.0
    C["ch3"] = _bf(np.stack([np.cos(a3) * same * norm, np.sin(a3) * same * norm], 0).transpose(1, 0, 2))
    same2 = (p[:, None] // 64) == (co[None, :] // 64)
    a4 = 2 * np.pi * (p[:, None] % 64) * (co[None, :] % 64) / 64.0
    C["chc"] = _bf(np.stack([np.cos(a4) * same2, np.sin(a4) * same2], 0).transpose(1, 0, 2))
    n = np.arange(256)[:, None].astype(np.float64)
    k = np.arange(256)[None, :].astype(np.float64)
    a5 = 2 * np.pi * n * k / 256.0
    nc_ = 1.0 / np.sqrt(256.0 * 64.0)
    t = np.stack([np.cos(a5) * nc_, -np.sin(a5) * nc_], 0)
    C["dft256"] = _bf(t.reshape(2, 2, 128, 256).transpose(2, 0, 1, 3))
    pp = np.arange(128)
    krl = pp // 64
    kc = pp % 64
    e = np.arange(NE)
    qc = np.arange(64)
    dr = 17 - e[None, :, None] + krl[:, None, None]
    cs = np.clip(qc - 8, 0, 48)
    colin = (kc[:, None, None] >= cs[None, None, :]) & (kc[:, None, None] < cs[None, None, :] + 16)
    m = (dr >= 0) & (dr <= 14) & colin
    C["maskF"] = _bf(m.reshape(128, TABW).astype(np.float32))
    mi = m & (dr >= 3) & (dr <= 10)
    C["maskI"] = _bf(mi[:, 7:16, :].reshape(128, 9 * 64).astype(np.float32))
    C["ones"] = _bf(np.ones((128, 128), np.float32))
    return C


def core_constants(half):
    C = {}
    p = np.arange(128)
    c = (p % 64)[:, None, None].astype(np.float64)
    k1 = np.arange(128)[None, :, None].astype(np.float64)
    k2 = (32 * half + np.arange(32))[None, None, :].astype(np.float64)
    ang = 2 * np.pi * c * (k1 + 128.0 * k2) / 8192.0
    gr, gi = np.cos(ang), -np.sin(ang)
    C["gtab"] = _bf(np.concatenate([-gi, gr, gi], axis=2))
    krl = p // 64
    rm = np.zeros((128, NBLK, 8, 8), np.float32)
    for i in range(NBLK):
        for j in range(8):
            for q in range(8):
                rq = 64 * half + 8 * i + q
                kr = 64 * half + 8 * i - 4 + 2 * j + krl
                rs = min(max(rq - 4, 0), 120)
                rm[:, i, j, q] = ((kr >= 0) & (kr < 128) & (kr >= rs) & (kr < rs + 8)).astype(np.float32)
    C["rowmask"] = _bf(rm)
    zf = np.zeros((128, 2), np.float32)
    zf[:, 0] = 0.0 if half == 0 else 1.0
    zf[:, 1] = 0.0 if half == 1 else 1.0
    C["zflag"] = zf
    return C


def gather_relbias(rel_bias):
    p = np.arange(128)
    krl = (p // 64)[:, None, None]
    kc = (p % 64)[:, None, None]
    e = np.arange(NE)[None, :, None]
    qc = np.arange(64)[None, None, :]
    dr = np.clip(17 - e + krl, 0, 14) + 0 * qc
    dc = np.clip(kc - qc, -15, 15) + 15 + 0 * e
    t = rel_bias[:, :, dr, dc]
    return np.ascontiguousarray(t.reshape(rel_bias.shape[0], rel_bias.shape[1], 128, TABW).astype(np.float32))


def build_nc(stage=99, debug=(), ncores=NCORES):
    nc = bass.Bass("TRN2", target_bir_lowering=False)
    S = Sched()
    D = {}

    def din(name, shape, dt=F32):
        D[name] = nc.dram_tensor(name, list(shape), dt, kind="ExternalInput").ap()

    def dscr(name, shape, dt):
        D[name] = nc.dram_tensor(name, list(shape), dt).ap()

    def dout(name, shape, dt=F32):
        D[name] = nc.dram_tensor(name, list(shape), dt, kind="ExternalOutput").ap()

    din("x", [TOK, DM]); din("ctx", [256, DM]); din("cvecT", [128, 16])
    din("ada_w", [2, DM, 6 * DM]); din("ada_bT", [2, 128, 48]); din("ada_b", [2, 6 * DM])
    din("n1gT", [2, 128, 8]); din("n2gT", [2, 128, 8]); din("final_g", [1, DM])
    din("w_in", [2, DM, INW]); din("conv_wT", [2, 128, 6]); din("rbt", [2, 8, 128, TABW])
    din("w_fourier", [2, 256, DM]); din("w_conv", [2, 256, DM]); din("w_attn", [2, 512, DM])
    din("w_o", [2, DM, DM]); din("mlp_w1", [2, DM, 4 * DM]); din("mlp_w2", [2, 4 * DM, DM])
    din("ident", [128, 128], BF16); din("cs128", [128, 256], BF16); din("ch3", [128, 2, 128], BF16)
    din("chc", [128, 2, 128], BF16); din("dft256", [128, 2, 2, 256], BF16); din("maskF", [128, TABW], BF16); din("maskI", [128, 576], BF16)
    din("ones", [128, 128], BF16); din("gtab", [128, 128, 96], BF16); din("rowmask", [128, NBLK, 8, 8], BF16)
    din("zflag", [128, 2])
    dout("out", [TOK, DM])
    dscr("x1", [TOK, DM], F32)
    dscr("hx_s", [NBLK, 128, 8, 512], BF16)
    dscr("xiu", [TOK, 256], BF16)
    dscr("xou", [2 * TOK, 256], BF16)
    dscr("exch_in", [RX, 256], BF16)
    dscr("exch_out", [2 * RX, 256], BF16)
    dscr("k_s", [128, 4, 72 * 64], BF16)
    dscr("v_s", [36, 128, 512], BF16)
    dscr("z_s", [128, 2, TOK + 32], BF16)
    dscr("y_s", [128, 2, TOK], BF16)
    dscr("gbc", [2, 2, 2, 128, DM], F32)
    WSHP = {"w_in": [DM, INW], "w_fourier": [256, DM], "w_conv": [256, DM], "w_attn": [512, DM], "w_o": [DM, DM],
            "mlp_w1": [DM, 4 * DM], "mlp_w2": [4 * DM, DM]}
    for nm_, shp_ in WSHP.items():
        dscr("b_" + nm_, [2] + shp_, BF16)

    dbg_layer = 1 if "L1" in debug else 0
    debug = [d_ for d_ in debug if d_ != "L1"]
    if "mix" in debug:
        dout("dbg_at", [128, 4, 512], BF16); dout("dbg_cv", [128, 2, 512], BF16); dout("dbg_m", [128, 8, 512], BF16)
        dout("dbg_xmix", [128, 4, DM], F32)
        dout("dbg_h2", [128, 8, 512], BF16); dout("dbg_h1", [128, 32, 512], BF16)
    ARENA = 212000
    arena = nc.alloc_sbuf_tensor("arena", [128, ARENA // 2], BF16)
    ps = [nc.alloc_psum_tensor(f"ps{i}", [128, 512], F32) for i in range(8)]

    class Region:
        def __init__(self, base, size):
            self.base, self.size, self.cur = base, size, base

        def reset(self):
            self.cur = self.base

        def alloc(self, shape, dt):
            n = int(np.prod(shape))
            nb = n * (4 if dt == F32 else 2)
            off = (self.cur + 63) // 64 * 64
            assert off + nb <= self.base + self.size, ("arena overflow", shape, off + nb - self.base, self.size)
            self.cur = off + nb
            ap = arena[:, off // 2:(off + nb) // 2]
            if dt == F32:
                ap = ap.bitcast(F32)
            if len(shape) == 2:
                ap = ap.rearrange("p (a b) -> p a b", b=int(shape[1]))
            elif len(shape) == 3:
                ap = ap.rearrange("p (a b c) -> p a b c", b=int(shape[1]), c=int(shape[2]))
            elif len(shape) == 4:
                ap = ap.rearrange("p (a b c d) -> p a b c d", b=int(shape[1]), c=int(shape[2]), d=int(shape[3]))
            return ap

    PERS = Region(0, 58000)
    TMP = Region(58000, ARENA - 58000)

    bank_ctr = [0]

    def nbank():
        b = bank_ctr[0] % 8
        bank_ctr[0] += 1
        return b

    def PK(b):
        return ("ps", b)

    def mm(out, lhsT, rhs, start, stop, reads, writes, tp=None):
        if tp is None:
            S.op("pe", lambda e: e.matmul(out, lhsT, rhs, start=start, stop=stop), reads, writes)
        else:
            S.op("pe", lambda e: e.matmul(out, lhsT, rhs, start=start, stop=stop, tile_position=tp), reads, writes)

    def act(out, in_, func, reads, writes, **kw):
        S.op("act", lambda e: e.activation(out, in_, func, **kw), reads, writes)

    def tt(out, in0, in1, op, reads, writes, eng="dve"):
        S.op(eng, lambda e: e.tensor_tensor(out, in0, in1, op), reads, writes)

    def ld(out, in_, reads, writes, eng="sp", **kw):
        S.dma(eng, lambda e: e.dma_start(out=out, in_=in_, **kw), reads, writes)

    def ldw(out, in_, reads, writes):
        S.dma("pool", lambda e: e.dma_start(out=out, in_=in_), reads, writes)

    def wview(name, l, r0, r1, c0, c1):
        return D[name][l, r0:r1, c0:c1].rearrange("(k p) n -> p k n", p=128)

    WMODE = {"cast_store": False}

    def ldwb(out, name, l, r0, r1, c0, c1, wkey):
        if WMODE["cast_store"]:
            ldw(out, wview(name, l, r0, r1, c0, c1), [], [wkey])
            S.dma("sp", lambda e: e.dma_start(out=D["b_" + name][l, r0:r1, c0:c1].rearrange("(k p) n -> p k n", p=128), in_=out),
                  [wkey], [("wb", name, l, r0, c0)])
            return
        S.dma("sp", lambda e: e.dma_start(out=out, in_=D["b_" + name][l, r0:r1, c0:c1].rearrange("(k p) n -> p k n", p=128)),
              [("wb", name, l, r0, c0)], [wkey])

    CONV_PIECES = [("w_in", 0, DM, 512, 768), ("w_in", 0, DM, 1024, 1536), ("w_in", 0, DM, 2560, 3584), ("w_in", 0, DM, 3584, 4608),
                   ("w_in", 0, DM, 4608, 5632), ("w_fourier", 0, 256, 0, DM), ("w_conv", 0, 256, 0, DM), ("w_attn", 0, 512, 0, DM),
                   ("w_o", 0, DM, 0, DM)] + [("mlp_w1", 0, DM, c * 1024, (c + 1) * 1024) for c in range(4)] + \
                  [("mlp_w2", r * 1024, (r + 1) * 1024, 0, DM) for r in range(4)]

    def convert_pieces(l, pieces, after_key):
        for (nm_, r0, r1, c0, c1) in pieces:
            S.dma("pool", lambda e, nm_=nm_, r0=r0, r1=r1, c0=c0, c1=c1: e.dma_start(
                out=D["b_" + nm_][l, r0:r1, c0:c1], in_=D[nm_][l, r0:r1, c0:c1]), [after_key], [("wb", nm_, l)])

    ident = PERS.alloc([128], BF16)
    ones = PERS.alloc([128], BF16)
    modc = PERS.alloc([2, 48, 2], F32)
    acol = PERS.alloc([2, 2, 4, 8], F32)
    convw = PERS.alloc([2, 6], F32)
    zflag = PERS.alloc([2], F32)
    rowmask = PERS.alloc([NBLK, 8, 8], BF16)
    tab = PERS.alloc([8, TABW], BF16)
    tabI = PERS.alloc([8, 576], BF16)
    kc_sb = PERS.alloc([4, 256], BF16)
    vc_sb = PERS.alloc([2, 512], BF16)
    ctx_sb = PERS.alloc([2, DM], F32)
    g1bc = PERS.alloc([DM], F32)
    g2bc = PERS.alloc([DM], F32)

    ld(ident, D["ident"], [], ["ident"])
    ld(ones, D["ones"], [], ["ones"])
    ld(convw, D["conv_wT"].rearrange("l p k -> p l k"), [], ["convw"])
    ld(zflag, D["zflag"], [], ["zflag"])
    ld(rowmask, D["rowmask"], [], ["rowmask"])
    ld(ctx_sb, D["ctx"].rearrange("(t p) d -> p t d", p=128), [], ["ctx_sb"])

    def phase0(layers):
        TMP.reset()
        cT = TMP.alloc([16], F32)
        sT = TMP.alloc([16], F32)
        sl = TMP.alloc([16, 128], BF16)
        sb = TMP.alloc([16], BF16)
        abT = TMP.alloc([2, 48], F32)
        ngT = TMP.alloc([2, 2, 8], F32)
        abb = TMP.alloc([2, DM], F32)
        W = [TMP.alloc([8, 1024], BF16) for _ in range(6)]
        gt = [TMP.alloc([512], F32) for _ in range(2)]
        ld(cT, D["cvecT"], [], ["cT"])
        ld(abT, D["ada_bT"].rearrange("l p c -> p l c"), [], ["abT"])
        ld(ngT[:, :, 0, :], D["n1gT"].rearrange("l p c -> p l c"), [], ["ngT0"])
        ld(ngT[:, :, 1, :], D["n2gT"].rearrange("l p c -> p l c"), [], ["ngT1"])
        act(sT, cT, AF.Silu, ["cT"], ["sT"])
        S.op("dve", lambda e: e.tensor_copy(sb, sT), ["sT"], ["sb"])
        S.op("dve", lambda e: e.tensor_copy(sl, sT.unsqueeze(2).to_broadcast([128, 16, 128])), ["sT"], ["sl"])
        for l in layers:
            for s in range(6):
                ldw(W[s], wview("ada_w", l, 0, DM, s * 1024, (s + 1) * 1024), [], [("p0w", s)])
            for gi, c0 in enumerate((2048, 5120)):
                ld(abb[:, gi], D["ada_b"][l:l + 1, c0:c0 + 1024].partition_broadcast(128), [("gbcw", l, gi)], [("abb", gi)])
            for s in range(6):
                w = W[s]
                wk = ("p0w", s)
                b = nbank()
                for fc in range(8):
                    for k in range(8):
                        mm(ps[b][:, 2 * fc:2 * fc + 2], w[:, k, fc * 128:(fc + 1) * 128], sb[:, 2 * k:2 * k + 2],
                           k == 0, k == 7, [wk, "sb"], [PK(b)])
                tt(modc[:, l, s * 8:(s + 1) * 8, :], ps[b][:, 0:16].rearrange("p (c t) -> p c t", t=2),
                   abT[:, l, s * 8:(s + 1) * 8].unsqueeze(2).to_broadcast([128, 8, 2]), ALU.add,
                   [PK(b), "abT"], [("modc", l)])
                if s in (2, 5):
                    gi = 0 if s == 2 else 1
                    for t in range(2):
                        for hh in range(2):
                            b = nbank()
                            for k in range(8):
                                mm(ps[b][:, :], sl[:, 2 * k + t, :], w[:, k, hh * 512:(hh + 1) * 512],
                                   k == 0, k == 7, [wk, "sl"], [PK(b)])
                            g = gt[(t * 2 + hh) % 2]
                            gk = ("gt", (t * 2 + hh) % 2)
                            tt(g, ps[b][:, :], abb[:, gi, hh * 512:(hh + 1) * 512], ALU.add, [PK(b), ("abb", gi)], [gk])
                            ld(D["gbc"][l, t, gi, :, hh * 512:(hh + 1) * 512], g, [gk], [("gbcw", l, gi), ("gbc", l, t)])
            for t in range(2):
                for (vi, sc_c, ng_i) in ((0, 8, 0), (2, 32, 1)):
                    S.op("dve", lambda e, l=l, t=t, vi=vi, sc_c=sc_c, ng_i=ng_i: e.scalar_tensor_tensor(
                        acol[:, l, t, vi, :], modc[:, l, sc_c:sc_c + 8, t], 1.0, ngT[:, l, ng_i, :], ALU.add, ALU.mult),
                        [("modc", l), "ngT0", "ngT1"], [("acol", l)])
                for (vi, sh_c) in ((1, 0), (3, 24)):
                    S.op("dve", lambda e, l=l, t=t, vi=vi, sh_c=sh_c: e.tensor_copy(
                        acol[:, l, t, vi, :], modc[:, l, sh_c:sh_c + 8, t]), [("modc", l)], [("acol", l)])
        S.barrier()

    phase0([0])

    def norm_transpose(xt, ntile, A, B, hT, keys_x, key_h, scr, sfx=0):
        junk, ss, xn = scr
        for t in range(ntile):
            act(xn[:, t, :], xt[:, t, :], AF.Square, keys_x, [("xn", sfx, t), ("ss", sfx, t)], accum_out=ss[:, t:t + 1])
            S.op("dve", lambda e, t=t: e.tensor_scalar(ss[:, t:t + 1], ss[:, t:t + 1], 1.0 / DM, EPS, ALU.mult, ALU.add),
                 [("ss", sfx, t)], [("ss", sfx, t)])
            act(ss[:, t:t + 1], ss[:, t:t + 1], AF.Sqrt, [("ss", sfx, t)], [("ss", sfx, t)])
            S.op("dve", lambda e, t=t: e.reciprocal(ss[:, t:t + 1], ss[:, t:t + 1]), [("ss", sfx, t)], [("ss", sfx, t)])
            act(xn[:, t, :], xt[:, t, :], AF.Copy, keys_x + [("ss", sfx, t)], [("xn", sfx, t)], scale=ss[:, t:t + 1])
        for k in range(8):
            b = nbank()
            pb = ps[b].bitcast(BF16)
            for t in range(ntile):
                S.op("pe", lambda e, t=t, k=k, pb=pb: e.transpose(pb[:, t * 128:(t + 1) * 128], xn[:, t, k * 128:(k + 1) * 128], ident),
                     [("xn", sfx, t), "ident"], [PK(b)])
            act(hT[:, k, 0:ntile * 128], pb[:, 0:ntile * 128], AF.Identity, [PK(b)], [key_h],
                scale=A[:, k:k + 1], bias=B[:, k:k + 1])

    def proj_fm(hT, T, W, wkey, c0, nchunk, sink, hkey):
        for c in range(nchunk):
            b = nbank()
            for k in range(8):
                mm(ps[b][:, 0:T], W[:, k, c0 + c * 128:c0 + (c + 1) * 128], hT[:, k, 0:T], k == 0, k == 7,
                   [wkey, hkey], [PK(b)])
            sink(c, b)

    def alloc_scr():
        return (TMP.alloc([DM], BF16), TMP.alloc([8], F32), TMP.alloc([4, DM], BF16))

    class NS:
        pass

    U1K = ["kwin", "vwin", "zwin", "ywin", ("rden", 0), ("rden", 1)] + [("cacc", c) for c in range(2)] + \
          [("E", n) for n in range(5)] + [("P", n) for n in range(5)] + [("sg", n) for n in range(2)] + [("mtmp", n) for n in range(2)]

    def alloc_mixer():
        M = NS()
        M.WS = [TMP.alloc([4096], BF16) for _ in range(4)]
        M.ws_ctr = 0
        u0 = TMP.cur
        M.kwin = TMP.alloc([4, 1024], BF16)
        M.vwin = TMP.alloc([8, 512], BF16)
        M.Et = [TMP.alloc([512], BF16) for _ in range(5)]
        M.Pt = [TMP.alloc([512], BF16) for _ in range(5)]
        M.sg = [TMP.alloc([512], F32) for _ in range(2)]
        M.mtmp = [TMP.alloc([512], F32) for _ in range(2)]
        M.cacc = TMP.alloc([2, 512], F32)
        M.zwin = TMP.alloc([2, 514], BF16)
        M.ywin = TMP.alloc([2, 512], BF16)
        M.rden = TMP.alloc([512], F32)
        u1 = TMP.cur
        TMP.cur = u0
        M.h1 = TMP.alloc([32, 512], BF16)
        TMP.cur = max(TMP.cur, u1)
        M.atT = TMP.alloc([4, 512], BF16)
        M.cvT = TMP.alloc([2, 512], BF16)
        M.q_sb = TMP.alloc([4, 2, 512], BF16)
        S.op("dve", lambda e, q=M.q_sb: e.memset(q, 0.0), [], ["q_sb"])
        M.mT = TMP.alloc([8, 512], BF16)
        M.rr = [TMP.alloc([512], BF16) for _ in range(2)]
        M.otmp = [TMP.alloc([512], F32) for _ in range(2)]
        M.ep_ctr = 0
        M.sg_ctr = 0
        M.h1_free = set()
        return M

    def layer(l):
        last = (l == 1)
        xin = D["x"] if l == 0 else D["x1"]
        xout = D["x1"] if l == 0 else D["out"]
        xkin = "xin0" if l == 0 else "x1"
        xkout = "x1" if l == 0 else "outk"
        A1 = acol[:, l, 0, 0, :]; B1 = acol[:, l, 0, 1, :]; A2 = acol[:, l, 0, 2, :]; B2 = acol[:, l, 0, 3, :]
        cA1 = acol[:, l, 1, 0, :]; cB1 = acol[:, l, 1, 1, :]; cA2 = acol[:, l, 1, 2, :]; cB2 = acol[:, l, 1, 3, :]

        def load_wk():
            wk1 = TMP.alloc([8, 768], BF16)
            wk2 = TMP.alloc([8, 1024], BF16)
            ldw(wk1[:, :, 0:512], wview("w_in", l, 0, DM, 0, 512), [], ["wk1"])
            ldw(wk1[:, :, 512:768], wview("w_in", l, 0, DM, 768, 1024), [], ["wk1"])
            ldw(wk2, wview("w_in", l, 0, DM, 1536, 2560), [], ["wk2"])
            return wk1, wk2

        TMP.reset()
        maskF = TMP.alloc([TABW], BF16)
        ld(maskF, D["maskF"], [], ["maskF"])
        maskI = TMP.alloc([576], BF16)
        ld(maskI, D["maskI"], [], ["maskI"])
        rb = [TMP.alloc([TABW], F32) for _ in range(2)]
        eb = [TMP.alloc([TABW], BF16) for _ in range(2)]
        for h in range(8):
            ld(rb[h % 2], D["rbt"][l, h], [], [("rb", h % 2)])
            act(eb[h % 2], rb[h % 2], AF.Exp, [("rb", h % 2)], [("eb", h % 2)])
            tt(tab[:, h, :], eb[h % 2], maskF, ALU.mult, [("eb", h % 2), "maskF"], ["tab"])
            tt(tabI[:, h, :], eb[h % 2][:, 7 * 64:16 * 64], maskI, ALU.mult, [("eb", h % 2), "maskI"], ["tab"])
        S.barrier()

        def slab(M):
            n = M.ws_ctr % 4
            M.ws_ctr += 1
            return M.WS[n], ("ws", n)

        def mixer_block(M, scr, T, hT, hkey, xres, xkey, yT, ykey, zw, zkey, keychunks, Am, Bm, pre_wo=None):
            nt = T // 128
            cacc, q_sb, atT, cvT, mT, h1 = M.cacc, M.q_sb, M.atT, M.cvT, M.mT, M.h1
            for c in range(2):
                S.op("dve", lambda e, c=c: e.tensor_scalar(cacc[:, c, 0:T], zw[:, c, 0:T], convw[:, l, 3 * c:3 * c + 1], None, ALU.mult),
                     [zkey, "convw"], [("cacc", c)])
                for kk in (1, 2):
                    S.op("dve", lambda e, c=c, kk=kk: e.scalar_tensor_tensor(cacc[:, c, 0:T], zw[:, c, kk:kk + T],
                                                                            convw[:, l, 3 * c + kk:3 * c + kk + 1], cacc[:, c, 0:T], ALU.mult, ALU.add),
                         [zkey, "convw", ("cacc", c)], [("cacc", c)])
            w, wk = slab(M)
            wv = w.rearrange("p (k n) -> p k n", k=8)
            ldwb(wv[:, :, 0:256], "w_in", l, 0, DM, 512, 768, wk)

            def sink_cb(c, b):
                tt(cvT[:, c, 0:T], ps[b][:, 0:T], cacc[:, c, 0:T], ALU.mult, [PK(b), ("cacc", c)], ["cvT"])
            proj_fm(hT, T, wv, wk, 0, 2, sink_cb, hkey)
            w, wk = slab(M)
            wv = w.rearrange("p (k n) -> p k n", k=8)
            ldwb(wv, "w_in", l, 0, DM, 1024, 1536, wk)

            def sink_q(c, b):
                act(q_sb[0:64, c, 0, 0:T], ps[b][0:64, 0:T], AF.Copy, [PK(b)], ["q_sb"])
                act(q_sb[64:128, c, 1, 0:T], ps[b][64:128, 0:T], AF.Copy, [PK(b)], ["q_sb"])
            proj_fm(hT, T, wv, wk, 0, 4, sink_q, hkey)
            nkc = len(keychunks)
            sctr = 0
            for hp in range(4):
                OB, DB = (3, 5), (4, 6)
                order = [nkc - 2] + list(range(nkc - 2)) + [nkc - 1]
                items = [(hh, ci) for hh in range(2) for ci in order]
                sbanks = {}

                def crange(ci):
                    lo, hi = keychunks[ci][6], keychunks[ci][7]
                    return (0, T) if lo is None else (lo * 64, (hi + 1) * 64)

                def emit_S(it):
                    nonlocal sctr
                    hh, ci = it
                    kT_ap, kkey = keychunks[ci][0], keychunks[ci][4]
                    c0, c1 = crange(ci)
                    b = (0, 1, 2, 7)[sctr % 4]
                    sctr += 1
                    sbanks[it] = b
                    mm(ps[b][:, c0:c1], kT_ap(hp), q_sb[:, hp, hh, c0:c1], True, True, [kkey, "q_sb"], [PK(b)])

                def emit_PV(it):
                    hh, ci = it
                    _, v_ap, tspec, rmask, kkey, vkey, lo, hi = keychunks[ci]
                    c0, c1 = crange(ci)
                    b = sbanks[it]
                    n = M.ep_ctr % 5
                    M.ep_ctr += 1
                    E, P = M.Et[n], M.Pt[n]
                    act(E[:, c0:c1], ps[b][:, c0:c1], AF.Exp, [PK(b)], [("E", n)], scale=0.125)
                    src, skey = E, ("E", n)
                    h = 2 * hp + hh
                    if tspec is not None:
                        kind, e0 = tspec
                        if kind == "int":
                            tt(P[:, c0:c1], E[:, c0:c1], tabI[:, h, e0 * 64:e0 * 64 + (c1 - c0)], ALU.mult, [("E", n), "tab"], [("P", n)])
                        else:
                            tt(P[:, 0:T], E[:, 0:T], tab[:, h, e0 * 64:e0 * 64 + T], ALU.mult, [("E", n), "tab"], [("P", n)])
                            Pv = P[:, 0:T].rearrange("p (a b) -> p a b", b=64)
                            tt(Pv, Pv, rmask.unsqueeze(2).to_broadcast([128, 8, 64]), ALU.mult, [("P", n), "rowmask"], [("P", n)], eng="pool")
                        src, skey = P, ("P", n)
                    first, lastc = (ci == order[0]), (ci == order[-1])
                    mm(ps[OB[hh]][:, c0:c1], v_ap(hp), src[:, c0:c1], first, lastc, [skey, vkey], [PK(OB[hh])])
                    mm(ps[DB[hh]][:, c0:c1], ones, src[:, c0:c1], first, lastc, [skey, "ones"], [PK(DB[hh])])

                LA = 3
                for n_, it in enumerate(items):
                    emit_S(it)
                    if n_ >= LA:
                        emit_PV(items[n_ - LA])
                for it in items[len(items) - LA:]:
                    emit_PV(it)
                for hh in range(2):
                    pr = slice(64 * hh, 64 * hh + 64)
                    S.op("dve", lambda e, hh=hh, pr=pr: e.reciprocal(M.rden[pr, 0:T], ps[DB[hh]][pr, 0:T]), [PK(DB[hh])], [("rden", hh)])
                    tt(atT[pr, hp, 0:T], ps[OB[hh]][pr, 0:T], M.rden[pr, 0:T], ALU.mult, [PK(OB[hh]), ("rden", hh)], ["atT"])
            branches = ((2560, "w_fourier", 256, yT, ykey, 2), (3584, "w_conv", 256, cvT, "cvT", 2), (4608, "w_attn", 512, atT, "atT", 4))
            for bi, (gc0, wname, wrows, src, skey, nk) in enumerate(branches):
                wb, wbk = slab(M)
                wbv = wb.rearrange("p (k n) -> p k n", n=1024)
                ldwb(wbv[:, 0:nk, :], wname, l, 0, wrows, 0, DM, wbk)
                for half in range(2):
                    wg, wgk = slab(M)
                    wgv = wg.rearrange("p (k n) -> p k n", k=8)
                    ldwb(wgv, "w_in", l, 0, DM, gc0 + half * 512, gc0 + (half + 1) * 512, wgk)
                    for o4 in range(4):
                        oc = half * 4 + o4
                        bg = nbank()
                        for k in range(8):
                            mm(ps[bg][:, 0:T], wgv[:, k, o4 * 128:(o4 + 1) * 128], hT[:, k, 0:T], k == 0, k == 7, [wgk, hkey], [PK(bg)])
                        n = M.sg_ctr % 2
                        M.sg_ctr += 1
                        act(M.sg[n][:, 0:T], ps[bg][:, 0:T], AF.Sigmoid, [PK(bg)], [("sg", n)])
                        bp = nbank()
                        for k in range(nk):
                            mm(ps[bp][:, 0:T], wbv[:, k, oc * 128:(oc + 1) * 128], src[:, k, 0:T], k == 0, k == nk - 1,
                               [wbk, skey], [PK(bp)])
                        if bi == 0:
                            tt(mT[:, oc, 0:T], ps[bp][:, 0:T], M.sg[n][:, 0:T], ALU.mult, [PK(bp), ("sg", n)], [("mT", oc)])
                        else:
                            tt(M.mtmp[n][:, 0:T], ps[bp][:, 0:T], M.sg[n][:, 0:T], ALU.mult, [PK(bp), ("sg", n)], [("mtmp", n)])
                            tt(mT[:, oc, 0:T], mT[:, oc, 0:T], M.mtmp[n][:, 0:T], ALU.add, [("mT", oc), ("mtmp", n)], [("mT", oc)])
            mTk = [("mT", oc) for oc in range(8)]
            if getattr(M, "dbg", False) and T == 512:
                ld(D["dbg_at"], atT, ["atT"], [])
                ld(D["dbg_cv"], cvT, ["cvT"], [])
                ld(D["dbg_m"], mT, mTk, [])
            if pre_wo is not None:
                pre_wo()
            for hh in range(2):
                wo, wok = slab(M)
                wov = wo.rearrange("p (k n) -> p k n", k=8)
                ldwb(wov, "w_o", l, 0, DM, hh * 512, (hh + 1) * 512, wok)
                for t in range(nt):
                    b = nbank()
                    for k in range(8):
                        mm(ps[b][:, :], mT[:, k, t * 128:(t + 1) * 128], wov[:, k, :], k == 0, k == 7, mTk + [wok], [PK(b)])
                    n = M.sg_ctr % 2
                    M.sg_ctr += 1
                    tt(M.otmp[n], ps[b][:, :], g1bc[:, hh * 512:(hh + 1) * 512], ALU.mult, [PK(b), "g1bc"], [("otmp", n)])
                    tt(xres[:, t, hh * 512:(hh + 1) * 512], xres[:, t, hh * 512:(hh + 1) * 512], M.otmp[n], ALU.add,
                       [xkey, ("otmp", n)], [xkey])
            if getattr(M, "dbg", False) and T == 512:
                ld(D["dbg_xmix"], xres, [xkey], [])
            norm_transpose(xres, nt, Am, Bm, hT, [xkey], hkey, scr)
            first = True
            for s in range(8):
                w1, w1k = slab(M)
                w1v = w1.rearrange("p (k n) -> p k n", k=8)
                ldwb(w1v, "mlp_w1", l, 0, DM, s * 512, (s + 1) * 512, w1k)
                for c in range(4):
                    b = nbank()
                    for k in range(8):
                        mm(ps[b][:, 0:T], w1v[:, k, c * 128:(c + 1) * 128], hT[:, k, 0:T], k == 0, k == 7, [w1k, hkey], [PK(b)])
                    n = M.sg_ctr % 2
                    M.sg_ctr += 1
                    act(M.rr[n][:, 0:T], ps[b][:, 0:T], AF.Relu, [PK(b)], [("rr", n)])
                    tt(h1[:, s * 4 + c, 0:T], M.rr[n][:, 0:T], M.rr[n][:, 0:T], ALU.mult, [("rr", n)],
                       ["h1"] + (U1K if first else []))
                    first = False
            if getattr(M, "dbg", False) and T == 512:
                ld(D["dbg_h2"], hT, [hkey], [])
                ld(D["dbg_h1"], h1, ["h1"], [])
                M.dbg = False
            obanks = [(t, hh, nbank()) for t in range(nt) for hh in range(2)]
            lasttok = None
            for s in range(8):
                w2, w2k = slab(M)
                w2v = w2.rearrange("p (k n) -> p k n", k=4)
                ldwb(w2v, "mlp_w2", l, s * 512, (s + 1) * 512, 0, DM, w2k)
                for (t, hh, b) in obanks:
                    for k in range(4):
                        mm(ps[b][:, :], h1[:, s * 4 + k, t * 128:(t + 1) * 128], w2v[:, k, hh * 512:(hh + 1) * 512],
                           s == 0 and k == 0, s == 7 and k == 3, ["h1", w2k], [PK(b)])
            lasttok = ("c", "pe", len(S.ops["pe"]) - 1)
            for k_ in U1K:
                S.extra.setdefault(k_, set()).add(lasttok)
            for (t, hh, b) in obanks:
                n = M.sg_ctr % 2
                M.sg_ctr += 1
                tt(M.otmp[n], ps[b][:, :], g2bc[:, hh * 512:(hh + 1) * 512], ALU.mult, [PK(b), "g2bc"], [("otmp", n)])
                tt(xres[:, t, hh * 512:(hh + 1) * 512], xres[:, t, hh * 512:(hh + 1) * 512], M.otmp[n], ALU.add,
                   [xkey, ("otmp", n)], [xkey])

        def ctx_keychunks():
            kch = []
            for ci in range(2):
                kch.append((lambda hp, ci=ci: kc_sb[:, hp, ci * 128:(ci + 1) * 128],
                            lambda hp, ci=ci: vc_sb[:, ci, hp * 128:(hp + 1) * 128], None, None, "kc_sb", "vc_sb", None, None))
            return kch

        TMP.reset()
        scr2 = [alloc_scr() for _ in range(2)]
        wk1, wk2 = load_wk()
        xt2b = [TMP.alloc([4, DM], F32) for _ in range(2)]
        hx2b = [TMP.alloc([8, 512], BF16) for _ in range(2)]
        cust = [TMP.alloc([512], F32) for _ in range(2)]
        kst = [TMP.alloc([512], BF16) for _ in range(8)]
        p1ctr = [0]
        XU = D["xiu"][0:TOK, :].rearrange("(r x) n -> r (x n)", x=64).rearrange("r (h c) -> r h c", c=64)

        def stg():
            n = p1ctr[0] % 8
            p1ctr[0] += 1
            return kst[n], ("kst", n)

        for i in range(NBLK):
            xt, hxT, scr = xt2b[i % 2], hx2b[i % 2], scr2[i % 2]
            xtk, hxk = ("xt", i % 2), ("hxT1", i % 2)
            ld(xt, xin[i * 512:(i + 1) * 512, :].rearrange("(t p) d -> p t d", p=128), [xkin], [xtk])
            norm_transpose(xt, 4, A1, B1, hxT, [xtk], hxk, scr, sfx=i % 2)
            ld(D["hx_s"][i], hxT, [hxk], [("hx_s", i)], eng="pool")
            def sink_u(c, b, i=i):
                st_, sk_ = stg()
                S.op("dve", lambda e, st_=st_, b=b: e.tensor_copy(st_, ps[b][:, :]), [PK(b)], [sk_])
                ld(XU[8 * i:8 * i + 8, c * 128:(c + 1) * 128, :].rearrange("r h c -> h r c"),
                   st_.rearrange("p (r c) -> p r c", c=64), [sk_], ["xiu"], eng="pool")
            proj_fm(hxT, 512, wk1, "wk1", 0, 2, sink_u, hxk)

            def sink_z(c, b, i=i):
                if c < 2:
                    act(cust[c], ps[b][:, :], AF.Copy, [PK(b)], [("cust", c)])
                else:
                    st_, sk_ = stg()
                    tt(st_, ps[b][:, :], cust[c - 2], ALU.mult, [PK(b), ("cust", c - 2)], [sk_])
                    ld(D["z_s"][:, c - 2, 16 + i * 512:16 + (i + 1) * 512], st_, [sk_], ["z_s"], eng="pool")
            proj_fm(hxT, 512, wk1, "wk1", 256, 4, sink_z, hxk)

            def sink_k(c, b, i=i):
                st_, sk_ = stg()
                S.op("dve", lambda e, st_=st_, b=b: e.tensor_copy(st_, ps[b][:, :]), [PK(b)], [sk_])
                ld(D["k_s"][:, c, (4 + 8 * i) * 64:(4 + 8 * i) * 64 + 512], st_, [sk_], ["k_s"], eng="pool")
            proj_fm(hxT, 512, wk2, "wk2", 0, 4, sink_k, hxk)
            for t in range(4):
                b = nbank()
                for k in range(8):
                    mm(ps[b][:, :], hxT[:, k, t * 128:(t + 1) * 128], wk2[:, k, 512:1024], k == 0, k == 7, ["wk2", hxk], [PK(b)])
                st_, sk_ = stg()
                S.op("dve", lambda e, st_=st_, b=b: e.tensor_copy(st_, ps[b][:, :]), [PK(b)], [sk_])
                ld(D["v_s"][2 + 4 * i + t], st_, [sk_], ["v_s"], eng="pool")

        XI, XO = D["exch_in"], D["exch_out"]
        ld(XI[0:512, :].rearrange("(p c) n -> p c n", c=4), D["k_s"][:, :, 256:512], ["k_s"], ["exch_in"])
        ld(XI[512:1024, :].rearrange("(p c) n -> p c n", c=4), D["k_s"][:, :, 64 * 64:68 * 64], ["k_s"], ["exch_in"])
        ld(XI[1024:1536, :].rearrange("(c p h) n -> c p (h n)", c=2, p=128), D["v_s"][2:4], ["v_s"], ["exch_in"])
        ld(XI[1536:2048, :].rearrange("(c p h) n -> c p (h n)", c=2, p=128), D["v_s"][32:34], ["v_s"], ["exch_in"])
        ld(XI[2048:2064, :].rearrange("r (pp c t) -> (r pp) c t", pp=8, c=2, t=16), D["z_s"][:, :, 16:32], ["z_s"], ["exch_in"])
        ld(XI[2064:2080, :].rearrange("r (pp c t) -> (r pp) c t", pp=8, c=2, t=16), D["z_s"][:, :, TOK:TOK + 16], ["z_s"], ["exch_in"])
        RG = [[2 * g, 2 * g + 1] for g in range(ncores // 2)]
        S.op("pool", lambda e: e.collective_compute("AllGather", ALU.bypass, replica_groups=RG, ins=[D["xiu"]], outs=[D["xou"]]),
             ["xiu"], ["xou"], force_sig=True)
        S.op("pool", lambda e: e.collective_compute("AllGather", ALU.bypass, replica_groups=RG, ins=[XI], outs=[XO]),
             ["exch_in"], ["exch_out"], force_sig=True)
        S.barrier(skip_cc=True)
        if l == 0:
            phase0([1])
        TMP.reset()
        scr = alloc_scr()
        hcT = TMP.alloc([8, 256], BF16)
        if not last:
            ucT = TMP.alloc([2, 256], BF16)
            zc = TMP.alloc([2, 258], BF16)
            cuc = TMP.alloc([2, 256], F32)
        markA = TMP.cur
        wk1, wk2 = load_wk()
        norm_transpose(ctx_sb, 2, cA1, cB1, hcT, ["ctx_sb"], "hcT", scr)

        def sink_kc(c, b):
            act(kc_sb[:, c, :], ps[b][:, 0:256], AF.Copy, [PK(b)], ["kc_sb"])
        proj_fm(hcT, 256, wk2, "wk2", 0, 4, sink_kc, "hcT")
        for t in range(2):
            b = nbank()
            for k in range(8):
                mm(ps[b][:, :], hcT[:, k, t * 128:(t + 1) * 128], wk2[:, k, 512:1024], k == 0, k == 7, ["wk2", "hcT"], [PK(b)])
            act(vc_sb[:, t, :], ps[b][:, :], AF.Copy, [PK(b)], ["vc_sb"])
        if not last:
            S.op("dve", lambda e: e.memset(zc, 0.0), [], ["zc"])

            def sink_c1(c, b):
                if c < 2:
                    act(ucT[:, c, :], ps[b][:, 0:256], AF.Copy, [PK(b)], ["ucT"])
                elif c < 4:
                    act(cuc[:, c - 2, :], ps[b][:, 0:256], AF.Copy, [PK(b)], [("cuc", c - 2)])
                else:
                    tt(zc[:, c - 4, 1:257], ps[b][:, 0:256], cuc[:, c - 4, :], ALU.mult, [PK(b), ("cuc", c - 4)], ["zc"])
            proj_fm(hcT, 256, wk1, "wk1", 0, 6, sink_c1, "hcT")
        S.barrier()
        if not last:
            TMP.cur = markA
            M = alloc_mixer()
            ld(g1bc, D["gbc"][l, 1, 0], [("gbc", l, 1)], ["g1bc"])
            ld(g2bc, D["gbc"][l, 1, 1], [("gbc", l, 1)], ["g2bc"])
            chc = TMP.alloc([2, 128], BF16)
            d256 = TMP.alloc([2, 2, 256], BF16)
            zri = TMP.alloc([2, 2, 256], BF16)
            ycT = TMP.alloc([2, 256], BF16)
            ld(chc, D["chc"], [], ["chc"])
            ld(d256, D["dft256"], [], ["d256"])
            for ri in range(2):
                for tc in range(2):
                    b = nbank()
                    for cc in range(2):
                        mm(ps[b][:, cc * 128:(cc + 1) * 128], ucT[:, cc, tc * 128:(tc + 1) * 128], chc[:, ri, :], True, True,
                           ["ucT", "chc"], [PK(b)])
                    act(zri[:, ri, tc, :], ps[b][:, 0:256], AF.Copy, [PK(b)], ["zri"])
            for cc in range(2):
                b = nbank()
                n_ = 0
                for ri in range(2):
                    for tc in range(2):
                        mm(ps[b][:, 0:256], zri[:, ri, tc, cc * 128:(cc + 1) * 128], d256[:, ri, tc, :], n_ == 0, n_ == 3,
                           ["zri", "d256"], [PK(b)])
                        n_ += 1
                act(ycT[:, cc, :], ps[b][:, 0:256], AF.Copy, [PK(b)], ["ycT"])
            WMODE["cast_store"] = True
            mixer_block(M, scr, 256, hcT, "hcT", ctx_sb, "ctx_sb", ycT, "ycT", zc, "zc", ctx_keychunks(), cA2, cB2)
            WMODE["cast_store"] = False
            S.barrier()

        ld(D["k_s"][:, :, 0:256], XO[512:1024, :].rearrange("(p c) n -> p c n", c=4), ["exch_out"], ["k_s"])
        ld(D["k_s"][:, :, 68 * 64:72 * 64], XO[RX:RX + 512, :].rearrange("(p c) n -> p c n", c=4), ["exch_out"], ["k_s"])
        ld(D["v_s"][0:2], XO[1536:2048, :].rearrange("(c p h) n -> c p (h n)", c=2, p=128), ["exch_out"], ["v_s"])
        ld(D["v_s"][34:36], XO[RX + 1024:RX + 1536, :].rearrange("(c p h) n -> c p (h n)", c=2, p=128), ["exch_out"], ["v_s"])
        ld(D["z_s"][:, :, 0:16], XO[2064:2080, :].rearrange("r (pp c t) -> (r pp) c t", pp=8, c=2, t=16), ["exch_out"], ["z_s"])
        ld(D["z_s"][:, :, TOK + 16:TOK + 32], XO[RX + 2048:RX + 2064, :].rearrange("r (pp c t) -> (r pp) c t", pp=8, c=2, t=16), ["exch_out"], ["z_s"])
        S.barrier()
        if stage <= 2:
            return

        TMP.reset()
        cs128 = TMP.alloc([256], BF16)
        ch3 = TMP.alloc([2, 128], BF16)
        gtab = TMP.alloc([128, 96], BF16)
        U = TMP.alloc([128, 64], BF16)
        Asb = TMP.alloc([64, 256], BF16)
        Xsb = TMP.alloc([2, TOK], BF16)
        yst = [TMP.alloc([512], BF16) for _ in range(2)]
        ld(cs128, D["cs128"], [], ["cs128"])
        ld(ch3, D["ch3"], [], ["ch3"])
        ld(gtab, D["gtab"], [], ["gtab"])
        Uf = U.rearrange("p h c -> p (h c)")
        Xv = Xsb.rearrange("p r (k2 k1) -> p r k2 k1", k1=128)
        for cc in range(2):
            XOU0 = D["xou"][0:TOK, :].rearrange("(r x) n -> r (x n)", x=64).rearrange("r (h c) -> r h c", c=64)
            XOU1 = D["xou"][TOK:2 * TOK, :].rearrange("(r x) n -> r (x n)", x=64).rearrange("r (h c) -> r h c", c=64)
            ld(U[0:64], XOU0[:, cc * 128:(cc + 1) * 128, :], ["xou"], ["U"])
            ld(U[64:128], XOU1[:, cc * 128:(cc + 1) * 128, :], ["xou"], ["U"])
            for qq in range(32):
                b = nbank()
                for j in range(2):
                    q = 2 * qq + j
                    mm(ps[b][:, j * 256:(j + 1) * 256], Uf[:, 128 * q:128 * q + 128], cs128, True, True, ["U", "cs128"], [PK(b)])
                src = ps[b][:, :].rearrange("p (a b) -> p a b", a=2)
                if qq % 2 == 0:
                    act(Asb[:, 2 * qq:2 * qq + 2, :], src, AF.Copy, [PK(b)], ["Asb"])
                else:
                    S.op("dve", lambda e, qq=qq, src=src: e.tensor_copy(Asb[:, 2 * qq:2 * qq + 2, :], src), [PK(b)], ["Asb"])
            for g8 in range(16):
                b = nbank()
                for kl in range(8):
                    k1 = g8 * 8 + kl
                    for j in range(2):
                        pr = slice(64 * j, 64 * j + 64)
                        mm(ps[b][pr, kl * 64:(kl + 1) * 64], Asb[pr, :, k1], gtab[pr, k1, 32:96], True, False,
                           ["Asb", "gtab"], [PK(b)], tp=(64 * j, 64 * j))
                        mm(ps[b][pr, kl * 64:(kl + 1) * 64], Asb[pr, :, 128 + k1], gtab[pr, k1, 0:64], False, True,
                           ["Asb", "gtab"], [PK(b)], tp=(64 * j, 64 * j))
                src = ps[b][:, :].rearrange("p (kl r k2) -> p r k2 kl", kl=8, r=2, k2=32)
                dst = Xv[:, :, :, g8 * 8:(g8 + 1) * 8]
                if g8 % 2 == 0:
                    act(dst, src, AF.Copy, [PK(b)], ["Xsb"])
                else:
                    S.op("dve", lambda e, dst=dst, src=src: e.tensor_copy(dst, src), [PK(b)], ["Xsb"])
            for i in range(NBLK):
                b = nbank()
                mm(ps[b][:, :], ch3[:, 0, :], Xsb[:, 0, i * 512:(i + 1) * 512], True, False, ["Xsb", "ch3"], [PK(b)])
                mm(ps[b][:, :], ch3[:, 1, :], Xsb[:, 1, i * 512:(i + 1) * 512], False, True, ["Xsb", "ch3"], [PK(b)])
                y = yst[i % 2]
                act(y, ps[b][:, :], AF.Copy, [PK(b)], [("yst", i % 2)])
                ld(D["y_s"][:, cc, i * 512:(i + 1) * 512], y, [("yst", i % 2)], ["y_s"], eng="pool")
        S.barrier()
        if stage <= 3:
            return

        TMP.reset()
        scr = alloc_scr()
        junk, ss, xn = scr
        M = alloc_mixer()
        M.dbg = ("mix" in debug) and l == dbg_layer
        hx2 = [TMP.alloc([8, 512], BF16) for _ in range(2)]
        xt2 = TMP.alloc([4, DM], F32)
        fgb = TMP.alloc([DM], F32)
        ld(g1bc, D["gbc"][l, 0, 0], [("gbc", l, 0)], ["g1bc"])
        ld(g2bc, D["gbc"][l, 0, 1], [("gbc", l, 0)], ["g2bc"])
        if last:
            ld(fgb, D["final_g"].partition_broadcast(128), [], ["fgb"])
        nblk2 = NBLK if stage > 4 else 1
        ld(hx2[0], D["hx_s"][0], [("hx_s", 0)], [("hxT", 0)])
        for i in range(nblk2):
            hxT, hkey_ = hx2[i % 2], ("hxT", i % 2)
            if i + 1 < nblk2:
                ld(hx2[(i + 1) % 2], D["hx_s"][i + 1], [("hx_s", i + 1)], [("hxT", (i + 1) % 2)])
            ld(M.zwin, D["z_s"][:, :, 15 + i * 512:15 + i * 512 + 514], ["z_s"], ["zwin"])
            ld(M.ywin, D["y_s"][:, :, i * 512:(i + 1) * 512], ["y_s"], ["ywin"])
            ld(M.kwin, D["k_s"][:, :, 8 * i * 64:(8 * i + 16) * 64], ["k_s"], ["kwin"])
            ld(M.vwin, D["v_s"][4 * i:4 * i + 8].rearrange("c p n -> p c n"), ["v_s"], ["vwin"])

            def pre_wo(i=i):
                ld(xt2, xin[i * 512:(i + 1) * 512, :].rearrange("(t p) d -> p t d", p=128), [xkin], ["xt2"])
            if i == 0:
                S.op("dve", lambda e: e.tensor_scalar(M.zwin[:, :, 0:1], M.zwin[:, :, 0:1], zflag[:, 0:1], None, ALU.mult),
                     ["zwin", "zflag"], ["zwin"])
            if i == NBLK - 1:
                S.op("dve", lambda e: e.tensor_scalar(M.zwin[:, :, 513:514], M.zwin[:, :, 513:514], zflag[:, 1:2], None, ALU.mult),
                     ["zwin", "zflag"], ["zwin"])
            kch = []
            RNG = [(0, 1), (0, 3), (0, 5), (0, 7), (1, 7), (3, 7), (5, 7), (7, 7)]
            for j in range(8):
                if 0 < i < NBLK - 1:
                    lo, hi = RNG[j]
                    kch.append((lambda hp, j=j: M.kwin[:, hp, j * 128:(j + 1) * 128],
                                lambda hp, j=j: M.vwin[:, j, hp * 128:(hp + 1) * 128], ("int", 7 - 2 * j + lo), None, "kwin", "vwin", lo, hi))
                else:
                    kch.append((lambda hp, j=j: M.kwin[:, hp, j * 128:(j + 1) * 128],
                                lambda hp, j=j: M.vwin[:, j, hp * 128:(hp + 1) * 128], ("full", 14 - 2 * j), rowmask[:, i, j, :], "kwin", "vwin", None, None))
            kch += ctx_keychunks()
            WMODE["cast_store"] = (last and i == 0)
            mixer_block(M, scr, 512, hxT, hkey_, xt2, "xt2", M.ywin, "ywin", M.zwin, "zwin", kch, A2, B2, pre_wo=pre_wo)
            WMODE["cast_store"] = False
            if last:
                for t in range(4):
                    act(xn[:, t, :], xt2[:, t, :], AF.Square, ["xt2"], [("xn", 0, t), ("ss", 0, t)], accum_out=ss[:, t:t + 1])
                    S.op("dve", lambda e, t=t: e.tensor_scalar(ss[:, t:t + 1], ss[:, t:t + 1], 1.0 / DM, EPS, ALU.mult, ALU.add),
                         [("ss", 0, t)], [("ss", 0, t)])
                    act(ss[:, t:t + 1], ss[:, t:t + 1], AF.Sqrt, [("ss", 0, t)], [("ss", 0, t)])
                    S.op("dve", lambda e, t=t: e.reciprocal(ss[:, t:t + 1], ss[:, t:t + 1]), [("ss", 0, t)], [("ss", 0, t)])
                    S.op("dve", lambda e, t=t: e.scalar_tensor_tensor(xt2[:, t, :], xt2[:, t, :], ss[:, t:t + 1], fgb, ALU.mult, ALU.mult),
                         ["xt2", ("ss", 0, t), "fgb"], ["xt2"])
            ld(xout[i * 512:(i + 1) * 512, :].rearrange("(t p) d -> p t d", p=128), xt2, ["xt2"], [xkout, ("blkdone", l, i)], eng="pool")
        S.barrier()

    for l in range(2):
        layer(l)
        if stage <= 5:
            break

    S.barrier()
    DUMPS = {"kc_sb": ([128, 4, 256], BF16, kc_sb), "vc_sb": ([128, 2, 512], BF16, vc_sb), "ctx_sb": ([128, 2, DM], F32, ctx_sb),
             "acol": ([128, 2, 2, 4, 8], F32, acol), "modc": ([128, 2, 48, 2], F32, modc), "tab": ([128, 8, TABW], BF16, tab)}
    for nm in debug:
        if nm == "mix":
            continue
        if nm in DUMPS:
            shp, dt, src = DUMPS[nm]
        else:
            src = D[nm]
            shp, dt = list(src.shape), src.dtype
        dout("dbg_" + nm, shp, dt)
        ld(D["dbg_" + nm], src, [], [])
    S.wait_all("sp")

    nsig = S.plan()
    import contextlib
    with contextlib.ExitStack() as st:
        esems = {e: [st.enter_context(nc.semaphore(f"s_{e}_{n}")) for n in range(nsig[e] // EPOCH + 1)] for e in Sched.ENGS}
        dsems = {q: [st.enter_context(nc.semaphore(f"d_{q}_{n}")) for n in range(NSLOT)] for q in ("sp", "pool")}
        block = st.enter_context(nc.Block())

        @block.tensor
        def _(h):
            S.emit("pe", h, esems, dsems)

        @block.scalar
        def _(h):
            S.emit("act", h, esems, dsems)

        @block.vector
        def _(h):
            S.emit("dve", h, esems, dsems)

        @block.gpsimd
        def _(h):
            S.emit("pool", h, esems, dsems)

        @block.sync
        def _(h):
            S.emit("sp", h, esems, dsems)
    return nc


def make_in_maps(inp):
    f32 = lambda a: np.ascontiguousarray(np.asarray(a, dtype=np.float32))
    HC = host_constants()
    CC = [core_constants(h) for h in range(2)]
    shared = {
        "ada_w": f32(inp["ada_w"]), "ada_b": f32(inp["ada_b"]),
        "ada_bT": f32(np.asarray(inp["ada_b"]).reshape(2, 48, 128).transpose(0, 2, 1)),
        "n1gT": f32(np.asarray(inp["norm1_g"]).reshape(2, 8, 128).transpose(0, 2, 1)),
        "n2gT": f32(np.asarray(inp["norm2_g"]).reshape(2, 8, 128).transpose(0, 2, 1)),
        "final_g": f32(np.asarray(inp["final_g"]).reshape(1, DM)),
        "w_in": f32(inp["w_in"]),
        "conv_wT": f32(np.asarray(inp["conv_w"]).reshape(2, 3, 2, 128).transpose(0, 3, 2, 1).reshape(2, 128, 6)),
        "rbt": gather_relbias(np.asarray(inp["rel_bias"], dtype=np.float32)),
        "w_fourier": f32(inp["w_fourier"]), "w_conv": f32(inp["w_conv"]), "w_attn": f32(inp["w_attn"]),
        "w_o": f32(inp["w_o"]), "mlp_w1": f32(inp["mlp_w1"]), "mlp_w2": f32(inp["mlp_w2"]),
    }
    shared.update(HC)
    maps = []
    x = np.asarray(inp["x"]); c = np.asarray(inp["c"]); ctx = np.asarray(inp["ctx"]); c_ctx = np.asarray(inp["c_ctx"])
    for core in range(NCORES):
        b, half = core // 2, core % 2
        m = dict(shared)
        m.update(CC[half])
        m["x"] = f32(x[b, half * TOK:(half + 1) * TOK])
        m["ctx"] = f32(ctx[b])
        cv = np.stack([c[b], c_ctx], axis=1).reshape(8, 128, 2).transpose(1, 0, 2).reshape(128, 16)
        m["cvecT"] = f32(cv)
        maps.append(m)
    return maps


_NC_CACHE = {}


def kernel(**inputs):
    if "nc" not in _NC_CACHE:
        _NC_CACHE["nc"] = build_nc()
    nc = _NC_CACHE["nc"]
    maps = make_in_maps(inputs)
    res = run_bass_kernel_spmd(nc, maps, core_ids=list(range(NCORES)))
    out = np.empty((4, 8192, DM), np.float32)
    for core in range(NCORES):
        b, half = core // 2, core % 2
        out[b, half * TOK:(half + 1) * TOK] = res.results[core]["out"]
    return out
```

```python
import numpy as np
import ml_dtypes
import concourse.bass as bass
import concourse.mybir as mybir
from concourse.bass_utils import run_bass_kernel_spmd

F32 = mybir.dt.float32
BF16 = mybir.dt.bfloat16
AF = mybir.ActivationFunctionType
ALU = mybir.AluOpType

NSLOT = 16
EPOCH = 8000
NCORES = 8
DM = 1024
TOK = 4096
NBLK = 8
INW = 5632
RX = 2080
NE = 22
TABW = NE * 64
EPS = 1e-6


class Sched:
    ENGS = ["pe", "act", "dve", "pool", "sp"]

    def __init__(self):
        self.ops = {e: [] for e in self.ENGS}
        self.res = {}
        self.ndma = {e: 0 for e in self.ENGS}
        self.pending = {e: set() for e in self.ENGS}
        self.extra = {}

    def _collect(self, eng, reads, writes):
        deps = set(self.pending[eng])
        self.pending[eng] = set()
        for k in reads:
            r = self.res.get(k)
            if r is not None and r[0] is not None:
                t = r[0]
                if t[0] == "c" and t[1] == eng and eng in ("pe", "sp"):
                    continue
                deps.add(t)
        for k in writes:
            if k in self.extra:
                deps |= self.extra.pop(k)
            r = self.res.get(k)
            if r is not None:
                for t in ([r[0]] if r[0] is not None else []) + list(r[1]):
                    if t[0] == "c" and t[1] == eng and eng == "pe":
                        continue
                    deps.add(t)
        return deps

    def _update(self, reads, writes, tok):
        for k in reads:
            r = self.res.setdefault(k, [None, []])
            r[1].append(tok)
        for k in writes:
            self.res[k] = [tok, []]

    def op(self, eng, fn, reads=(), writes=(), force_sig=False):
        deps = self._collect(eng, reads, writes)
        tok = ("c", eng, len(self.ops[eng]))
        self.ops[eng].append(dict(fn=fn, deps=deps, tok=tok, dma=None, force_sig=force_sig, cc=force_sig))
        self._update(reads, writes, tok)
        return tok

    def dma(self, eng, fn, reads=(), writes=()):
        deps = self._collect(eng, reads, writes)
        j = self.ndma[eng]
        self.ndma[eng] += 1
        if j >= NSLOT:
            deps.add(("d", eng, j - NSLOT))
        tok = ("d", eng, j)
        self.ops[eng].append(dict(fn=fn, deps=deps, tok=tok, dma=j))
        self._update(reads, writes, tok)
        return tok

    def barrier(self, skip_cc=False):
        toks = set()
        for e in self.ENGS:
            last = None
            for o in reversed(self.ops[e]):
                if skip_cc and o.get("cc"):
                    continue
                if o["dma"] is None and o["fn"] is not None:
                    last = o["tok"]
                    break
            if last is not None:
                toks.add(last)
        for q in self.ENGS:
            for j in range(max(0, self.ndma[q] - NSLOT), self.ndma[q]):
                toks.add(("d", q, j))
        for e in self.ENGS:
            self.pending[e] |= toks

    def wait_all(self, eng):
        self.barrier()
        deps = set(self.pending[eng])
        self.pending[eng] = set()
        self.ops[eng].append(dict(fn=None, deps=deps, tok=("c", eng, len(self.ops[eng])), dma=None))

    def plan(self):
        sig = {e: set() for e in self.ENGS}
        for e in self.ENGS:
            fc = {}
            fd = {}
            for idx, o in enumerate(self.ops[e]):
                waits = []
                for t in sorted(o["deps"], key=str):
                    if t[0] == "c":
                        _, e2, i2 = t
                        if e2 == e and i2 >= idx:
                            continue
                        if fc.get(e2, -1) >= i2:
                            continue
                        fc[e2] = i2
                        waits.append(t)
                        sig[e2].add(i2)
                    else:
                        q, j = t[1], t[2]
                        slot, val = (q, j % NSLOT), j // NSLOT
                        if fd.get(slot, -1) >= val:
                            continue
                        fd[slot] = val
                        waits.append(t)
                o["waits"] = waits
        self.ordinal = {}
        self.nsig = {}
        for e in self.ENGS:
            for idx, o in enumerate(self.ops[e]):
                if o.get("force_sig"):
                    sig[e].add(idx)
        for e in self.ENGS:
            for n, i in enumerate(sorted(sig[e])):
                self.ordinal[(e, i)] = n
            self.nsig[e] = len(sig[e])
        return self.nsig

    def emit(self, eng, h, esems, dsems):
        for idx, o in enumerate(self.ops[eng]):
            for t in o["waits"]:
                if t[0] == "c":
                    n = self.ordinal[(t[1], t[2])]
                    h.wait_ge(esems[t[1]][n // EPOCH], n % EPOCH + 1)
                else:
                    q, j = t[1], t[2]
                    h.wait_ge(dsems[q][j % NSLOT], 16 * (j // NSLOT + 1))
            if o["fn"] is None:
                continue
            ins = o["fn"](h)
            if o["dma"] is not None:
                ins.then_inc(dsems[eng][o["dma"] % NSLOT], 16)
            elif (eng, idx) in self.ordinal:
                n = self.ordinal[(eng, idx)]
                ins.then_inc(esems[eng][n // EPOCH], 1)


def _bf(a):
    return np.ascontiguousarray(a.astype(ml_dtypes.bfloat16))


def host_constants():
    C = {}
    C["ident"] = _bf(np.eye(128, dtype=np.float32))
    r = np.arange(128)[:, None].astype(np.float64)
    k1 = np.arange(128)[None, :].astype(np.float64)
    ang = 2 * np.pi * r * k1 / 128.0
    C["cs128"] = _bf(np.concatenate([np.cos(ang), -np.sin(ang)], axis=1))
    norm = 1.0 / np.sqrt(8192.0 * 64.0)
    p = np.arange(128)
    chp = 2 * (p % 64) + (p // 64)
    co = np.arange(128)
    same = (chp[:, None] // 64) == (co[None, :] // 64)
    a3 = 2 * np.pi * (chp[:, None] % 64) * (co[None, :] % 64) / 64.0
    C["ch3"] = _bf(np.stack([np.cos(a3) * same * norm, np.sin(a3) * same * norm], 0).transpose(1, 0, 2))
    same2 = (p[:, None] // 64) == (co[None, :] // 64)
    a4 = 2 * np.pi * (p[:, None] % 64) * (co[None, :] % 64) / 64.0
    C["chc"] = _bf(np.stack([np.cos(a4) * same2, np.sin(a4) * same2], 0).transpose(1, 0, 2))
    n = np.arange(256)[:, None].astype(np.float64)
    k = np.arange(256)[None, :].astype(np.float64)
    a5 = 2 * np.pi * n * k / 256.0
    nc_ = 1.0 / np.sqrt(256.0 * 64.0)
    t = np.stack([np.cos(a5) * nc_, -np.sin(a5) * nc_], 0)
    C["dft256"] = _bf(t.reshape(2, 2, 128, 256).transpose(2, 0, 1, 3))
    pp = np.arange(128)
    krl = pp // 64
    kc = pp % 64
    e = np.arange(NE)
    qc = np.arange(64)
    dr = 17 - e[None, :, None] + krl[:, None, None]
    cs = np.clip(qc - 8, 0, 48)
    colin = (kc[:, None, None] >= cs[None, None, :]) & (kc[:, None, None] < cs[None, None, :] + 16)
    m = (dr >= 0) & (dr <= 14) & colin
    C["maskF"] = _bf(m.reshape(128, TABW).astype(np.float32))
    mi = m & (dr >= 3) & (dr <= 10)
    C["maskI"] = _bf(mi[:, 7:16, :].reshape(128, 9 * 64).astype(np.float32))
    C["ones"] = _bf(np.ones((128, 128), np.float32))
    return C


def core_constants(half):
    C = {}
    p = np.arange(128)
    c = (p % 64)[:, None, None].astype(np.float64)
    k1 = np.arange(128)[None, :, None].astype(np.float64)
    k2 = (32 * half + np.arange(32))[None, None, :].astype(np.float64)
    ang = 2 * np.pi * c * (k1 + 128.0 * k2) / 8192.0
    gr, gi = np.cos(ang), -np.sin(ang)
    C["gtab"] = _bf(np.concatenate([-gi, gr, gi], axis=2))
    krl = p // 64
    rm = np.zeros((128, NBLK, 8, 8), np.float32)
    for i in range(NBLK):
        for j in range(8):
            for q in range(8):
                rq = 64 * half + 8 * i + q
                kr = 64 * half + 8 * i - 4 + 2 * j + krl
                rs = min(max(rq - 4, 0), 120)
                rm[:, i, j, q] = ((kr >= 0) & (kr < 128) & (kr >= rs) & (kr < rs + 8)).astype(np.float32)
    C["rowmask"] = _bf(rm)
    zf = np.zeros((128, 2), np.float32)
    zf[:, 0] = 0.0 if half == 0 else 1.0
    zf[:, 1] = 0.0 if half == 1 else 1.0
    C["zflag"] = zf
    return C


def gather_relbias(rel_bias):
    p = np.arange(128)
    krl = (p // 64)[:, None, None]
    kc = (p % 64)[:, None, None]
    e = np.arange(NE)[None, :, None]
    qc = np.arange(64)[None, None, :]
    dr = np.clip(17 - e + krl, 0, 14) + 0 * qc
    dc = np.clip(kc - qc, -15, 15) + 15 + 0 * e
    t = rel_bias[:, :, dr, dc]
    return np.ascontiguousarray(t.reshape(rel_bias.shape[0], rel_bias.shape[1], 128, TABW).astype(np.float32))


def build_nc(stage=99, debug=(), ncores=NCORES):
    nc = bass.Bass("TRN2", target_bir_lowering=False)
    S = Sched()
    D = {}

    def din(name, shape, dt=F32):
        D[name] = nc.dram_tensor(name, list(shape), dt, kind="ExternalInput").ap()

    def dscr(name, shape, dt):
        D[name] = nc.dram_tensor(name, list(shape), dt).ap()

    def dout(name, shape, dt=F32):
        D[name] = nc.dram_tensor(name, list(shape), dt, kind="ExternalOutput").ap()

    din("x", [TOK, DM]); din("ctx", [256, DM]); din("cvecT", [128, 16])
    din("ada_w", [2, DM, 6 * DM]); din("ada_bT", [2, 128, 48]); din("ada_b", [2, 6 * DM])
    din("n1gT", [2, 128, 8]); din("n2gT", [2, 128, 8]); din("final_g", [1, DM])
    din("w_in", [2, DM, INW]); din("conv_wT", [2, 128, 6]); din("rbt", [2, 8, 128, TABW])
    din("w_fourier", [2, 256, DM]); din("w_conv", [2, 256, DM]); din("w_attn", [2, 512, DM])
    din("w_o", [2, DM, DM]); din("mlp_w1", [2, DM, 4 * DM]); din("mlp_w2", [2, 4 * DM, DM])
    din("ident", [128, 128], BF16); din("cs128", [128, 256], BF16); din("ch3", [128, 2, 128], BF16)
    din("chc", [128, 2, 128], BF16); din("dft256", [128, 2, 2, 256], BF16); din("maskF", [128, TABW], BF16); din("maskI", [128, 576], BF16)
    din("ones", [128, 128], BF16); din("gtab", [128, 128, 96], BF16); din("rowmask", [128, NBLK, 8, 8], BF16)
    din("zflag", [128, 2])
    dout("out", [TOK, DM])
    dscr("x1", [TOK, DM], F32)
    dscr("hx_s", [NBLK, 128, 8, 512], BF16)
    dscr("xiu", [TOK, 256], BF16)
    dscr("xou", [2 * TOK, 256], BF16)
    dscr("exch_in", [RX, 256], BF16)
    dscr("exch_out", [2 * RX, 256], BF16)
    dscr("k_s", [128, 4, 72 * 64], BF16)
    dscr("v_s", [36, 128, 512], BF16)
    dscr("z_s", [128, 2, TOK + 32], BF16)
    dscr("y_s", [128, 2, TOK], BF16)
    dscr("gbc", [2, 2, 2, 128, DM], F32)
    WSHP = {"w_in": [DM, INW], "w_fourier": [256, DM], "w_conv": [256, DM], "w_attn": [512, DM], "w_o": [DM, DM],
            "mlp_w1": [DM, 4 * DM], "mlp_w2": [4 * DM, DM]}
    for nm_, shp_ in WSHP.items():
        dscr("b_" + nm_, [2] + shp_, BF16)

    dbg_layer = 1 if "L1" in debug else 0
    debug = [d_ for d_ in debug if d_ != "L1"]
    if "mix" in debug:
        dout("dbg_at", [128, 4, 512], BF16); dout("dbg_cv", [128, 2, 512], BF16); dout("dbg_m", [128, 8, 512], BF16)
        dout("dbg_xmix", [128, 4, DM], F32)
        dout("dbg_h2", [128, 8, 512], BF16); dout("dbg_h1", [128, 32, 512], BF16)
    ARENA = 212000
    arena = nc.alloc_sbuf_tensor("arena", [128, ARENA // 2], BF16)
    ps = [nc.alloc_psum_tensor(f"ps{i}", [128, 512], F32) for i in range(8)]

    class Region:
        def __init__(self, base, size):
            self.base, self.size, self.cur = base, size, base

        def reset(self):
            self.cur = self.base

        def alloc(self, shape, dt):
            n = int(np.prod(shape))
            nb = n * (4 if dt == F32 else 2)
            off = (self.cur + 63) // 64 * 64
            assert off + nb <= self.base + self.size, ("arena overflow", shape, off + nb - self.base, self.size)
            self.cur = off + nb
            ap = arena[:, off // 2:(off + nb) // 2]
            if dt == F32:
                ap = ap.bitcast(F32)
            if len(shape) == 2:
                ap = ap.rearrange("p (a b) -> p a b", b=int(shape[1]))
            elif len(shape) == 3:
                ap = ap.rearrange("p (a b c) -> p a b c", b=int(shape[1]), c=int(shape[2]))
            elif len(shape) == 4:
                ap = ap.rearrange("p (a b c d) -> p a b c d", b=int(shape[1]), c=int(shape[2]), d=int(shape[3]))
            return ap

    PERS = Region(0, 58000)
    TMP = Region(58000, ARENA - 58000)

    bank_ctr = [0]

    def nbank():
        b = bank_ctr[0] % 8
        bank_ctr[0] += 1
        return b

    def PK(b):
        return ("ps", b)

    def mm(out, lhsT, rhs, start, stop, reads, writes, tp=None):
        if tp is None:
            S.op("pe", lambda e: e.matmul(out, lhsT, rhs, start=start, stop=stop), reads, writes)
        else:
            S.op("pe", lambda e: e.matmul(out, lhsT, rhs, start=start, stop=stop, tile_position=tp), reads, writes)

    def act(out, in_, func, reads, writes, **kw):
        S.op("act", lambda e: e.activation(out, in_, func, **kw), reads, writes)

    def tt(out, in0, in1, op, reads, writes, eng="dve"):
        S.op(eng, lambda e: e.tensor_tensor(out, in0, in1, op), reads, writes)

    def ld(out, in_, reads, writes, eng="sp", **kw):
        S.dma(eng, lambda e: e.dma_start(out=out, in_=in_, **kw), reads, writes)

    def ldw(out, in_, reads, writes):
        S.dma("pool", lambda e: e.dma_start(out=out, in_=in_), reads, writes)

    def wview(name, l, r0, r1, c0, c1):
        return D[name][l, r0:r1, c0:c1].rearrange("(k p) n -> p k n", p=128)

    WMODE = {"cast_store": False}

    def ldwb(out, name, l, r0, r1, c0, c1, wkey):
        if WMODE["cast_store"]:
            ldw(out, wview(name, l, r0, r1, c0, c1), [], [wkey])
            S.dma("sp", lambda e: e.dma_start(out=D["b_" + name][l, r0:r1, c0:c1].rearrange("(k p) n -> p k n", p=128), in_=out),
                  [wkey], [("wb", name, l, r0, c0)])
            return
        S.dma("sp", lambda e: e.dma_start(out=out, in_=D["b_" + name][l, r0:r1, c0:c1].rearrange("(k p) n -> p k n", p=128)),
              [("wb", name, l, r0, c0)], [wkey])

    CONV_PIECES = [("w_in", 0, DM, 512, 768), ("w_in", 0, DM, 1024, 1536), ("w_in", 0, DM, 2560, 3584), ("w_in", 0, DM, 3584, 4608),
                   ("w_in", 0, DM, 4608, 5632), ("w_fourier", 0, 256, 0, DM), ("w_conv", 0, 256, 0, DM), ("w_attn", 0, 512, 0, DM),
                   ("w_o", 0, DM, 0, DM)] + [("mlp_w1", 0, DM, c * 1024, (c + 1) * 1024) for c in range(4)] + \
                  [("mlp_w2", r * 1024, (r + 1) * 1024, 0, DM) for r in range(4)]

    def convert_pieces(l, pieces, after_key):
        for (nm_, r0, r1, c0, c1) in pieces:
            S.dma("pool", lambda e, nm_=nm_, r0=r0, r1=r1, c0=c0, c1=c1: e.dma_start(
                out=D["b_" + nm_][l, r0:r1, c0:c1], in_=D[nm_][l, r0:r1, c0:c1]), [after_key], [("wb", nm_, l)])

    ident = PERS.alloc([128], BF16)
    ones = PERS.alloc([128], BF16)
    modc = PERS.alloc([2, 48, 2], F32)
    acol = PERS.alloc([2, 2, 4, 8], F32)
    convw = PERS.alloc([2, 6], F32)
    zflag = PERS.alloc([2], F32)
    rowmask = PERS.alloc([NBLK, 8, 8], BF16)
    tab = PERS.alloc([8, TABW], BF16)
    tabI = PERS.alloc([8, 576], BF16)
    kc_sb = PERS.alloc([4, 256], BF16)
    vc_sb = PERS.alloc([2, 512], BF16)
    ctx_sb = PERS.alloc([2, DM], F32)
    g1bc = PERS.alloc([DM], F32)
    g2bc = PERS.alloc([DM], F32)

    ld(ident, D["ident"], [], ["ident"])
    ld(ones, D["ones"], [], ["ones"])
    ld(convw, D["conv_wT"].rearrange("l p k -> p l k"), [], ["convw"])
    ld(zflag, D["zflag"], [], ["zflag"])
    ld(rowmask, D["rowmask"], [], ["rowmask"])
    ld(ctx_sb, D["ctx"].rearrange("(t p) d -> p t d", p=128), [], [("ctx_sb", 0), ("ctx_sb", 1)])

    def phase0(layers):
        TMP.reset()
        cT = TMP.alloc([16], F32)
        sT = TMP.alloc([16], F32)
        sl = TMP.alloc([16, 128], BF16)
        sb = TMP.alloc([16], BF16)
        abT = TMP.alloc([2, 48], F32)
        ngT = TMP.alloc([2, 2, 8], F32)
        abb = TMP.alloc([2, DM], F32)
        W = [TMP.alloc([8, 1024], BF16) for _ in range(6)]
        gt = [TMP.alloc([512], F32) for _ in range(2)]
        ld(cT, D["cvecT"], [], ["cT"])
        ld(abT, D["ada_bT"].rearrange("l p c -> p l c"), [], ["abT"])
        ld(ngT[:, :, 0, :], D["n1gT"].rearrange("l p c -> p l c"), [], ["ngT0"])
        ld(ngT[:, :, 1, :], D["n2gT"].rearrange("l p c -> p l c"), [], ["ngT1"])
        act(sT, cT, AF.Silu, ["cT"], ["sT"])
        S.op("dve", lambda e: e.tensor_copy(sb, sT), ["sT"], ["sb"])
        S.op("dve", lambda e: e.tensor_copy(sl, sT.unsqueeze(2).to_broadcast([128, 16, 128])), ["sT"], ["sl"])
        for l in layers:
            for s in range(6):
                ldw(W[s], wview("ada_w", l, 0, DM, s * 1024, (s + 1) * 1024), [], [("p0w", s)])
            for gi, c0 in enumerate((2048, 5120)):
                ld(abb[:, gi], D["ada_b"][l:l + 1, c0:c0 + 1024].partition_broadcast(128), [("gbcw", l, gi)], [("abb", gi)])
            for s in range(6):
                w = W[s]
                wk = ("p0w", s)
                b = nbank()
                for fc in range(8):
                    for k in range(8):
                        mm(ps[b][:, 2 * fc:2 * fc + 2], w[:, k, fc * 128:(fc + 1) * 128], sb[:, 2 * k:2 * k + 2],
                           k == 0, k == 7, [wk, "sb"], [PK(b)])
                tt(modc[:, l, s * 8:(s + 1) * 8, :], ps[b][:, 0:16].rearrange("p (c t) -> p c t", t=2),
                   abT[:, l, s * 8:(s + 1) * 8].unsqueeze(2).to_broadcast([128, 8, 2]), ALU.add,
                   [PK(b), "abT"], [("modc", l)])
                if s in (2, 5):
                    gi = 0 if s == 2 else 1
                    for t in range(2):
                        for hh in range(2):
                            b = nbank()
                            for k in range(8):
                                mm(ps[b][:, :], sl[:, 2 * k + t, :], w[:, k, hh * 512:(hh + 1) * 512],
                                   k == 0, k == 7, [wk, "sl"], [PK(b)])
                            g = gt[(t * 2 + hh) % 2]
                            gk = ("gt", (t * 2 + hh) % 2)
                            tt(g, ps[b][:, :], abb[:, gi, hh * 512:(hh + 1) * 512], ALU.add, [PK(b), ("abb", gi)], [gk])
                            ld(D["gbc"][l, t, gi, :, hh * 512:(hh + 1) * 512], g, [gk], [("gbcw", l, gi), ("gbc", l, t)])
            for t in range(2):
                for (vi, sc_c, ng_i) in ((0, 8, 0), (2, 32, 1)):
                    S.op("dve", lambda e, l=l, t=t, vi=vi, sc_c=sc_c, ng_i=ng_i: e.scalar_tensor_tensor(
                        acol[:, l, t, vi, :], modc[:, l, sc_c:sc_c + 8, t], 1.0, ngT[:, l, ng_i, :], ALU.add, ALU.mult),
                        [("modc", l), "ngT0", "ngT1"], [("acol", l)])
                for (vi, sh_c) in ((1, 0), (3, 24)):
                    S.op("dve", lambda e, l=l, t=t, vi=vi, sh_c=sh_c: e.tensor_copy(
                        acol[:, l, t, vi, :], modc[:, l, sh_c:sh_c + 8, t]), [("modc", l)], [("acol", l)])
        S.barrier()

    phase0([0])

    def norm_transpose(xt, ntile, A, B, hT, keys_x, key_h, scr, sfx=0):
        junk, ss, xn = scr
        kxf = keys_x if callable(keys_x) else (lambda t_: keys_x)
        for t in range(ntile):
            act(xn[:, t, :], xt[:, t, :], AF.Square, kxf(t), [("xn", sfx, t), ("ss", sfx, t)], accum_out=ss[:, t:t + 1])
            S.op("dve", lambda e, t=t: e.tensor_scalar(ss[:, t:t + 1], ss[:, t:t + 1], 1.0 / DM, EPS, ALU.mult, ALU.add),
                 [("ss", sfx, t)], [("ss", sfx, t)])
            act(ss[:, t:t + 1], ss[:, t:t + 1], AF.Sqrt, [("ss", sfx, t)], [("ss", sfx, t)])
            S.op("dve", lambda e, t=t: e.reciprocal(ss[:, t:t + 1], ss[:, t:t + 1]), [("ss", sfx, t)], [("ss", sfx, t)])
            act(xn[:, t, :], xt[:, t, :], AF.Copy, kxf(t) + [("ss", sfx, t)], [("xn", sfx, t)], scale=ss[:, t:t + 1])
        for k in range(8):
            b = nbank()
            pb = ps[b].bitcast(BF16)
            for t in range(ntile):
                S.op("pe", lambda e, t=t, k=k, pb=pb: e.transpose(pb[:, t * 128:(t + 1) * 128], xn[:, t, k * 128:(k + 1) * 128], ident),
                     [("xn", sfx, t), "ident"], [PK(b)])
            act(hT[:, k, 0:ntile * 128], pb[:, 0:ntile * 128], AF.Identity, [PK(b)], [key_h],
                scale=A[:, k:k + 1], bias=B[:, k:k + 1])

    def proj_fm(hT, T, W, wkey, c0, nchunk, sink, hkey):
        for c in range(nchunk):
            b = nbank()
            for k in range(8):
                mm(ps[b][:, 0:T], W[:, k, c0 + c * 128:c0 + (c + 1) * 128], hT[:, k, 0:T], k == 0, k == 7,
                   [wkey, hkey], [PK(b)])
            sink(c, b)

    def alloc_scr():
        return (TMP.alloc([DM], BF16), TMP.alloc([8], F32), TMP.alloc([4, DM], BF16))

    class NS:
        pass

    U1K = ["kwin", "vwin", "zwin", "ywin", ("rden", 0), ("rden", 1)] + [("cacc", c) for c in range(2)] + \
          [("E", n) for n in range(5)] + [("P", n) for n in range(5)] + [("sg", n) for n in range(2)] + [("mtmp", n) for n in range(2)]

    def alloc_mixer():
        M = NS()
        M.WS = [TMP.alloc([4096], BF16) for _ in range(4)]
        M.ws_ctr = 0
        u0 = TMP.cur
        M.kwin = TMP.alloc([4, 1024], BF16)
        M.vwin = TMP.alloc([8, 512], BF16)
        M.Et = [TMP.alloc([512], BF16) for _ in range(5)]
        M.Pt = [TMP.alloc([512], BF16) for _ in range(5)]
        M.sg = [TMP.alloc([512], F32) for _ in range(2)]
        M.mtmp = [TMP.alloc([512], F32) for _ in range(2)]
        M.cacc = TMP.alloc([2, 512], F32)
        M.zwin = TMP.alloc([2, 514], BF16)
        M.ywin = TMP.alloc([2, 512], BF16)
        M.rden = TMP.alloc([512], F32)
        u1 = TMP.cur
        TMP.cur = u0
        M.h1 = TMP.alloc([32, 512], BF16)
        TMP.cur = max(TMP.cur, u1)
        M.atT = TMP.alloc([4, 512], BF16)
        M.cvT = TMP.alloc([2, 512], BF16)
        M.q_sb = TMP.alloc([4, 2, 512], BF16)
        S.op("dve", lambda e, q=M.q_sb: e.memset(q, 0.0), [], ["q_sb"])
        M.mT = TMP.alloc([8, 512], BF16)
        M.rr = [TMP.alloc([512], BF16) for _ in range(2)]
        M.otmp = [TMP.alloc([512], F32) for _ in range(2)]
        M.ep_ctr = 0
        M.sg_ctr = 0
        M.h1_free = set()
        return M

    def layer(l):
        last = (l == 1)
        xin = D["x"] if l == 0 else D["x1"]
        xout = D["x1"] if l == 0 else D["out"]
        xkin = "xin0" if l == 0 else "x1"
        xkout = "x1" if l == 0 else "outk"
        A1 = acol[:, l, 0, 0, :]; B1 = acol[:, l, 0, 1, :]; A2 = acol[:, l, 0, 2, :]; B2 = acol[:, l, 0, 3, :]
        cA1 = acol[:, l, 1, 0, :]; cB1 = acol[:, l, 1, 1, :]; cA2 = acol[:, l, 1, 2, :]; cB2 = acol[:, l, 1, 3, :]

        def load_wk():
            wk1 = TMP.alloc([8, 768], BF16)
            wk2 = TMP.alloc([8, 1024], BF16)
            ldw(wk1[:, :, 0:512], wview("w_in", l, 0, DM, 0, 512), [], ["wk1"])
            ldw(wk1[:, :, 512:768], wview("w_in", l, 0, DM, 768, 1024), [], ["wk1"])
            ldw(wk2, wview("w_in", l, 0, DM, 1536, 2560), [], ["wk2"])
            return wk1, wk2

        TMP.reset()
        maskF = TMP.alloc([TABW], BF16)
        ld(maskF, D["maskF"], [], ["maskF"])
        maskI = TMP.alloc([576], BF16)
        ld(maskI, D["maskI"], [], ["maskI"])
        rb = [TMP.alloc([TABW], F32) for _ in range(2)]
        eb = [TMP.alloc([TABW], BF16) for _ in range(2)]
        for h in range(8):
            ld(rb[h % 2], D["rbt"][l, h], [], [("rb", h % 2)])
            act(eb[h % 2], rb[h % 2], AF.Exp, [("rb", h % 2)], [("eb", h % 2)])
            tt(tab[:, h, :], eb[h % 2], maskF, ALU.mult, [("eb", h % 2), "maskF"], ["tab"])
            tt(tabI[:, h, :], eb[h % 2][:, 7 * 64:16 * 64], maskI, ALU.mult, [("eb", h % 2), "maskI"], ["tab"])
        S.barrier()

        def slab(M):
            n = M.ws_ctr % 4
            M.ws_ctr += 1
            return M.WS[n], ("ws", n)

        def mixer_block(M, scr, T, hT, hkey, xres, xkey, yT, ykey, zw, zkey, keychunks, Am, Bm, pre_wo=None):
            nt = T // 128
            cacc, q_sb, atT, cvT, mT, h1 = M.cacc, M.q_sb, M.atT, M.cvT, M.mT, M.h1
            for c in range(2):
                S.op("dve", lambda e, c=c: e.tensor_scalar(cacc[:, c, 0:T], zw[:, c, 0:T], convw[:, l, 3 * c:3 * c + 1], None, ALU.mult),
                     [zkey, "convw"], [("cacc", c)])
                for kk in (1, 2):
                    S.op("dve", lambda e, c=c, kk=kk: e.scalar_tensor_tensor(cacc[:, c, 0:T], zw[:, c, kk:kk + T],
                                                                            convw[:, l, 3 * c + kk:3 * c + kk + 1], cacc[:, c, 0:T], ALU.mult, ALU.add),
                         [zkey, "convw", ("cacc", c)], [("cacc", c)])
            w, wk = slab(M)
            wv = w.rearrange("p (k n) -> p k n", k=8)
            ldwb(wv[:, :, 0:256], "w_in", l, 0, DM, 512, 768, wk)

            def sink_cb(c, b):
                tt(cvT[:, c, 0:T], ps[b][:, 0:T], cacc[:, c, 0:T], ALU.mult, [PK(b), ("cacc", c)], ["cvT"])
            proj_fm(hT, T, wv, wk, 0, 2, sink_cb, hkey)
            w, wk = slab(M)
            wv = w.rearrange("p (k n) -> p k n", k=8)
            ldwb(wv, "w_in", l, 0, DM, 1024, 1536, wk)

            def sink_q(c, b):
                act(q_sb[0:64, c, 0, 0:T], ps[b][0:64, 0:T], AF.Copy, [PK(b)], ["q_sb"])
                act(q_sb[64:128, c, 1, 0:T], ps[b][64:128, 0:T], AF.Copy, [PK(b)], ["q_sb"])
            proj_fm(hT, T, wv, wk, 0, 4, sink_q, hkey)
            nkc = len(keychunks)
            sctr = 0
            for hp in range(4):
                OB, DB = (3, 5), (4, 6)
                order = [nkc - 2] + list(range(nkc - 2)) + [nkc - 1]
                items = [(hh, ci) for hh in range(2) for ci in order]
                sbanks = {}

                def crange(ci):
                    lo, hi = keychunks[ci][6], keychunks[ci][7]
                    return (0, T) if lo is None else (lo * 64, (hi + 1) * 64)

                def emit_S(it):
                    nonlocal sctr
                    hh, ci = it
                    kT_ap, kkey = keychunks[ci][0], keychunks[ci][4]
                    c0, c1 = crange(ci)
                    b = (0, 1, 2, 7)[sctr % 4]
                    sctr += 1
                    sbanks[it] = b
                    mm(ps[b][:, c0:c1], kT_ap(hp), q_sb[:, hp, hh, c0:c1], True, True, [kkey, "q_sb"], [PK(b)])

                def emit_PV(it):
                    hh, ci = it
                    _, v_ap, tspec, rmask, kkey, vkey, lo, hi = keychunks[ci]
                    c0, c1 = crange(ci)
                    b = sbanks[it]
                    n = M.ep_ctr % 5
                    M.ep_ctr += 1
                    E, P = M.Et[n], M.Pt[n]
                    act(E[:, c0:c1], ps[b][:, c0:c1], AF.Exp, [PK(b)], [("E", n)], scale=0.125)
                    src, skey = E, ("E", n)
                    h = 2 * hp + hh
                    if tspec is not None:
                        kind, e0 = tspec
                        if kind == "int":
                            tt(P[:, c0:c1], E[:, c0:c1], tabI[:, h, e0 * 64:e0 * 64 + (c1 - c0)], ALU.mult, [("E", n), "tab"], [("P", n)])
                        else:
                            tt(P[:, 0:T], E[:, 0:T], tab[:, h, e0 * 64:e0 * 64 + T], ALU.mult, [("E", n), "tab"], [("P", n)])
                            Pv = P[:, 0:T].rearrange("p (a b) -> p a b", b=64)
                            tt(Pv, Pv, rmask.unsqueeze(2).to_broadcast([128, 8, 64]), ALU.mult, [("P", n), "rowmask"], [("P", n)], eng="pool")
                        src, skey = P, ("P", n)
                    first, lastc = (ci == order[0]), (ci == order[-1])
                    mm(ps[OB[hh]][:, c0:c1], v_ap(hp), src[:, c0:c1], first, lastc, [skey, vkey], [PK(OB[hh])])
                    mm(ps[DB[hh]][:, c0:c1], ones, src[:, c0:c1], first, lastc, [skey, "ones"], [PK(DB[hh])])

                LA = 3
                for n_, it in enumerate(items):
                    emit_S(it)
                    if n_ >= LA:
                        emit_PV(items[n_ - LA])
                for it in items[len(items) - LA:]:
                    emit_PV(it)
                for hh in range(2):
                    pr = slice(64 * hh, 64 * hh + 64)
                    S.op("dve", lambda e, hh=hh, pr=pr: e.reciprocal(M.rden[pr, 0:T], ps[DB[hh]][pr, 0:T]), [PK(DB[hh])], [("rden", hh)])
                    tt(atT[pr, hp, 0:T], ps[OB[hh]][pr, 0:T], M.rden[pr, 0:T], ALU.mult, [PK(OB[hh]), ("rden", hh)], ["atT"])
            if pre_wo is not None:
                pre_wo()
            branches = ((2560, "w_fourier", 256, yT, ykey, 2), (3584, "w_conv", 256, cvT, "cvT", 2), (4608, "w_attn", 512, atT, "atT", 4))
            for bi, (gc0, wname, wrows, src, skey, nk) in enumerate(branches):
                wb, wbk = slab(M)
                wbv = wb.rearrange("p (k n) -> p k n", n=1024)
                ldwb(wbv[:, 0:nk, :], wname, l, 0, wrows, 0, DM, wbk)
                for half in range(2):
                    wg, wgk = slab(M)
                    wgv = wg.rearrange("p (k n) -> p k n", k=8)
                    ldwb(wgv, "w_in", l, 0, DM, gc0 + half * 512, gc0 + (half + 1) * 512, wgk)
                    for o4 in range(4):
                        oc = half * 4 + o4
                        bg = nbank()
                        for k in range(8):
                            mm(ps[bg][:, 0:T], wgv[:, k, o4 * 128:(o4 + 1) * 128], hT[:, k, 0:T], k == 0, k == 7, [wgk, hkey], [PK(bg)])
                        n = M.sg_ctr % 2
                        M.sg_ctr += 1
                        act(M.sg[n][:, 0:T], ps[bg][:, 0:T], AF.Sigmoid, [PK(bg)], [("sg", n)])
                        bp = nbank()
                        for k in range(nk):
                            mm(ps[bp][:, 0:T], wbv[:, k, oc * 128:(oc + 1) * 128], src[:, k, 0:T], k == 0, k == nk - 1,
                               [wbk, skey], [PK(bp)])
                        if bi == 0:
                            tt(mT[:, oc, 0:T], ps[bp][:, 0:T], M.sg[n][:, 0:T], ALU.mult, [PK(bp), ("sg", n)], [("mT", oc)])
                        else:
                            tt(M.mtmp[n][:, 0:T], ps[bp][:, 0:T], M.sg[n][:, 0:T], ALU.mult, [PK(bp), ("sg", n)], [("mtmp", n)])
                            tt(mT[:, oc, 0:T], mT[:, oc, 0:T], M.mtmp[n][:, 0:T], ALU.add, [("mT", oc), ("mtmp", n)], [("mT", oc)])
            mTk = [("mT", oc) for oc in range(8)]
            if getattr(M, "dbg", False) and T == 512:
                ld(D["dbg_at"], atT, ["atT"], [])
                ld(D["dbg_cv"], cvT, ["cvT"], [])
                ld(D["dbg_m"], mT, mTk, [])
            wos = []
            for hh in range(2):
                wo, wok = slab(M)
                wov = wo.rearrange("p (k n) -> p k n", k=8)
                ldwb(wov, "w_o", l, 0, DM, hh * 512, (hh + 1) * 512, wok)
                wos.append((wov, wok))
            for t in range(nt):
                for hh in range(2):
                    wov, wok = wos[hh]
                    b = nbank()
                    for k in range(8):
                        mm(ps[b][:, :], mT[:, k, t * 128:(t + 1) * 128], wov[:, k, :], k == 0, k == 7, mTk + [wok], [PK(b)])
                    n = M.sg_ctr % 2
                    M.sg_ctr += 1
                    tt(M.otmp[n], ps[b][:, :], g1bc[:, hh * 512:(hh + 1) * 512], ALU.mult, [PK(b), "g1bc"], [("otmp", n)])
                    tt(xres[:, t, hh * 512:(hh + 1) * 512], xres[:, t, hh * 512:(hh + 1) * 512], M.otmp[n], ALU.add,
                       [(xkey, t), ("otmp", n)], [(xkey, t)])
            if getattr(M, "dbg", False) and T == 512:
                ld(D["dbg_xmix"], xres, [(xkey, t_) for t_ in range(nt)], [])
            norm_transpose(xres, nt, Am, Bm, hT, lambda t_: [(xkey, t_)], hkey, scr)
            first = True
            for s in range(8):
                w1, w1k = slab(M)
                w1v = w1.rearrange("p (k n) -> p k n", k=8)
                ldwb(w1v, "mlp_w1", l, 0, DM, s * 512, (s + 1) * 512, w1k)
                for c in range(4):
                    b = nbank()
                    for k in range(8):
                        mm(ps[b][:, 0:T], w1v[:, k, c * 128:(c + 1) * 128], hT[:, k, 0:T], k == 0, k == 7, [w1k, hkey], [PK(b)])
                    n = M.sg_ctr % 2
                    M.sg_ctr += 1
                    act(M.rr[n][:, 0:T], ps[b][:, 0:T], AF.Relu, [PK(b)], [("rr", n)])
                    tt(h1[:, s * 4 + c, 0:T], M.rr[n][:, 0:T], M.rr[n][:, 0:T], ALU.mult, [("rr", n)],
                       ["h1"] + (U1K if first else []))
                    first = False
            if getattr(M, "dbg", False) and T == 512:
                ld(D["dbg_h2"], hT, [hkey], [])
                ld(D["dbg_h1"], h1, ["h1"], [])
                M.dbg = False
            obanks = [(t, hh, nbank()) for t in range(nt) for hh in range(2)]
            lasttok = None
            for s in range(8):
                w2, w2k = slab(M)
                w2v = w2.rearrange("p (k n) -> p k n", k=4)
                ldwb(w2v, "mlp_w2", l, s * 512, (s + 1) * 512, 0, DM, w2k)
                for (t, hh, b) in obanks:
                    for k in range(4):
                        mm(ps[b][:, :], h1[:, s * 4 + k, t * 128:(t + 1) * 128], w2v[:, k, hh * 512:(hh + 1) * 512],
                           s == 0 and k == 0, s == 7 and k == 3, ["h1", w2k], [PK(b)])
            lasttok = ("c", "pe", len(S.ops["pe"]) - 1)
            for k_ in U1K:
                S.extra.setdefault(k_, set()).add(lasttok)
            for (t, hh, b) in obanks:
                n = M.sg_ctr % 2
                M.sg_ctr += 1
                tt(M.otmp[n], ps[b][:, :], g2bc[:, hh * 512:(hh + 1) * 512], ALU.mult, [PK(b), "g2bc"], [("otmp", n)])
                tt(xres[:, t, hh * 512:(hh + 1) * 512], xres[:, t, hh * 512:(hh + 1) * 512], M.otmp[n], ALU.add,
                   [(xkey, t), ("otmp", n)], [(xkey, t)])

        def ctx_keychunks():
            kch = []
            for ci in range(2):
                kch.append((lambda hp, ci=ci: kc_sb[:, hp, ci * 128:(ci + 1) * 128],
                            lambda hp, ci=ci: vc_sb[:, ci, hp * 128:(hp + 1) * 128], None, None, "kc_sb", "vc_sb", None, None))
            return kch

        TMP.reset()
        scr2 = [alloc_scr() for _ in range(2)]
        wk1, wk2 = load_wk()
        xt2b = [TMP.alloc([4, DM], F32) for _ in range(2)]
        hx2b = [TMP.alloc([8, 512], BF16) for _ in range(2)]
        cust = [TMP.alloc([512], F32) for _ in range(2)]
        kst = [TMP.alloc([512], BF16) for _ in range(8)]
        p1ctr = [0]
        XU = D["xiu"][0:TOK, :].rearrange("(r x) n -> r (x n)", x=64).rearrange("r (h c) -> r h c", c=64)

        def stg():
            n = p1ctr[0] % 8
            p1ctr[0] += 1
            return kst[n], ("kst", n)

        for i in range(NBLK):
            xt, hxT, scr = xt2b[i % 2], hx2b[i % 2], scr2[i % 2]
            xtk, hxk = ("xt", i % 2), ("hxT1", i % 2)
            ld(xt, xin[i * 512:(i + 1) * 512, :].rearrange("(t p) d -> p t d", p=128), [xkin], [xtk])
            norm_transpose(xt, 4, A1, B1, hxT, [xtk], hxk, scr, sfx=i % 2)
            ld(D["hx_s"][i], hxT, [hxk], [("hx_s", i)], eng="pool")
            def sink_u(c, b, i=i):
                st_, sk_ = stg()
                S.op("dve", lambda e, st_=st_, b=b: e.tensor_copy(st_, ps[b][:, :]), [PK(b)], [sk_])
                ld(XU[8 * i:8 * i + 8, c * 128:(c + 1) * 128, :].rearrange("r h c -> h r c"),
                   st_.rearrange("p (r c) -> p r c", c=64), [sk_], ["xiu"], eng="pool")
            proj_fm(hxT, 512, wk1, "wk1", 0, 2, sink_u, hxk)

            def sink_z(c, b, i=i):
                if c < 2:
                    act(cust[c], ps[b][:, :], AF.Copy, [PK(b)], [("cust", c)])
                else:
                    st_, sk_ = stg()
                    tt(st_, ps[b][:, :], cust[c - 2], ALU.mult, [PK(b), ("cust", c - 2)], [sk_])
                    ld(D["z_s"][:, c - 2, 16 + i * 512:16 + (i + 1) * 512], st_, [sk_], ["z_s"], eng="pool")
            proj_fm(hxT, 512, wk1, "wk1", 256, 4, sink_z, hxk)

            def sink_k(c, b, i=i):
                st_, sk_ = stg()
                S.op("dve", lambda e, st_=st_, b=b: e.tensor_copy(st_, ps[b][:, :]), [PK(b)], [sk_])
                ld(D["k_s"][:, c, (4 + 8 * i) * 64:(4 + 8 * i) * 64 + 512], st_, [sk_], ["k_s"], eng="pool")
            proj_fm(hxT, 512, wk2, "wk2", 0, 4, sink_k, hxk)
            for t in range(4):
                b = nbank()
                for k in range(8):
                    mm(ps[b][:, :], hxT[:, k, t * 128:(t + 1) * 128], wk2[:, k, 512:1024], k == 0, k == 7, ["wk2", hxk], [PK(b)])
                st_, sk_ = stg()
                S.op("dve", lambda e, st_=st_, b=b: e.tensor_copy(st_, ps[b][:, :]), [PK(b)], [sk_])
                ld(D["v_s"][2 + 4 * i + t], st_, [sk_], ["v_s"], eng="pool")

        XI, XO = D["exch_in"], D["exch_out"]
        ld(XI[0:512, :].rearrange("(p c) n -> p c n", c=4), D["k_s"][:, :, 256:512], ["k_s"], ["exch_in"])
        ld(XI[512:1024, :].rearrange("(p c) n -> p c n", c=4), D["k_s"][:, :, 64 * 64:68 * 64], ["k_s"], ["exch_in"])
        ld(XI[1024:1536, :].rearrange("(c p h) n -> c p (h n)", c=2, p=128), D["v_s"][2:4], ["v_s"], ["exch_in"])
        ld(XI[1536:2048, :].rearrange("(c p h) n -> c p (h n)", c=2, p=128), D["v_s"][32:34], ["v_s"], ["exch_in"])
        ld(XI[2048:2064, :].rearrange("r (pp c t) -> (r pp) c t", pp=8, c=2, t=16), D["z_s"][:, :, 16:32], ["z_s"], ["exch_in"])
        ld(XI[2064:2080, :].rearrange("r (pp c t) -> (r pp) c t", pp=8, c=2, t=16), D["z_s"][:, :, TOK:TOK + 16], ["z_s"], ["exch_in"])
        RG = [[2 * g, 2 * g + 1] for g in range(ncores // 2)]
        S.op("pool", lambda e: e.collective_compute("AllGather", ALU.bypass, replica_groups=RG, ins=[D["xiu"]], outs=[D["xou"]]),
             ["xiu"], ["xou"], force_sig=True)
        S.op("pool", lambda e: e.collective_compute("AllGather", ALU.bypass, replica_groups=RG, ins=[XI], outs=[XO]),
             ["exch_in"], ["exch_out"], force_sig=True)
        S.barrier(skip_cc=True)
        if l == 0:
            phase0([1])
        TMP.reset()
        scr = alloc_scr()
        hcT = TMP.alloc([8, 256], BF16)
        if not last:
            ucT = TMP.alloc([2, 256], BF16)
            zc = TMP.alloc([2, 258], BF16)
            cuc = TMP.alloc([2, 256], F32)
        markA = TMP.cur
        wk1, wk2 = load_wk()
        norm_transpose(ctx_sb, 2, cA1, cB1, hcT, lambda t_: [("ctx_sb", t_)], "hcT", scr)

        def sink_kc(c, b):
            act(kc_sb[:, c, :], ps[b][:, 0:256], AF.Copy, [PK(b)], ["kc_sb"])
        proj_fm(hcT, 256, wk2, "wk2", 0, 4, sink_kc, "hcT")
        for t in range(2):
            b = nbank()
            for k in range(8):
                mm(ps[b][:, :], hcT[:, k, t * 128:(t + 1) * 128], wk2[:, k, 512:1024], k == 0, k == 7, ["wk2", "hcT"], [PK(b)])
            act(vc_sb[:, t, :], ps[b][:, :], AF.Copy, [PK(b)], ["vc_sb"])
        if not last:
            S.op("dve", lambda e: e.memset(zc, 0.0), [], ["zc"])

            def sink_c1(c, b):
                if c < 2:
                    act(ucT[:, c, :], ps[b][:, 0:256], AF.Copy, [PK(b)], ["ucT"])
                elif c < 4:
                    act(cuc[:, c - 2, :], ps[b][:, 0:256], AF.Copy, [PK(b)], [("cuc", c - 2)])
                else:
                    tt(zc[:, c - 4, 1:257], ps[b][:, 0:256], cuc[:, c - 4, :], ALU.mult, [PK(b), ("cuc", c - 4)], ["zc"])
            proj_fm(hcT, 256, wk1, "wk1", 0, 6, sink_c1, "hcT")
        S.barrier()
        if not last:
            TMP.cur = markA
            M = alloc_mixer()
            ld(g1bc, D["gbc"][l, 1, 0], [("gbc", l, 1)], ["g1bc"])
            ld(g2bc, D["gbc"][l, 1, 1], [("gbc", l, 1)], ["g2bc"])
            chc = TMP.alloc([2, 128], BF16)
            d256 = TMP.alloc([2, 2, 256], BF16)
            zri = TMP.alloc([2, 2, 256], BF16)
            ycT = TMP.alloc([2, 256], BF16)
            ld(chc, D["chc"], [], ["chc"])
            ld(d256, D["dft256"], [], ["d256"])
            for ri in range(2):
                for tc in range(2):
                    b = nbank()
                    for cc in range(2):
                        mm(ps[b][:, cc * 128:(cc + 1) * 128], ucT[:, cc, tc * 128:(tc + 1) * 128], chc[:, ri, :], True, True,
                           ["ucT", "chc"], [PK(b)])
                    act(zri[:, ri, tc, :], ps[b][:, 0:256], AF.Copy, [PK(b)], ["zri"])
            for cc in range(2):
                b = nbank()
                n_ = 0
                for ri in range(2):
                    for tc in range(2):
                        mm(ps[b][:, 0:256], zri[:, ri, tc, cc * 128:(cc + 1) * 128], d256[:, ri, tc, :], n_ == 0, n_ == 3,
                           ["zri", "d256"], [PK(b)])
                        n_ += 1
                act(ycT[:, cc, :], ps[b][:, 0:256], AF.Copy, [PK(b)], ["ycT"])
            WMODE["cast_store"] = True
            mixer_block(M, scr, 256, hcT, "hcT", ctx_sb, "ctx_sb", ycT, "ycT", zc, "zc", ctx_keychunks(), cA2, cB2)
            WMODE["cast_store"] = False
            S.barrier()

        ld(D["k_s"][:, :, 0:256], XO[512:1024, :].rearrange("(p c) n -> p c n", c=4), ["exch_out"], ["k_s"])
        ld(D["k_s"][:, :, 68 * 64:72 * 64], XO[RX:RX + 512, :].rearrange("(p c) n -> p c n", c=4), ["exch_out"], ["k_s"])
        ld(D["v_s"][0:2], XO[1536:2048, :].rearrange("(c p h) n -> c p (h n)", c=2, p=128), ["exch_out"], ["v_s"])
        ld(D["v_s"][34:36], XO[RX + 1024:RX + 1536, :].rearrange("(c p h) n -> c p (h n)", c=2, p=128), ["exch_out"], ["v_s"])
        ld(D["z_s"][:, :, 0:16], XO[2064:2080, :].rearrange("r (pp c t) -> (r pp) c t", pp=8, c=2, t=16), ["exch_out"], ["z_s"])
        ld(D["z_s"][:, :, TOK + 16:TOK + 32], XO[RX + 2048:RX + 2064, :].rearrange("r (pp c t) -> (r pp) c t", pp=8, c=2, t=16), ["exch_out"], ["z_s"])
        S.barrier()
        if stage <= 2:
            return

        TMP.reset()
        cs128 = TMP.alloc([256], BF16)
        ch3 = TMP.alloc([2, 128], BF16)
        gtab = TMP.alloc([128, 96], BF16)
        U = TMP.alloc([128, 64], BF16)
        Asb = TMP.alloc([64, 256], BF16)
        Xsb = TMP.alloc([2, TOK], BF16)
        yst = [TMP.alloc([512], BF16) for _ in range(2)]
        ld(cs128, D["cs128"], [], ["cs128"])
        ld(ch3, D["ch3"], [], ["ch3"])
        ld(gtab, D["gtab"], [], ["gtab"])
        Uf = U.rearrange("p h c -> p (h c)")
        Xv = Xsb.rearrange("p r (k2 k1) -> p r k2 k1", k1=128)
        for cc in range(2):
            XOU0 = D["xou"][0:TOK, :].rearrange("(r x) n -> r (x n)", x=64).rearrange("r (h c) -> r h c", c=64)
            XOU1 = D["xou"][TOK:2 * TOK, :].rearrange("(r x) n -> r (x n)", x=64).rearrange("r (h c) -> r h c", c=64)
            ld(U[0:64], XOU0[:, cc * 128:(cc + 1) * 128, :], ["xou"], ["U"])
            ld(U[64:128], XOU1[:, cc * 128:(cc + 1) * 128, :], ["xou"], ["U"])
            for qq in range(32):
                b = nbank()
                for j in range(2):
                    q = 2 * qq + j
                    mm(ps[b][:, j * 256:(j + 1) * 256], Uf[:, 128 * q:128 * q + 128], cs128, True, True, ["U", "cs128"], [PK(b)])
                src = ps[b][:, :].rearrange("p (a b) -> p a b", a=2)
                if qq % 2 == 0:
                    act(Asb[:, 2 * qq:2 * qq + 2, :], src, AF.Copy, [PK(b)], ["Asb"])
                else:
                    S.op("dve", lambda e, qq=qq, src=src: e.tensor_copy(Asb[:, 2 * qq:2 * qq + 2, :], src), [PK(b)], ["Asb"])
            for g8 in range(16):
                b = nbank()
                for kl in range(8):
                    k1 = g8 * 8 + kl
                    for j in range(2):
                        pr = slice(64 * j, 64 * j + 64)
                        mm(ps[b][pr, kl * 64:(kl + 1) * 64], Asb[pr, :, k1], gtab[pr, k1, 32:96], True, False,
                           ["Asb", "gtab"], [PK(b)], tp=(64 * j, 64 * j))
                        mm(ps[b][pr, kl * 64:(kl + 1) * 64], Asb[pr, :, 128 + k1], gtab[pr, k1, 0:64], False, True,
                           ["Asb", "gtab"], [PK(b)], tp=(64 * j, 64 * j))
                src = ps[b][:, :].rearrange("p (kl r k2) -> p r k2 kl", kl=8, r=2, k2=32)
                dst = Xv[:, :, :, g8 * 8:(g8 + 1) * 8]
                if g8 % 2 == 0:
                    act(dst, src, AF.Copy, [PK(b)], ["Xsb"])
                else:
                    S.op("dve", lambda e, dst=dst, src=src: e.tensor_copy(dst, src), [PK(b)], ["Xsb"])
            for i in range(NBLK):
                b = nbank()
                mm(ps[b][:, :], ch3[:, 0, :], Xsb[:, 0, i * 512:(i + 1) * 512], True, False, ["Xsb", "ch3"], [PK(b)])
                mm(ps[b][:, :], ch3[:, 1, :], Xsb[:, 1, i * 512:(i + 1) * 512], False, True, ["Xsb", "ch3"], [PK(b)])
                y = yst[i % 2]
                act(y, ps[b][:, :], AF.Copy, [PK(b)], [("yst", i % 2)])
                ld(D["y_s"][:, cc, i * 512:(i + 1) * 512], y, [("yst", i % 2)], ["y_s"], eng="pool")
        S.barrier()
        if stage <= 3:
            return

        TMP.reset()
        scr = alloc_scr()
        junk, ss, xn = scr
        M = alloc_mixer()
        M.dbg = ("mix" in debug) and l == dbg_layer
        hx2 = [TMP.alloc([8, 512], BF16) for _ in range(2)]
        xt2 = TMP.alloc([4, DM], F32)
        fgb = TMP.alloc([DM], F32)
        ld(g1bc, D["gbc"][l, 0, 0], [("gbc", l, 0)], ["g1bc"])
        ld(g2bc, D["gbc"][l, 0, 1], [("gbc", l, 0)], ["g2bc"])
        if last:
            ld(fgb, D["final_g"].partition_broadcast(128), [], ["fgb"])
        nblk2 = NBLK if stage > 4 else 1
        ld(hx2[0], D["hx_s"][0], [("hx_s", 0)], [("hxT", 0)])
        for i in range(nblk2):
            hxT, hkey_ = hx2[i % 2], ("hxT", i % 2)
            if i + 1 < nblk2:
                ld(hx2[(i + 1) % 2], D["hx_s"][i + 1], [("hx_s", i + 1)], [("hxT", (i + 1) % 2)])
            ld(M.zwin, D["z_s"][:, :, 15 + i * 512:15 + i * 512 + 514], ["z_s"], ["zwin"])
            ld(M.ywin, D["y_s"][:, :, i * 512:(i + 1) * 512], ["y_s"], ["ywin"])
            ld(M.kwin, D["k_s"][:, :, 8 * i * 64:(8 * i + 16) * 64], ["k_s"], ["kwin"])
            ld(M.vwin, D["v_s"][4 * i:4 * i + 8].rearrange("c p n -> p c n"), ["v_s"], ["vwin"])

            def pre_wo(i=i):
                ld(xt2, xin[i * 512:(i + 1) * 512, :].rearrange("(t p) d -> p t d", p=128), [xkin], [("xt2", t_) for t_ in range(4)])
            if i == 0:
                S.op("dve", lambda e: e.tensor_scalar(M.zwin[:, :, 0:1], M.zwin[:, :, 0:1], zflag[:, 0:1], None, ALU.mult),
                     ["zwin", "zflag"], ["zwin"])
            if i == NBLK - 1:
                S.op("dve", lambda e: e.tensor_scalar(M.zwin[:, :, 513:514], M.zwin[:, :, 513:514], zflag[:, 1:2], None, ALU.mult),
                     ["zwin", "zflag"], ["zwin"])
            kch = []
            RNG = [(0, 1), (0, 3), (0, 5), (0, 7), (1, 7), (3, 7), (5, 7), (7, 7)]
            for j in range(8):
                if 0 < i < NBLK - 1:
                    lo, hi = RNG[j]
                    kch.append((lambda hp, j=j: M.kwin[:, hp, j * 128:(j + 1) * 128],
                                lambda hp, j=j: M.vwin[:, j, hp * 128:(hp + 1) * 128], ("int", 7 - 2 * j + lo), None, "kwin", "vwin", lo, hi))
                else:
                    kch.append((lambda hp, j=j: M.kwin[:, hp, j * 128:(j + 1) * 128],
                                lambda hp, j=j: M.vwin[:, j, hp * 128:(hp + 1) * 128], ("full", 14 - 2 * j), rowmask[:, i, j, :], "kwin", "vwin", None, None))
            kch += ctx_keychunks()
            WMODE["cast_store"] = (last and i == 0)
            mixer_block(M, scr, 512, hxT, hkey_, xt2, "xt2", M.ywin, "ywin", M.zwin, "zwin", kch, A2, B2, pre_wo=pre_wo)
            WMODE["cast_store"] = False
            if last:
                for t in range(4):
                    act(xn[:, t, :], xt2[:, t, :], AF.Square, [("xt2", t)], [("xn", 0, t), ("ss", 0, t)], accum_out=ss[:, t:t + 1])
                    S.op("dve", lambda e, t=t: e.tensor_scalar(ss[:, t:t + 1], ss[:, t:t + 1], 1.0 / DM, EPS, ALU.mult, ALU.add),
                         [("ss", 0, t)], [("ss", 0, t)])
                    act(ss[:, t:t + 1], ss[:, t:t + 1], AF.Sqrt, [("ss", 0, t)], [("ss", 0, t)])
                    S.op("dve", lambda e, t=t: e.reciprocal(ss[:, t:t + 1], ss[:, t:t + 1]), [("ss", 0, t)], [("ss", 0, t)])
                    S.op("dve", lambda e, t=t: e.scalar_tensor_tensor(xt2[:, t, :], xt2[:, t, :], ss[:, t:t + 1], fgb, ALU.mult, ALU.mult),
                         [("xt2", t), ("ss", 0, t), "fgb"], [("xt2", t)])
            ld(xout[i * 512:(i + 1) * 512, :].rearrange("(t p) d -> p t d", p=128), xt2, [("xt2", t_) for t_ in range(4)], [xkout, ("blkdone", l, i)], eng="pool")
        S.barrier()

    for l in range(2):
        layer(l)
        if stage <= 5:
            break

    S.barrier()
    DUMPS = {"kc_sb": ([128, 4, 256], BF16, kc_sb), "vc_sb": ([128, 2, 512], BF16, vc_sb), "ctx_sb": ([128, 2, DM], F32, ctx_sb),
             "acol": ([128, 2, 2, 4, 8], F32, acol), "modc": ([128, 2, 48, 2], F32, modc), "tab": ([128, 8, TABW], BF16, tab)}
    for nm in debug:
        if nm == "mix":
            continue
        if nm in DUMPS:
            shp, dt, src = DUMPS[nm]
        else:
            src = D[nm]
            shp, dt = list(src.shape), src.dtype
        dout("dbg_" + nm, shp, dt)
        ld(D["dbg_" + nm], src, [], [])
    S.wait_all("sp")

    nsig = S.plan()
    import contextlib
    with contextlib.ExitStack() as st:
        esems = {e: [st.enter_context(nc.semaphore(f"s_{e}_{n}")) for n in range(nsig[e] // EPOCH + 1)] for e in Sched.ENGS}
        dsems = {q: [st.enter_context(nc.semaphore(f"d_{q}_{n}")) for n in range(NSLOT)] for q in ("sp", "pool")}
        block = st.enter_context(nc.Block())

        @block.tensor
        def _(h):
            S.emit("pe", h, esems, dsems)

        @block.scalar
        def _(h):
            S.emit("act", h, esems, dsems)

        @block.vector
        def _(h):
            S.emit("dve", h, esems, dsems)

        @block.gpsimd
        def _(h):
            S.emit("pool", h, esems, dsems)

        @block.sync
        def _(h):
            S.emit("sp", h, esems, dsems)
    return nc


def make_in_maps(inp):
    f32 = lambda a: np.ascontiguousarray(np.asarray(a, dtype=np.float32))
    HC = host_constants()
    CC = [core_constants(h) for h in range(2)]
    shared = {
        "ada_w": f32(inp["ada_w"]), "ada_b": f32(inp["ada_b"]),
        "ada_bT": f32(np.asarray(inp["ada_b"]).reshape(2, 48, 128).transpose(0, 2, 1)),
        "n1gT": f32(np.asarray(inp["norm1_g"]).reshape(2, 8, 128).transpose(0, 2, 1)),
        "n2gT": f32(np.asarray(inp["norm2_g"]).reshape(2, 8, 128).transpose(0, 2, 1)),
        "final_g": f32(np.asarray(inp["final_g"]).reshape(1, DM)),
        "w_in": f32(inp["w_in"]),
        "conv_wT": f32(np.asarray(inp["conv_w"]).reshape(2, 3, 2, 128).transpose(0, 3, 2, 1).reshape(2, 128, 6)),
        "rbt": gather_relbias(np.asarray(inp["rel_bias"], dtype=np.float32)),
        "w_fourier": f32(inp["w_fourier"]), "w_conv": f32(inp["w_conv"]), "w_attn": f32(inp["w_attn"]),
        "w_o": f32(inp["w_o"]), "mlp_w1": f32(inp["mlp_w1"]), "mlp_w2": f32(inp["mlp_w2"]),
    }
    shared.update(HC)
    maps = []
    x = np.asarray(inp["x"]); c = np.asarray(inp["c"]); ctx = np.asarray(inp["ctx"]); c_ctx = np.asarray(inp["c_ctx"])
    for core in range(NCORES):
        b, half = core // 2, core % 2
        m = dict(shared)
        m.update(CC[half])
        m["x"] = f32(x[b, half * TOK:(half + 1) * TOK])
        m["ctx"] = f32(ctx[b])
        cv = np.stack([c[b], c_ctx], axis=1).reshape(8, 128, 2).transpose(1, 0, 2).reshape(128, 16)
        m["cvecT"] = f32(cv)
        maps.append(m)
    return maps


_NC_CACHE = {}


def kernel(**inputs):
    if "nc" not in _NC_CACHE:
        _NC_CACHE["nc"] = build_nc()
    nc = _NC_CACHE["nc"]
    maps = make_in_maps(inputs)
    res = run_bass_kernel_spmd(nc, maps, core_ids=list(range(NCORES)))
    out = np.empty((4, 8192, DM), np.float32)
    for core in range(NCORES):
        b, half = core // 2, core % 2
        out[b, half * TOK:(half + 1) * TOK] = res.results[core]["out"]
    return out
```
